# Optimizing a Trainium2 kernel written in Bass

```python
import jax, jax.numpy as jnp
from jax import lax
import numpy as np

D_MODEL = 1024
BATCH = 16
SEQ = 256
DEPTH = 4
DEC_BATCH = 8
DEC_SEQ = 2048
PAST_LEN = 512

GRID_W = 64
N_MIXERS = 2
N_SGU = (DEPTH + 1) // 2
N_RWKV = DEPTH // 2
CHUNK = 128
SGU_WIDTH = D_MODEL
SGU_GROUPS = 16
HEAD_DIM = 64
N_HEADS = D_MODEL // HEAD_DIM
LORA_W = 64
LORA_A = 64
LORA_V = 32
LORA_G = 128
N_DIR = 2
D_FF = 4 * D_MODEL
NORM_EPS = 1e-6
GN_EPS = 64e-5

kernel_name = "hybrid_sgu_rwkv7_diffusion_step"


def _rmsnorm(x, g):
    xf = x.astype(jnp.float32)
    y = xf * lax.rsqrt(jnp.mean(xf * xf, axis=-1, keepdims=True) + NORM_EPS)
    return y.astype(x.dtype) * g


def _layernorm(x, g, b, eps):
    xf = x.astype(jnp.float32)
    mu = jnp.mean(xf, axis=-1, keepdims=True)
    var = jnp.mean(jnp.square(xf - mu), axis=-1, keepdims=True)
    return ((xf - mu) * lax.rsqrt(var + eps)).astype(x.dtype) * g + b


def _centred_shift(x):
    xp = jnp.pad(x, ((0, 0), (1, 1), (0, 0)))
    return 0.5 * (xp[:, :-2] + xp[:, 2:]) - x


def _heads(t):
    return t.reshape(t.shape[:-1] + (N_HEADS, HEAD_DIM))


def _sgu_mixer(h, w_in, ln_g, ln_b, w_s, b_s, w_out):
    B, L, _ = h.shape
    n_chunks = L // CHUNK
    z = jax.nn.gelu(h @ w_in)
    u, v = jnp.split(z, 2, axis=-1)
    v = _layernorm(v, ln_g, ln_b, NORM_EPS)
    v = v.reshape(B, n_chunks, CHUNK, SGU_GROUPS, SGU_WIDTH // SGU_GROUPS)
    v = jnp.einsum('gts,bcsgd->bctgd', w_s, v) + b_s.T[:, :, None]
    return (u * v.reshape(B, L, SGU_WIDTH)) @ w_out


def _wkv_scan(s0, r, decay, k, v, kk, a, reverse):
    def step(s, inp):
        r_t, w_t, k_t, v_t, kk_t, a_t = inp
        sa = jnp.einsum('bhij,bhj->bhi', s, -kk_t)
        s = (s * w_t[:, :, None, :]
             + sa[..., None] * (kk_t * a_t)[:, :, None, :]
             + v_t[..., None] * k_t[:, :, None, :])
        return s, jnp.einsum('bhij,bhj->bhi', s, r_t)
    xs = tuple(jnp.swapaxes(t, 0, 1) for t in (r, decay, k, v, kk, a))
    s_final, ys = lax.scan(step, s0, xs, reverse=reverse)
    return s_final, jnp.swapaxes(ys, 0, 1)


def _rwkv_mixer(h, s0, v_first, P, i):
    B, L, D = h.shape
    f32 = jnp.float32
    xx = _centred_shift(h)
    mu = P['rwkv_mu'][i]
    xr, xw, xk, xv, xa, xg = [h + xx * mu[j] for j in range(6)]
    r = _heads(xr @ P['rwkv_w_r'][i])
    k = _heads(xk @ P['rwkv_w_k'][i])
    v = xv @ P['rwkv_w_v'][i]
    if v_first is None:
        v_first = v
    else:
        j = i - 1
        v = v + (v_first - v) * jax.nn.sigmoid(P['rwkv_v0'][j] + (xv @ P['rwkv_v1'][j]) @ P['rwkv_v2'][j])
    v = _heads(v)
    g = jax.nn.sigmoid(xg @ P['rwkv_g1'][i]) @ P['rwkv_g2'][i]
    kk = (k * _heads(P['rwkv_k_k'][i])).astype(f32)
    kk = kk / jnp.maximum(jnp.sqrt(jnp.sum(kk * kk, axis=-1, keepdims=True)), 1e-12)
    k_a = _heads(P['rwkv_k_a'][i])
    r_k = P['rwkv_r_k'][i]
    y = jnp.zeros(r.shape, f32)
    bonus = 0.0
    finals = []
    for d in range(N_DIR):
        z = (P['rwkv_w0'][i, d] + jnp.tanh(xw @ P['rwkv_w1'][i, d]) @ P['rwkv_w2'][i, d]).astype(f32)
        decay = _heads(jnp.exp(-jnp.exp(-jax.nn.softplus(-z) - 0.5)))
        a = _heads(jax.nn.sigmoid(P['rwkv_a0'][i, d] + (xa @ P['rwkv_a1'][i, d]) @ P['rwkv_a2'][i, d]))
        k_d = k * (1 + (a - 1) * k_a)
        s_fin, y_d = _wkv_scan(s0[:, d].astype(f32), r.astype(f32), decay, k_d.astype(f32),
                               v.astype(f32), kk, a.astype(f32), reverse=(d == 1))
        y = y + y_d
        bonus = bonus + jnp.sum(r * k_d * r_k, axis=-1, keepdims=True)
        finals.append(s_fin)
    y = _layernorm(y, _heads(P['rwkv_ln_g'][i]), _heads(P['rwkv_ln_b'][i]), GN_EPS).astype(h.dtype)
    y = y + bonus * v
    out = (y.reshape(B, L, D) * g) @ P['rwkv_w_o'][i]
    return out, jnp.stack(finals, axis=1), v_first


def _trunk(x, cond, s_init, P):
    B = x.shape[0]
    ada_in = jax.nn.silu(cond)
    v_first = None
    finals = []
    for l in range(DEPTH):
        mod = (ada_in @ P['ada_w'][l] + P['ada_b'][l])[:, None, :]
        sh1, sc1, g1, sh2, sc2, g2 = jnp.split(mod, 6, axis=-1)
        h = _rmsnorm(x, P['norm1_g'][l]) * (1 + sc1) + sh1
        i = l // N_MIXERS
        if l % N_MIXERS == 0:
            m = _sgu_mixer(h, P['sgu_w_in'][i], P['sgu_ln_g'][i], P['sgu_ln_b'][i],
                           P['sgu_w_s'][i], P['sgu_b_s'][i], P['sgu_w_out'][i])
        else:
            if s_init is None:
                s0 = jnp.zeros((B, N_DIR, N_HEADS, HEAD_DIM, HEAD_DIM), jnp.float32)
            else:
                s0 = s_init[:, i]
            m, s_fin, v_first = _rwkv_mixer(h, s0, v_first, P, i)
            finals.append(s_fin)
        x = x + g1 * m
        h = _rmsnorm(x, P['norm2_g'][l]) * (1 + sc2) + sh2
        x = x + g2 * (jnp.square(jax.nn.relu(h @ P['mlp_w1'][l])) @ P['mlp_w2'][l])
    return _rmsnorm(x, P['final_g']), jnp.stack(finals, axis=1)


def setup_inputs(seed: int = 0) -> dict:
    key = jax.random.key(seed)
    ks = iter(jax.random.split(key, 48))

    def nrm(shape, scale):
        return jax.random.normal(next(ks), shape, jnp.float32) * scale

    def unif(shape, lo, hi):
        return jax.random.uniform(next(ks), shape, jnp.float32, lo, hi)

    D, H, N = D_MODEL, N_HEADS, HEAD_DIM
    NV = max(N_RWKV - 1, 0)
    return {
        'x_prompt': nrm((BATCH, SEQ, D), 1.0),
        'x_sample': nrm((DEC_BATCH, DEC_SEQ, D), 1.0),
        'state_wkv': nrm((DEC_BATCH, N_RWKV, N_DIR, H, N, N), 1.0),
        'c': nrm((DEC_BATCH, D), 1.0),
        'c_ctx': nrm((D,), 1.0),
        'norm1_g': 1.0 + nrm((DEPTH, D), 0.02),
        'norm2_g': 1.0 + nrm((DEPTH, D), 0.02),
        'ada_w': nrm((DEPTH, D, 6 * D), 0.5 * D ** -0.5),
        'ada_b': nrm((DEPTH, 6 * D), 0.02),
        'sgu_w_in': nrm((N_SGU, D, 2 * SGU_WIDTH), D ** -0.5),
        'sgu_ln_g': 1.0 + nrm((N_SGU, SGU_WIDTH), 0.02),
        'sgu_ln_b': nrm((N_SGU, SGU_WIDTH), 0.02),
        'sgu_w_s': nrm((N_SGU, SGU_GROUPS, CHUNK, CHUNK), CHUNK ** -0.5),
        'sgu_b_s': 1.0 + nrm((N_SGU, SGU_GROUPS, CHUNK), 0.02),
        'sgu_w_out': nrm((N_SGU, SGU_WIDTH, D), SGU_WIDTH ** -0.5),
        'rwkv_mu': unif((N_RWKV, 6, D), 0.0, 1.0),
        'rwkv_w_r': nrm((N_RWKV, D, D), D ** -0.5),
        'rwkv_w_k': nrm((N_RWKV, D, D), D ** -0.5),
        'rwkv_w_v': nrm((N_RWKV, D, D), D ** -0.5),
        'rwkv_w_o': nrm((N_RWKV, D, D), D ** -0.5),
        'rwkv_w0': unif((N_RWKV, N_DIR, D), -6.0, 1.0),
        'rwkv_w1': nrm((N_RWKV, N_DIR, D, LORA_W), D ** -0.5),
        'rwkv_w2': nrm((N_RWKV, N_DIR, LORA_W, D), 0.1 * LORA_W ** -0.5),
        'rwkv_a0': nrm((N_RWKV, N_DIR, D), 0.1),
        'rwkv_a1': nrm((N_RWKV, N_DIR, D, LORA_A), D ** -0.5),
        'rwkv_a2': nrm((N_RWKV, N_DIR, LORA_A, D), 0.5 * LORA_A ** -0.5),
        'rwkv_v0': nrm((NV, D), 0.1),
        'rwkv_v1': nrm((NV, D, LORA_V), D ** -0.5),
        'rwkv_v2': nrm((NV, LORA_V, D), 0.5 * LORA_V ** -0.5),
        'rwkv_g1': nrm((N_RWKV, D, LORA_G), D ** -0.5),
        'rwkv_g2': nrm((N_RWKV, LORA_G, D), LORA_G ** -0.5),
        'rwkv_k_k': 0.85 + nrm((N_RWKV, D), 0.02),
        'rwkv_k_a': 1.0 + nrm((N_RWKV, D), 0.02),
        'rwkv_r_k': nrm((N_RWKV, H, N), 0.1),
        'rwkv_ln_g': 1.0 + nrm((N_RWKV, D), 0.02),
        'rwkv_ln_b': nrm((N_RWKV, D), 0.02),
        'mlp_w1': nrm((DEPTH, D, D_FF), D ** -0.5),
        'mlp_w2': nrm((DEPTH, D_FF, D), D_FF ** -0.5),
        'final_g': 1.0 + nrm((D,), 0.02),
    }


def reference(x_prompt, x_sample, state_wkv, c, c_ctx, norm1_g, norm2_g, ada_w, ada_b,
              sgu_w_in, sgu_ln_g, sgu_ln_b, sgu_w_s, sgu_b_s, sgu_w_out,
              rwkv_mu, rwkv_w_r, rwkv_w_k, rwkv_w_v, rwkv_w_o, rwkv_w0, rwkv_w1, rwkv_w2,
              rwkv_a0, rwkv_a1, rwkv_a2, rwkv_v0, rwkv_v1, rwkv_v2, rwkv_g1, rwkv_g2,
              rwkv_k_k, rwkv_k_a, rwkv_r_k, rwkv_ln_g, rwkv_ln_b, mlp_w1, mlp_w2, final_g):
    P = dict(norm1_g=norm1_g, norm2_g=norm2_g, ada_w=ada_w, ada_b=ada_b,
             sgu_w_in=sgu_w_in, sgu_ln_g=sgu_ln_g, sgu_ln_b=sgu_ln_b, sgu_w_s=sgu_w_s,
             sgu_b_s=sgu_b_s, sgu_w_out=sgu_w_out,
             rwkv_mu=rwkv_mu, rwkv_w_r=rwkv_w_r, rwkv_w_k=rwkv_w_k, rwkv_w_v=rwkv_w_v,
             rwkv_w_o=rwkv_w_o, rwkv_w0=rwkv_w0, rwkv_w1=rwkv_w1, rwkv_w2=rwkv_w2,
             rwkv_a0=rwkv_a0, rwkv_a1=rwkv_a1, rwkv_a2=rwkv_a2, rwkv_v0=rwkv_v0,
             rwkv_v1=rwkv_v1, rwkv_v2=rwkv_v2, rwkv_g1=rwkv_g1, rwkv_g2=rwkv_g2,
             rwkv_k_k=rwkv_k_k, rwkv_k_a=rwkv_k_a, rwkv_r_k=rwkv_r_k,
             rwkv_ln_g=rwkv_ln_g, rwkv_ln_b=rwkv_ln_b,
             mlp_w1=mlp_w1, mlp_w2=mlp_w2, final_g=final_g)
    y_prompt, new_state_wkv = _trunk(x_prompt, c_ctx[None, :], None, P)
    y_sample, _ = _trunk(x_sample, c, state_wkv, P)
    return (y_prompt, y_sample, new_state_wkv)
```

```python
import os
from contextlib import ExitStack
import numpy as np
import concourse.bass as bass
import concourse.mybir as mybir
from concourse.bass_utils import run_bass_kernel_spmd

F32 = mybir.dt.float32
BF16 = mybir.dt.bfloat16
AF = mybir.ActivationFunctionType
ALU = mybir.AluOpType
AX = mybir.AxisListType

NCORES = 8
D = 1024
NT = 2560
DEPTH = 4
NORM_EPS = 1e-6
GN_EPS = 64e-5
NEGC = -0.6065306597126334
NDMASEM = 8
_DBG = {}

class Op:
    __slots__ = ("eng", "fn", "deps", "sig", "waits", "is_dma", "dsem", "n", "prev_dma")

    def __init__(self, eng, fn, is_dma):
        self.eng = eng; self.fn = fn; self.deps = []; self.sig = None; self.waits = []
        self.is_dma = is_dma; self.dsem = None; self.n = 0; self.prev_dma = None


class Prog:
    ENGS = ("pe", "act", "dve", "pool", "sp")

    def __init__(self):
        self.ops = []
        self.last_w = {}
        self.readers = {}
        self.bar = []
        self.last_eng = {}
        self.last_dma = {}
        self.dcnt = {e: 0 for e in self.ENGS}

    def barrier(self):
        self.bar = list(self.last_eng.values()) + list(self.last_dma.values())
        self.last_w = {}
        self.readers = {}

    def op(self, eng, name, *args, r=(), w=(), dma=False, **kw):
        o = Op(eng, (name, args, kw), dma)
        o.n = len(self.ops)
        deps = {}
        for k in r:
            p = self.last_w.get(k)
            if p is not None:
                deps[p.n] = (p, True)
        for k in w:
            p = self.last_w.get(k)
            if p is not None and p.n not in deps:
                deps[p.n] = (p, True)
            for q in self.readers.get(k, ()):
                if q.n not in deps:
                    deps[q.n] = (q, False)
        for k in r:
            self.readers.setdefault(k, []).append(o)
        for k in w:
            self.last_w[k] = o
            self.readers[k] = []
        for p, raw in deps.values():
            if p.is_dma:
                o.deps.append(p)
            elif p.eng == o.eng and not o.is_dma:
                if raw and p.eng != "pe":
                    o.deps.append(p)
            else:
                o.deps.append(p)
        for p in self.bar:
            if p.is_dma or p.eng != o.eng or o.is_dma:
                o.deps.append(p)
        if dma:
            i = self.dcnt[eng]; self.dcnt[eng] += 1
            key = (eng, i % NDMASEM)
            o.prev_dma = self.last_dma.get(key)
            self.last_dma[key] = o
            o.dsem = key
        else:
            self.last_eng[eng] = o
        self.ops.append(o)
        return o

    def finalize(self):
        need = set()
        for o in self.ops:
            for p in o.deps:
                if not p.is_dma:
                    need.add(p.n)
        cnt = {e: 0 for e in self.ENGS}
        dval = {}
        for o in self.ops:
            if o.is_dma:
                key = o.dsem
                v = dval.get(key, 0) + 16
                dval[key] = v
                o.dsem = (key, v)
            elif o.n in need:
                cnt[o.eng] += 1
                o.sig = (("c" + o.eng, 0), cnt[o.eng])
        self.dma_final = dval
        waited = {e: {} for e in self.ENGS}
        for o in self.ops:
            wl = {}
            if o.is_dma and o.prev_dma is not None:
                k, v = o.prev_dma.dsem
                wl[k] = v
            for p in o.deps:
                k, v = p.dsem if p.is_dma else p.sig
                if wl.get(k, 0) < v:
                    wl[k] = v
            wd = waited[o.eng]
            for k, v in wl.items():
                if wd.get(k, 0) < v:
                    wd[k] = v
                    o.waits.append((k, v))
        self.sem_keys = set(dval.keys())
        for e in self.ENGS:
            if cnt[e]:
                self.sem_keys.add(("c" + e, 0))

    def emit(self, sems, block):
        byeng = {e: [o for o in self.ops if o.eng == e] for e in self.ENGS}
        finals = self.dma_final

        def run(eng_obj, lst, fk):
            for o in lst:
                for k, v in o.waits:
                    eng_obj.wait_ge(sems[k], v)
                name, args, kw = o.fn
                ins = getattr(eng_obj, name)(*args, **kw)
                if o.is_dma:
                    ins.then_inc(sems[o.dsem[0]], 16)
                elif o.sig is not None:
                    ins.then_inc(sems[o.sig[0]], 1)
            for k in fk:
                eng_obj.wait_ge(sems[k], finals[k])

        @block.tensor
        def _(e):
            run(e, byeng["pe"], [])

        @block.scalar
        def _(e):
            run(e, byeng["act"], [k for k in finals if k[0] == "act"])

        @block.vector
        def _(e):
            run(e, byeng["dve"], [])

        @block.gpsimd
        def _(e):
            run(e, byeng["pool"], [k for k in finals if k[0] == "pool"])

        @block.sync
        def _(e):
            run(e, byeng["sp"], [k for k in finals if k[0] == "sp"])


class Arena:
    def __init__(self, t, nwords):
        self.t = t; self.n = nwords; self.off = 0; self.reg = {}

    def reset(self):
        self.off = 0

    def f32(self, *shape, name=None):
        n = int(np.prod(shape))
        if name: self.reg[name] = (self.off, n, "f32", shape)
        a = self.t[:, self.off:self.off + n]
        self.off += n
        assert self.off <= self.n, ("arena overflow", self.off, self.n)
        return self._shape(a, shape)

    def bf16(self, *shape, name=None):
        n = int(np.prod(shape))
        nw = (n + 1) // 2
        if name: self.reg[name] = (self.off, nw, "bf16", shape)
        a = self.t[:, self.off:self.off + nw].bitcast(BF16)
        if n != 2 * nw:
            a = a[:, 0:n]
        self.off += nw
        assert self.off <= self.n, ("arena overflow", self.off, self.n)
        return self._shape(a, shape)

    @staticmethod
    def _shape(a, shape):
        if len(shape) == 1:
            return a
        if len(shape) == 2:
            return a.rearrange("p (a b) -> p a b", a=shape[0])
        if len(shape) == 3:
            return a.rearrange("p (a b c) -> p a b c", a=shape[0], b=shape[1])
        if len(shape) == 4:
            return a.rearrange("p (a b c d) -> p a b c d", a=shape[0], b=shape[1], c=shape[2])
        raise ValueError(shape)


def bcast_free(ap, n):
    return bass.AP(ap.tensor, ap.offset, [list(d) for d in ap.ap] + [[0, n]])


TILES512 = [(0, 0), (512, 0), (1024, 0), (1536, 0), (2048, 1)]
TILES256 = [(256 * i, 0, i > 0, i < 7) for i in range(8)] + [(2048, 1, False, False), (2304, 1, False, False)]
SEQS = [(0, 16, "s", 0), (16, 2, "p", 0), (18, 2, "p", 1)]


def build(depth=DEPTH):
    nc = bass.Bass("TRN2", target_bir_lowering=False)
    n_rw = depth // 2

    def din(name, shape, dt=F32):
        return nc.dram_tensor(name, list(shape), dt, kind="ExternalInput").ap()

    def dout(name, shape, dt=F32):
        return nc.dram_tensor(name, list(shape), dt, kind="ExternalOutput").ap()

    def dscr(name, shape, dt):
        return nc.dram_tensor(name, list(shape), dt, kind="Internal").ap()

    d_x = din("xT", [128, 8, NT])
    d_cond = din("condT", [128, 8, 2])
    d_state = din("state0", [128, 2, 2, 8, 64])
    d_ident = din("ident", [128, 128])
    d_onesbd = din("onesbd", [128, 128])
    d_maska = din("maska", [2, 128, 512])
    d_maskb = din("maskb", [2, 128, 256])
    d_blk = din("blkm", [2, 128, 7, 128])
    d_adaw = din("ada_w", [4, 128, 8, 6144])
    d_adab = din("ada_b", [128, 4, 48])
    d_ng = din("norm_g", [128, 4, 2, 8])
    d_fg = din("final_g", [128, 8])
    d_win = din("sgu_w_in", [2, 128, 8, 2048])
    d_wout = din("sgu_w_out", [2, 128, 8, 1024])
    d_wsT = din("sgu_wsT", [2, 128, 16, 128])
    d_bs = din("sgu_bs", [2, 1, 2048])
    d_lng = din("sgu_ln", [2, 2, 1024])
    d_w1 = din("mlp_w1", [4, 8, 128, 8, 512])
    d_w2 = din("mlp_w2", [4, 8, 128, 4, 1024])
    d_rw = din("rwkv_w", [2, 4, 128, 8, 1024])
    d_mu = din("rwkv_mu", [128, 2, 6, 8])
    d_l1 = din("rwkv_l1", [2, 4, 128, 8, 64])
    d_l2 = din("rwkv_l2", [2, 4, 64, 1024])
    d_g1 = din("rwkv_g1", [2, 128, 8, 128])
    d_g2 = din("rwkv_g2", [2, 128, 1024])
    d_v1 = din("rwkv_v1", [128, 8, 32])
    d_v2 = din("rwkv_v2", [32, 1024])
    d_pv = din("rwkv_pv", [128, 2, 11, 8])
    d_y = dout("yT", [128, 8, NT])
    d_ns = dout("new_state", [2, 2, 2, 128, 8, 64])
    scr = {}
    for i in range(n_rw):
        for nm in ("r", "k", "v", "kk", "af", "ab", "g", "bv"):
            scr[(i, nm)] = dscr("scr_%s_%d" % (nm, i), [20, 128, 8, 128], BF16)
        for nm in ("sf", "sb"):
            scr[(i, nm)] = dscr("scr_%s_%d" % (nm, i), [20, 128, 8, 128], F32)
        scr[(i, "yf")] = dscr("scr_yf_%d" % i, [20, 128, 1024], BF16)

    P = Prog()
    with ExitStack() as es:
        def sbt(name, shape, dt):
            return es.enter_context(nc.sbuf_tensor(name, list(shape), dt))

        x_sb = sbt("x_sb", [128, 8, NT], F32)
        modT = sbt("modT", [128, 4, 48, 2], F32)
        gsc = sbt("gsc", [128, 4, 2, 8, 2], F32)
        adab = sbt("adab", [128, 4, 48], F32)
        ng = sbt("ng", [128, 4, 2, 8], F32)
        fg = sbt("fg", [128, 8], F32)
        ident = sbt("ident_sb", [128, 128], BF16)
        onesbd = sbt("onesbdb", [128, 128], BF16)
        onesm = sbt("onesm", [128, 128], BF16)
        onesf = sbt("onesf", [128, 128], F32)
        maska = sbt("maska_sb", [128, 2, 512], BF16)
        maskb = sbt("maskb_sb", [128, 2, 256], BF16)
        blkm = sbt("blk_sb", [128, 2, 7, 128], BF16)
        epsn = sbt("epsn", [128, 1], F32)
        epsg = sbt("epsg", [128, 1], F32)
        epsk = sbt("epsk", [128, 1], F32)
        hm = sbt("hm", [128, 2], F32)
        pv = sbt("pv", [128, 2, 11, 8], F32)
        mu = sbt("mu", [128, 2, 6, 8], F32)
        omm = sbt("omm", [128, 2, 6, 8], F32)
        hmu = sbt("hmu", [128, 2, 6, 8], F32)
        omka = sbt("omka", [128, 2, 8], F32)
        tmka = sbt("tmka", [128, 2, 8], F32)
        Hst = sbt("Hst", [128, 8, 64], F32)
        Hb = sbt("Hb", [128, 8, 64], BF16)
        condb = sbt("condb", [128, 8, 2], BF16)
        ARW = 27500
        arena_t = sbt("arena", [128, ARW], F32)
        AR = Arena(arena_t, ARW)
        _DBG['AR'] = AR
        banks = [es.enter_context(nc.psum_tensor("bank%d" % i, [128, 512], F32)) for i in range(8)]
        bankbf = [b[:].bitcast(BF16) for b in banks]

        P.op("sp", "dma_start", out=adab[:], in_=d_adab[:, :, :], w=["adab"], dma=True)
        P.op("sp", "dma_start", out=ng[:], in_=d_ng[:, :, :, :], w=["ng"], dma=True)
        P.op("sp", "dma_start", out=fg[:], in_=d_fg[:, :], w=["fg"], dma=True)
        P.op("sp", "dma_start", out=pv[:], in_=d_pv[:, :, :, :], w=["pv"], dma=True)
        P.op("sp", "dma_start", out=mu[:], in_=d_mu[:, :, :, :], w=["mu"], dma=True)
        P.op("pool", "dma_start", out=ident[:], in_=d_ident[:, :], w=["ident"], dma=True)
        P.op("pool", "dma_start", out=onesbd[:], in_=d_onesbd[:, :], w=["onesbd"], dma=True)
        for dd in range(2):
            P.op("pool", "dma_start", out=maska[:, dd, :], in_=d_maska[dd], w=["maska"], dma=True)
            P.op("pool", "dma_start", out=maskb[:, dd, :], in_=d_maskb[dd], w=["maskb"], dma=True)
            P.op("pool", "dma_start", out=blkm[:, dd, :, :], in_=d_blk[dd], w=["blk"], dma=True)
        P.op("dve", "memset", onesm[:], 1.0 / 1024.0, w=["onesm"])
        P.op("dve", "memset", onesf[:], 1.0, w=["onesf"])
        P.op("dve", "memset", epsn[:], NORM_EPS, w=["epsn"])
        P.op("dve", "memset", epsg[:], GN_EPS, w=["epsg"])
        P.op("dve", "memset", epsk[:], 1e-24, w=["epsk"])
        P.op("dve", "memset", hm[0:64, 0:1], 1.0, w=["hm00"])
        P.op("dve", "memset", hm[64:128, 0:1], 0.0, w=["hm10"])
        P.op("dve", "memset", hm[0:64, 1:2], 0.0, w=["hm01"])
        P.op("dve", "memset", hm[64:128, 1:2], 1.0, w=["hm11"])
        P.op("dve", "tensor_scalar", omm[:], mu[:], -1.0, 1.0, op0=ALU.mult, op1=ALU.add, r=["mu"], w=["omm"])
        P.op("dve", "tensor_scalar", hmu[:], mu[:], 0.5, None, op0=ALU.mult, r=["mu"], w=["hmu"])
        P.op("dve", "tensor_scalar", omka[:], pv[:, :, 6, :], -1.0, 1.0, op0=ALU.mult, op1=ALU.add, r=["pv"], w=["omka"])
        P.op("dve", "tensor_scalar", tmka[:], pv[:, :, 6, :], -2.0, 2.0, op0=ALU.mult, op1=ALU.add, r=["pv"], w=["tmka"])

        def xkeys(t0, n):
            return [("x", k) for k in range(t0 // 128, (t0 + n + 127) // 128)]

        for (t0, _) in TILES512:
            P.op("sp", "dma_start", out=x_sb[:, :, t0:t0 + 512], in_=d_x[:, :, t0:t0 + 512], w=xkeys(t0, 512), dma=True)

        AR.reset()
        condf = AR.f32(8, 2)
        adaw = [AR.bf16(8, 512) for _ in range(2)]
        P.op("sp", "dma_start", out=condf, in_=d_cond[:, :, :], w=["condf"], dma=True)
        P.op("act", "activation", out=condb[:], in_=condf, func=AF.Silu, r=["condf"], w=["condb"])
        pm = banks[0][:, 0:96]
        nblk = 0
        for l in range(depth):
            for blk in range(12):
                buf = adaw[nblk % 2]
                bkey = ("adaw", nblk % 2)
                nblk += 1
                P.op("pool", "dma_start", out=buf, in_=d_adaw[l, :, :, blk * 512:(blk + 1) * 512], w=[bkey], dma=True)
                for m in range(4):
                    n = blk * 4 + m
                    for kc in range(8):
                        P.op("pe", "matmul", pm[:, 2 * n:2 * n + 2], buf[:, kc, m * 128:(m + 1) * 128], condb[:, kc, :],
                             start=(kc == 0), stop=(kc == 7), r=[bkey, "condb"], w=["pm"])
            P.op("dve", "tensor_tensor", modT[:, l, :, :], pm.rearrange("p (n w) -> p n w", w=2), bcast_free(adab[:, l, :], 2), op=ALU.add,
                 r=["pm", "adab"], w=["modT"])
            for j in range(2):
                P.op("dve", "tensor_scalar", gsc[:, l, j, :, :], modT[:, l, (3 * j + 1) * 8:(3 * j + 2) * 8, :], 1.0, None, op0=ALU.add,
                     r=["modT"], w=["gsc"])
                P.op("dve", "tensor_tensor", gsc[:, l, j, :, :], gsc[:, l, j, :, :], bcast_free(ng[:, l, j, :], 2), op=ALU.mult,
                     r=["gsc", "ng"], w=["gsc"])

        def emit_norm(t0, n, scale_ap, bias_ap, out_ap, tmp, ps_bank, out_keys):
            sq, sd, rstd, tmpn = tmp
            ps = ps_bank[:, 0:n]
            for c in range(8):
                s = sq[c % 2]
                P.op("act", "activation", out=s[:, 0:n], in_=x_sb[:, c, t0:t0 + n], func=AF.Square, r=xkeys(t0, n), w=[("nsq", c % 2)])
                P.op("pe", "matmul", ps, onesm[:], s[:, 0:n], start=(c == 0), stop=(c == 7), r=[("nsq", c % 2), "onesm"], w=["nps"])
            P.op("act", "activation", out=sd[:, 0:n], in_=ps, func=AF.Sqrt, bias=epsn[:, 0:1], scale=1.0, r=["nps", "epsn"], w=["nsd"])
            P.op("dve", "reciprocal", rstd[:, 0:n], sd[:, 0:n], r=["nsd"], w=["nrstd"])
            for c in range(8):
                tn = tmpn[c % 2]
                P.op("dve", "tensor_tensor", tn[:, 0:n], x_sb[:, c, t0:t0 + n], rstd[:, 0:n], op=ALU.mult,
                     r=xkeys(t0, n) + ["nrstd"], w=[("ntmp", c % 2)])
                if bias_ap is not None:
                    P.op("act", "activation", out=out_ap(c), in_=tn[:, 0:n], func=AF.Identity, scale=scale_ap(c), bias=bias_ap(c),
                         r=[("ntmp", c % 2), "gsc", "modT", "fg"], w=out_keys(c))
                else:
                    P.op("act", "activation", out=out_ap(c), in_=tn[:, 0:n], func=AF.Identity, scale=scale_ap(c),
                         r=[("ntmp", c % 2), "gsc", "modT", "fg"], w=out_keys(c))

        def norm_tmp(n):
            return ([AR.bf16(n) for _ in range(2)], AR.f32(n), AR.f32(n), [AR.f32(n) for _ in range(2)])

        def sgu_layer(l):
            i = l // 2
            P.barrier()
            AR.reset()
            w_in = AR.bf16(8, 2048)
            w_out = AR.bf16(8, 1024)
            wsT = AR.bf16(16, 128)
            lnG = AR.f32(1024)
            lnB = AR.f32(1024)
            bsrow = AR.bf16(2048)
            onesrow = AR.bf16(64)
            ntmp = norm_tmp(512)
            h1 = AR.bf16(8, 512)
            u = AR.bf16(8, 512)
            vt = AR.f32(1024)
            vsq = AR.f32(1024)
            vn = AR.bf16(4, 1024)
            st = AR.f32(8)
            for q in range(4):
                P.op("pool", "dma_start", out=w_in[:, :, q * 512:(q + 1) * 512], in_=d_win[i, :, :, q * 512:(q + 1) * 512], w=[("w_in", q)], dma=True)
            P.op("pool", "dma_start", out=wsT, in_=d_wsT[i], w=["wsT"], dma=True)
            P.op("pool", "dma_start", out=bsrow[0:1, :], in_=d_bs[i], w=["bsrow"], dma=True)
            for q in range(2):
                P.op("pool", "dma_start", out=w_out[:, :, q * 512:(q + 1) * 512], in_=d_wout[i, :, :, q * 512:(q + 1) * 512], w=[("w_out", q)], dma=True)
            P.op("sp", "dma_start", out=lnG, in_=bass.AP(d_lng.tensor, d_lng[i, 0].offset, [[0, 128], [1, 1024]]), w=["lnG"], dma=True)
            P.op("sp", "dma_start", out=lnB, in_=bass.AP(d_lng.tensor, d_lng[i, 1].offset, [[0, 128], [1, 1024]]), w=["lnB"], dma=True)
            P.op("dve", "memset", onesrow[0:1, :], 1.0, w=["onesrow"])
            for (t0, w) in TILES512:
                emit_norm(t0, 512, lambda c: gsc[:, l, 0, c, w:w + 1], lambda c: modT[:, l, 0 + c, w:w + 1],
                          lambda c: h1[:, c, :], ntmp, banks[0], lambda c: ["h1"])
                for oc in range(8):
                    ps = banks[1 + oc % 2]
                    pk = ("bk", 1 + oc % 2)
                    for kc in range(8):
                        P.op("pe", "matmul", ps[:], w_in[:, kc, oc * 128:(oc + 1) * 128], h1[:, kc, :], start=(kc == 0), stop=(kc == 7),
                             r=[("w_in", oc // 4), "h1"], w=[pk])
                    P.op("act", "activation", out=u[:, oc, :], in_=ps[:], func=AF.Gelu_apprx_tanh, r=[pk], w=[("u", oc)])
                for q in range(4):
                    for nb in range(2):
                        ps = banks[3 + nb]
                        pk = ("bk", 3 + nb)
                        for kc in range(8):
                            P.op("pe", "matmul", ps[:], h1[:, kc, q * 128:(q + 1) * 128], w_in[:, kc, 1024 + nb * 512:1024 + (nb + 1) * 512],
                                 start=(kc == 0), stop=(kc == 7), r=[("w_in", 2 + nb), "h1"], w=[pk])
                        P.op("act", "activation", out=vt[:, nb * 512:(nb + 1) * 512], in_=ps[:], func=AF.Gelu_apprx_tanh, r=[pk], w=["vt"])
                    P.op("dve", "tensor_reduce", out=st[:, 0:1], in_=vt, axis=AX.X, op=ALU.add, r=["vt"], w=["st0"])
                    P.op("act", "activation", out=vsq, in_=vt, func=AF.Square, r=["vt"], w=["vsq"])
                    P.op("dve", "tensor_reduce", out=st[:, 1:2], in_=vsq, axis=AX.X, op=ALU.add, r=["vsq"], w=["st1"])
                    P.op("dve", "tensor_scalar", st[:, 2:3], st[:, 0:1], 1.0 / 1024.0, None, op0=ALU.mult, r=["st0"], w=["st2"])
                    P.op("dve", "tensor_tensor", st[:, 3:4], st[:, 2:3], st[:, 2:3], op=ALU.mult, r=["st2"], w=["st3"])
                    P.op("dve", "scalar_tensor_tensor", st[:, 4:5], st[:, 1:2], 1.0 / 1024.0, st[:, 3:4], op0=ALU.mult, op1=ALU.subtract,
                         r=["st1", "st3"], w=["st4"])
                    P.op("act", "activation", out=st[:, 5:6], in_=st[:, 4:5], func=AF.Sqrt, bias=epsn[:, 0:1], scale=1.0, r=["st4", "epsn"], w=["st5"])
                    P.op("dve", "reciprocal", st[:, 6:7], st[:, 5:6], r=["st5"], w=["st6"])
                    P.op("dve", "tensor_scalar", vt, vt, st[:, 2:3], st[:, 6:7], op0=ALU.subtract, op1=ALU.mult, r=["vt", "st2", "st6"], w=["vt"])
                    P.op("dve", "tensor_tensor", vt, vt, lnG, op=ALU.mult, r=["vt", "lnG"], w=["vt"])
                    P.op("dve", "tensor_tensor", vn[:, q, :], vt, lnB, op=ALU.add, r=["vt", "lnB"], w=[("vn", q)])
                for c in range(8):
                    ps = banks[5 + c % 2]
                    pk = ("bk", 5 + c % 2)
                    for q in range(4):
                        for he in range(2):
                            g = 2 * c + he
                            P.op("pe", "matmul", ps[he * 64:(he + 1) * 64, q * 128:(q + 1) * 128], vn[:, q, g * 64:(g + 1) * 64], wsT[:, g, :],
                                 start=True, stop=False, tile_position=(0, he * 64), r=[("vn", q), "wsT"], w=[pk])
                            P.op("pe", "matmul", ps[he * 64:(he + 1) * 64, q * 128:(q + 1) * 128], onesrow[0:1, :], bsrow[0:1, g * 128:(g + 1) * 128],
                                 start=False, stop=True, tile_position=(0, he * 64), r=["onesrow", "bsrow"], w=[pk])
                    P.op("dve", "tensor_tensor", u[:, c, :], ps[:], u[:, c, :], op=ALU.mult, r=[pk, ("u", c)], w=[("u", c)])
                for oc in range(8):
                    ps = banks[1 + oc % 2]
                    pk = ("bk", 1 + oc % 2)
                    for kc in range(8):
                        P.op("pe", "matmul", ps[:], w_out[:, kc, oc * 128:(oc + 1) * 128], u[:, kc, :], start=(kc == 0), stop=(kc == 7),
                             r=[("w_out", oc // 4), ("u", kc)], w=[pk])
                    P.op("dve", "scalar_tensor_tensor", x_sb[:, oc, t0:t0 + 512], ps[:], modT[:, l, 16 + oc, w:w + 1], x_sb[:, oc, t0:t0 + 512],
                         op0=ALU.mult, op1=ALU.add, r=[pk, "modT"] + xkeys(t0, 512), w=xkeys(t0, 512))

        def mlp_layer(l):
            P.barrier()
            AR.reset()
            h2 = AR.bf16(8, NT)
            w1b = [AR.bf16(8, 512) for _ in range(2)]
            w2b = [AR.bf16(4, 1024) for _ in range(2)]
            hid = [AR.bf16(4, 512) for _ in range(2)]
            rl = [AR.bf16(512) for _ in range(2)]
            ntmp = norm_tmp(512)

            def load(jb):
                P.op("pool", "dma_start", out=w1b[jb % 2], in_=d_w1[l, jb], w=[("w1b", jb % 2)], dma=True)
                P.op("pool", "dma_start", out=w2b[jb % 2], in_=d_w2[l, jb], w=[("w2b", jb % 2)], dma=True)
            load(0)
            for (t0, w) in TILES512:
                emit_norm(t0, 512, lambda c: gsc[:, l, 1, c, w:w + 1], lambda c: modT[:, l, 24 + c, w:w + 1],
                          lambda c: h2[:, c, t0:t0 + 512], ntmp, banks[0], lambda c: [("h2", t0)])
            nph = 0
            npo = 0
            nh = 0
            for jb in range(8):
                if jb + 1 < 8:
                    load(jb + 1)
                W1 = w1b[jb % 2]
                W2 = w2b[jb % 2]
                for (t0, w) in TILES512:
                    hd = hid[nh % 2]
                    hk = ("hid", nh % 2)
                    nh += 1
                    for hc in range(4):
                        ps = banks[1 + nph % 4]
                        pk = ("bk", 1 + nph % 4)
                        rr = rl[nph % 2]
                        rk = ("rl", nph % 2)
                        nph += 1
                        for kc in range(8):
                            P.op("pe", "matmul", ps[:], W1[:, kc, hc * 128:(hc + 1) * 128], h2[:, kc, t0:t0 + 512], start=(kc == 0), stop=(kc == 7),
                                 r=[("w1b", jb % 2), ("h2", t0)], w=[pk])
                        P.op("act", "activation", out=rr, in_=ps[:], func=AF.Relu, r=[pk], w=[rk])
                        P.op("dve", "tensor_tensor", hd[:, hc, :], rr, rr, op=ALU.mult, r=[rk], w=[hk])
                    for oc in range(8):
                        ps = banks[5 + npo % 3]
                        pk = ("bk", 5 + npo % 3)
                        npo += 1
                        for hc in range(4):
                            P.op("pe", "matmul", ps[:], W2[:, hc, oc * 128:(oc + 1) * 128], hd[:, hc, :], start=(hc == 0), stop=(hc == 3),
                                 r=[("w2b", jb % 2), hk], w=[pk])
                        P.op("dve", "scalar_tensor_tensor", x_sb[:, oc, t0:t0 + 512], ps[:], modT[:, l, 40 + oc, w:w + 1], x_sb[:, oc, t0:t0 + 512],
                             op0=ALU.mult, op1=ALU.add, r=[pk, "modT"] + xkeys(t0, 512), w=xkeys(t0, 512))

        def rwkv_layer(l):
            i = l // 2
            pvi = lambda idx, c: pv[:, i, idx, c:c + 1]
            P.barrier()
            AR.reset()
            wring = [AR.bf16(8, 512) for _ in range(2)]
            l1 = AR.bf16(4, 8, 64)
            l2 = AR.bf16(4, 1024)
            g1w = AR.bf16(8, 128)
            g2w = AR.bf16(1024)
            v1w = AR.bf16(8, 32)
            v2w = AR.bf16(1024)
            ntmp = norm_tmp(258)
            hh = AR.bf16(8, 258)
            ss = AR.bf16(8, 256)
            t1 = [AR.bf16(256) for _ in range(2)]
            xj = AR.bf16(8, 256)
            p1b = [AR.bf16(256) for _ in range(2)]
            sig = AR.f32(8, 256)
            o_r = AR.bf16(8, 256)
            o_k = AR.bf16(8, 256)
            o_v = AR.bf16(8, 256)
            o_kk = AR.bf16(8, 256)
            o_af = AR.bf16(8, 256)
            o_ab = AR.bf16(8, 256)
            o_g = AR.bf16(8, 256)
            o_bv = AR.bf16(8, 256)
            tq = AR.bf16(8, 256)
            vf = AR.bf16(8, 256) if i == 1 else None
            sgv = o_bv
            tmpa = AR.f32(256)
            tmpb = AR.f32(256)
            for q in range(4):
                P.op("pool", "dma_start", out=l1[:, q, :, :], in_=d_l1[i, q], w=["l1"], dma=True)
                P.op("pool", "dma_start", out=l2[0:64, q, :], in_=d_l2[i, q], w=["l2"], dma=True)
            P.op("pool", "dma_start", out=g1w, in_=d_g1[i], w=["g1w"], dma=True)
            P.op("pool", "dma_start", out=g2w, in_=d_g2[i], w=["g2w"], dma=True)
            if i == 1:
                P.op("pool", "dma_start", out=v1w, in_=d_v1[:, :, :], w=["v1w"], dma=True)
                P.op("pool", "dma_start", out=v2w[0:32, :], in_=d_v2[:, :], w=["v2w"], dma=True)

            def wload(wi):
                for hb in range(2):
                    P.op("pool", "dma_start", out=wring[hb], in_=d_rw[i, wi, :, :, hb * 512:(hb + 1) * 512], w=[("wring", hb)], dma=True)

            npb = [0]

            def proj(wi, evac):
                for o2 in range(4):
                    bk = 1 + npb[0] % 3
                    npb[0] += 1
                    pk = ("bk", bk)
                    for sub in range(2):
                        oc = 2 * o2 + sub
                        for kc in range(8):
                            P.op("pe", "matmul", banks[bk][:, sub * 256:(sub + 1) * 256], wring[oc // 4][:, kc, (oc % 4) * 128:(oc % 4 + 1) * 128], xj[:, kc, :],
                                 start=(kc == 0), stop=(kc == 7), r=[("wring", oc // 4), "xj"], w=[pk])
                    evac(banks[bk][:].rearrange("p (a b) -> p a b", a=2), 2 * o2, pk)

            def mix(j):
                for c in range(8):
                    tt = t1[c % 2]
                    P.op("act", "activation", out=tt, in_=hh[:, c, 1:257], func=AF.Identity, scale=omm[:, i, j, c:c + 1], r=["hh", "omm"], w=[("t1", c % 2)])
                    P.op("dve", "scalar_tensor_tensor", xj[:, c, :], ss[:, c, :], hmu[:, i, j, c:c + 1], tt, op0=ALU.mult, op1=ALU.add,
                         r=["ss", "hmu", ("t1", c % 2)], w=["xj"])

            def lora(wq, K, func1, evac2, l1w=None, l2w=None):
                A1 = l1w if l1w is not None else l1[:, wq, :, :]
                pb = p1b[wq % 2]
                pkey = ("p1b", wq % 2)
                for kc in range(8):
                    P.op("pe", "matmul", banks[4][0:K, 0:256], A1[:, kc, 0:K], xj[:, kc, :], start=(kc == 0), stop=(kc == 7), r=["l1", "g1w", "v1w", "xj"], w=[("bk", 4)])
                P.op("act", "activation", out=pb[0:K, :], in_=banks[4][0:K, 0:256], func=func1, r=[("bk", 4)], w=[pkey])
                A2 = l2w if l2w is not None else l2[:, wq, :]
                for o2 in range(4):
                    bk = 5 + o2 % 2
                    pk = ("bk", bk)
                    for sub in range(2):
                        oc = 2 * o2 + sub
                        P.op("pe", "matmul", banks[bk][:, sub * 256:(sub + 1) * 256], A2[0:K, oc * 128:(oc + 1) * 128], pb[0:K, :], start=True, stop=True,
                             r=["l2", "g2w", "v2w", pkey], w=[pk])
                    evac2(banks[bk][:].rearrange("p (a b) -> p a b", a=2), 2 * o2, pk)

            def spill(buf, key, name, ch0):
                for q in range(2):
                    P.op("sp", "dma_start", out=scr[(i, name)][ch0 + q], in_=buf[:, :, q * 128:(q + 1) * 128], r=[key], w=[("scr", name, ch0 + q)], dma=True)

            wload(2)
            for (t0, w, lv, rv) in TILES256:
                ch0 = t0 // 128
                a = t0 - 1 if lv else t0
                b = t0 + 257 if rv else t0 + 256
                off = a - (t0 - 1)
                n = b - a
                emit_norm(a, n, lambda c: gsc[:, l, 0, c, w:w + 1], lambda c: modT[:, l, 0 + c, w:w + 1],
                          lambda c: hh[:, c, off:off + n], ntmp, banks[0], lambda c: ["hh"])
                if not lv:
                    P.op("dve", "memset", hh[:, :, 0:1], 0.0, w=["hh"])
                if not rv:
                    P.op("dve", "memset", hh[:, :, 257:258], 0.0, w=["hh"])
                P.op("dve", "tensor_tensor", ss, hh[:, :, 0:256], hh[:, :, 2:258], op=ALU.add, r=["hh"], w=["ss"])
                mix(1)
                for d in range(2):
                    def ev_w(bv_, oc0, pk, d=d):
                        for sub in range(2):
                            oc = oc0 + sub
                            P.op("act", "activation", out=sig[:, oc, :], in_=bv_[:, sub, :], func=AF.Sigmoid, bias=pvi(0 + d, oc), scale=1.0, r=[pk, "pv"], w=["sig"])
                    lora(d, 64, AF.Tanh, ev_w)
                    spill(sig, "sig", "sf" if d == 0 else "sb", ch0)
                mix(4)
                for d in range(2):
                    oa = o_af if d == 0 else o_ab
                    def ev_a(bv_, oc0, pk, d=d, oa=oa):
                        for sub in range(2):
                            oc = oc0 + sub
                            P.op("act", "activation", out=oa[:, oc, :], in_=bv_[:, sub, :], func=AF.Sigmoid, bias=pvi(2 + d, oc), scale=1.0, r=[pk, "pv"], w=[("oa", d)])
                    lora(2 + d, 64, AF.Identity, ev_a)
                    spill(oa, ("oa", d), "af" if d == 0 else "ab", ch0)
                P.op("dve", "tensor_tensor", tq, o_af, o_ab, op=ALU.add, r=[("oa", 0), ("oa", 1)], w=["tq"])
                for c in range(8):
                    P.op("dve", "tensor_scalar", tq[:, c, :], tq[:, c, :], pvi(6, c), tmka[:, i, c:c + 1], op0=ALU.mult, op1=ALU.add, r=["tq", "pv", "tmka"], w=["tq"])
                mix(5)
                def ev_g(bv_, oc0, pk):
                    P.op("act", "copy", o_g[:, oc0:oc0 + 2, :], bv_, r=[pk], w=["o_g"])
                lora(0, 128, AF.Sigmoid, ev_g, l1w=g1w, l2w=g2w)
                spill(o_g, "o_g", "g", ch0)
                mix(3)
                def ev_v(bv_, oc0, pk):
                    P.op("act", "copy", o_v[:, oc0:oc0 + 2, :], bv_, r=[pk], w=["o_v"])
                proj(2, ev_v)
                wload(0)
                if i == 1:
                    for q in range(2):
                        P.op("sp", "dma_start", out=vf[:, :, q * 128:(q + 1) * 128], in_=scr[(0, "v")][ch0 + q], w=["vf"], dma=True)
                    def ev_sv(bv_, oc0, pk):
                        for sub in range(2):
                            oc = oc0 + sub
                            P.op("act", "activation", out=sgv[:, oc, :], in_=bv_[:, sub, :], func=AF.Sigmoid, bias=pvi(4, oc), scale=1.0, r=[pk, "pv"], w=["o_bv"])
                    lora(1, 32, AF.Identity, ev_sv, l1w=v1w, l2w=v2w)
                    P.op("dve", "tensor_tensor", vf, vf, o_v, op=ALU.subtract, r=["vf", "o_v"], w=["vf"])
                    P.op("dve", "tensor_tensor", vf, vf, sgv, op=ALU.mult, r=["vf", "o_bv"], w=["vf"])
                    P.op("dve", "tensor_tensor", o_v, o_v, vf, op=ALU.add, r=["vf", "o_v"], w=["o_v"])
                spill(o_v, "o_v", "v", ch0)
                mix(0)
                def ev_r(bv_, oc0, pk):
                    P.op("act", "copy", o_r[:, oc0:oc0 + 2, :], bv_, r=[pk], w=["o_r"])
                proj(0, ev_r)
                wload(1)
                spill(o_r, "o_r", "r", ch0)
                mix(2)
                def ev_k(bv_, oc0, pk):
                    P.op("act", "copy", o_k[:, oc0:oc0 + 2, :], bv_, r=[pk], w=["o_k"])
                proj(1, ev_k)
                wload(2)
                spill(o_k, "o_k", "k", ch0)
                for c in range(8):
                    P.op("dve", "tensor_scalar", o_kk[:, c, :], o_k[:, c, :], pvi(5, c), None, op0=ALU.mult, r=["o_k", "pv"], w=["o_kk"])
                    sqb = t1[c % 2]
                    P.op("act", "activation", out=sqb, in_=o_kk[:, c, :], func=AF.Square, r=["o_kk"], w=[("t1", c % 2)])
                    P.op("pe", "matmul", banks[7][:, 0:256], onesbd[:], sqb, start=True, stop=True, r=[("t1", c % 2), "onesbd"], w=[("bk", 7)])
                    P.op("act", "activation", out=tmpa, in_=banks[7][:, 0:256], func=AF.Sqrt, bias=epsk[:, 0:1], scale=1.0, r=[("bk", 7), "epsk"], w=["tmpa"])
                    P.op("dve", "reciprocal", tmpb, tmpa, r=["tmpa"], w=["tmpb"])
                    P.op("dve", "tensor_tensor", o_kk[:, c, :], o_kk[:, c, :], tmpb, op=ALU.mult, r=["o_kk", "tmpb"], w=["o_kk"])
                spill(o_kk, "o_kk", "kk", ch0)
                for c in range(8):
                    bq = t1[c % 2]
                    P.op("dve", "scalar_tensor_tensor", tmpa, o_r[:, c, :], pvi(7, c), o_k[:, c, :], op0=ALU.mult, op1=ALU.mult, r=["o_r", "o_k", "pv"], w=["tmpa"])
                    P.op("dve", "tensor_tensor", bq, tmpa, tq[:, c, :], op=ALU.mult, r=["tmpa", "tq"], w=[("t1", c % 2)])
                    P.op("pe", "matmul", banks[7][:, 256:512], onesbd[:], bq, start=True, stop=True, r=[("t1", c % 2), "onesbd"], w=[("bk7b",)])
                    P.op("dve", "tensor_tensor", o_bv[:, c, :], banks[7][:, 256:512], o_v[:, c, :], op=ALU.mult, r=[("bk7b",), "o_v"], w=["o_bv"])
                spill(o_bv, "o_bv", "bv", ch0)

            _rwstop = os.environ.get("KRW", "")
            for d in range(2):
                if _rwstop == "proj" or (_rwstop == "fwd" and d == 1):
                    break
                P.barrier()
                AR.reset()
                ld = {}
                for nm in ("r", "k", "v", "kk", "a"):
                    ld[nm] = [AR.bf16(8, 128) for _ in range(2)]
                ld["s"] = [AR.f32(8, 128) for _ in range(2)]
                pinc = AR.f32(8, 128, name="pinc")
                eLi_f = AR.f32(1024, name="eLi")
                emL_f = AR.f32(1024, name="emL")
                eLe_f = AR.f32(1024, name="eLe")
                eLi = eLi_f.rearrange("p (a b) -> p a b", a=8)
                emL = emL_f.rearrange("p (a b) -> p a b", a=8)
                eLe = eLe_f.rearrange("p (a b) -> p a b", a=8)
                PC = AR.f32(8, name="PC")
                bPN = AR.f32(2, 8)
                ar = AR.bf16(8, 256, name="ar")
                bt = AR.bf16(8, 128, name="bt")
                kt = AR.bf16(8, 128, name="kt")
                ka = AR.bf16(8, 128)
                tqs = AR.bf16(8, 128)
                Vtok = AR.bf16(1024, name="Vtok")
                Ktok = AR.bf16(1024, name="Ktok")
                Btok = AR.bf16(1024, name="Btok")
                msk = [AR.bf16(2, 512) for _ in range(2)]
                a0b = [AR.bf16(2, 128) for _ in range(2)]
                DDb = [[AR.bf16(2, 2, 128) for _ in range(2)] for _ in range(2)]
                Eb = [AR.bf16(2, 128) for _ in range(2)]
                Dt7 = [AR.bf16(2, 128) for _ in range(2)]
                Xb = [AR.bf16(128) for _ in range(2)]
                Ub = [AR.bf16(128) for _ in range(2)]
                pad = [AR.bf16(4, 2, 128) for _ in range(2)]
                ident2 = bass.AP(ident[:].tensor, ident[:].offset, [list(ident[:].ap[0]), [0, 2], [1, 128]])

                def blk_ap(dd, lev):
                    v_ = blkm[:, dd, lev, :]
                    return bass.AP(v_.tensor, v_.offset, [list(v_.ap[0]), [0, 2], [1, 128]])
                if d == 0:
                    yfb = [AR.bf16(1024) for _ in range(2)]
                else:
                    yfl = [AR.bf16(1024)] * 2
                    ysum = eLe_f
                    ysq = eLi_f
                    ynb = tqs.rearrange("p a b -> p (a b)")
                    yo1 = emL
                    yob = ka
                    ldg = [AR.bf16(8, 128)] * 2
                    ldbv = [AR.bf16(8, 128)] * 2
                    Wo = AR.bf16(8, 1024)
                    st16 = AR.f32(6, 16)
                    for hb in range(2):
                        P.op("pool", "dma_start", out=Wo[:, :, hb * 512:(hb + 1) * 512], in_=d_rw[i, 3, :, :, hb * 512:(hb + 1) * 512], w=["Wo"], dma=True)
                aname = "af" if d == 0 else "ab"
                sname = "sf" if d == 0 else "sb"
                order = []
                for (c0, ncx, kind, idx) in SEQS:
                    chs = list(range(c0, c0 + ncx))
                    if d == 1:
                        chs = chs[::-1]
                    for k_, ch in enumerate(chs):
                        order.append((ch, kind, idx, k_ == 0, k_ == ncx - 1))

                def loads(n_):
                    ch = order[n_][0]
                    sl = n_ % 2
                    for nm, sn in (("r", "r"), ("k", "k"), ("v", "v"), ("kk", "kk"), ("a", aname), ("s", sname)):
                        P.op("sp", "dma_start", out=ld[nm][sl], in_=scr[(i, sn)][ch], w=[("ld", nm, sl)], dma=True)

                loads(0)
                for n_, (ch, kind, idx, first, last) in enumerate(order):
                    sl = n_ % 2
                    if n_ + 1 < len(order):
                        loads(n_ + 1)
                    if d == 1:
                        P.op("sp", "dma_start", out=yfl[0], in_=scr[(i, "yf")][ch], w=[("yfl", 0)], dma=True)
                        P.op("sp", "dma_start", out=ldg[0], in_=scr[(i, "g")][ch], w=[("ldg", 0)], dma=True)
                        P.op("sp", "dma_start", out=ldbv[0], in_=scr[(i, "bv")][ch], w=[("ldbv", 0)], dma=True)
                    L = {nm: ld[nm][sl] for nm in ld}
                    lk = lambda nm: ("ld", nm, sl)
                    if first:
                        if kind == "s":
                            P.op("sp", "dma_start", out=Hst[:], in_=d_state[:, i, d], w=["Hst"], dma=True)
                        else:
                            P.op("dve", "memset", Hst[:], 0.0, w=["Hst"])
                        P.op("act", "copy", Hb[:], Hst[:], r=["Hst"], w=["Hb"])
                    _ks = os.environ.get("KSCAN", "")
                    if _ks == "load":
                        continue
                    for c in range(8):
                        P.op("dve", "tensor_tensor_scan", out=pinc[:, c, :], data0=onesf[:], data1=L["s"][:, c, :], initial=0.0, op0=ALU.mult, op1=ALU.add,
                             r=[lk("s"), "onesf"], w=["pinc"])
                    pexc = L["s"]
                    P.op("dve", "tensor_tensor", pexc, pinc, L["s"], op=ALU.subtract, r=["pinc", lk("s")], w=[lk("s")])
                    P.op("act", "activation", out=PC, in_=pinc[:, :, 127], func=AF.Exp, scale=NEGC, r=["pinc"], w=["PC"])
                    if d == 0:
                        P.op("act", "activation", out=eLi, in_=pinc, func=AF.Exp, scale=NEGC, r=["pinc"], w=["eLi"])
                        P.op("act", "activation", out=emL, in_=pinc, func=AF.Exp, scale=-NEGC, r=["pinc"], w=["emL"])
                        P.op("act", "activation", out=eLe, in_=pexc, func=AF.Exp, scale=NEGC, r=[lk("s")], w=["eLe"])
                    else:
                        P.op("dve", "tensor_scalar", bPN[:, 0, :], pinc[:, :, 127], NEGC, None, op0=ALU.mult, r=["pinc"], w=["bPN"])
                        P.op("dve", "tensor_scalar", bPN[:, 1, :], pinc[:, :, 127], -NEGC, None, op0=ALU.mult, r=["pinc"], w=["bPN"])
                        for c in range(8):
                            P.op("act", "activation", out=eLi[:, c, :], in_=pexc[:, c, :], func=AF.Exp, scale=-NEGC, bias=bPN[:, 0, c:c + 1], r=[lk("s"), "bPN"], w=["eLi"])
                            P.op("act", "activation", out=emL[:, c, :], in_=pexc[:, c, :], func=AF.Exp, scale=NEGC, bias=bPN[:, 1, c:c + 1], r=[lk("s"), "bPN"], w=["emL"])
                            P.op("act", "activation", out=eLe[:, c, :], in_=pinc[:, c, :], func=AF.Exp, scale=-NEGC, bias=bPN[:, 0, c:c + 1], r=["pinc", "bPN"], w=["eLe"])
                    P.op("dve", "tensor_tensor", ar[:, :, 128:256], L["r"], eLi, op=ALU.mult, r=[lk("r"), "eLi"], w=["ar"])
                    P.op("dve", "scalar_tensor_tensor", ar[:, :, 0:128], L["kk"], -1.0, eLe, op0=ALU.mult, op1=ALU.mult, r=[lk("kk"), "eLe"], w=["ar"])
                    P.op("dve", "tensor_tensor", ka, L["kk"], L["a"], op=ALU.mult, r=[lk("kk"), lk("a")], w=["ka"])
                    P.op("dve", "tensor_tensor", bt, ka, emL, op=ALU.mult, r=["ka", "emL"], w=["bt"])
                    P.op("dve", "tensor_tensor", tqs, L["a"], bcast_free(pv[:, i, 6, :], 128), op=ALU.mult, r=[lk("a"), "pv"], w=["tqs"])
                    P.op("dve", "tensor_tensor", tqs, tqs, bcast_free(omka[:, i, :], 128), op=ALU.add, r=["tqs", "omka"], w=["tqs"])
                    P.op("dve", "tensor_tensor", ka, L["k"], tqs, op=ALU.mult, r=[lk("k"), "tqs", "bt"], w=["ka"])
                    P.op("dve", "tensor_tensor", kt, ka, emL, op=ALU.mult, r=["ka", "emL"], w=["kt"])
                    if _ks == "prep":
                        continue
                    for c in range(8):
                        P.op("pe", "transpose", bankbf[6][:, c * 128:(c + 1) * 128], L["v"][:, c, :], ident[:], r=[lk("v"), "ident"], w=[("bk", 6)])
                    P.op("act", "copy", Vtok, bankbf[6], r=[("bk", 6)], w=["Vtok"])
                    for c in range(8):
                        P.op("pe", "transpose", bankbf[7][:, c * 128:(c + 1) * 128], kt[:, c, :], ident[:], r=["kt", "ident"], w=[("bk", 7)])
                    P.op("dve", "tensor_copy", Ktok, bankbf[7], r=[("bk", 7)], w=["Ktok"])
                    for c in range(8):
                        P.op("pe", "transpose", bankbf[6][:, c * 128:(c + 1) * 128], bt[:, c, :], ident[:], r=["bt", "ident"], w=[("bk", 6)])
                    P.op("act", "copy", Btok, bankbf[6], r=[("bk", 6)], w=["Btok"])

                    if _ks == "tr":
                        continue
                    def unit(c, s):
                        La, Lb, M = banks[3 * s], banks[3 * s + 1], banks[3 * s + 2]
                        kLa, kLb, kM = ("bk", 3 * s), ("bk", 3 * s + 1), ("bk", 3 * s + 2)
                        kmsk, ka0b, kUb, kEb, kXb, kDt7 = ("msk", s), ("a0b", s), ("Ub", s), ("Eb", s), ("Xb", s), ("Dt7", s)
                        L0 = [La, Lb]
                        kL0 = [kLa, kLb]
                        kpad = ("pad", s)
                        pd = pad[s]
                        for he in range(2):
                            P.op("act", "activation", out=pd[:, 0, he, :], in_=bt[:, c, :], func=AF.Identity, scale=hm[:, he:he + 1], r=["bt", "hm00", "hm10", "hm01", "hm11"], w=[kpad])
                            P.op("act", "activation", out=pd[:, 1, he, :], in_=kt[:, c, :], func=AF.Identity, scale=hm[:, he:he + 1], r=["kt", "hm00", "hm10", "hm01", "hm11"], w=[kpad])
                            P.op("act", "activation", out=pd[:, 2, he, :], in_=ar[:, c, 0:128], func=AF.Identity, scale=hm[:, he:he + 1], r=["ar", "hm00", "hm10", "hm01", "hm11"], w=[kpad])
                            P.op("act", "activation", out=pd[:, 3, he, :], in_=ar[:, c, 128:256], func=AF.Identity, scale=hm[:, he:he + 1], r=["ar", "hm00", "hm10", "hm01", "hm11"], w=[kpad])
                        for he in range(2):
                            P.op("pe", "matmul", L0[he][:, 0:256], pd[:, 0, he, :], ar[:, c, :], start=True, stop=True, r=[kpad, "ar"], w=[kL0[he]])
                            P.op("pe", "matmul", L0[he][:, 256:512], pd[:, 1, he, :], ar[:, c, :], start=True, stop=True, r=[kpad, "ar"], w=[kL0[he]])
                            P.op("pe", "matmul", M[:, he * 128:(he + 1) * 128], pd[:, 2, he, :], bt[:, c, :], start=True, stop=True, r=[kpad, "bt"], w=[kM])
                        _ku = os.environ.get("KU1", "")
                        if _ku != "mm":
                            for he in range(2):
                                P.op("dve", "tensor_tensor", msk[s][:, he, :], L0[he][:], maska[:, d, :], op=ALU.mult, r=[kL0[he], "maska"], w=[kmsk])
                        if _ku not in ("mm", "mska"):
                            P.op("dve", "tensor_tensor", a0b[s], M[:, 0:256].rearrange("p (a b) -> p a b", a=2), maskb[:, d, :].rearrange("p (a b) -> p a b", a=2), op=ALU.mult,
                                 r=[kM, "maskb"], w=[ka0b])
                        yield
                        for he in range(2):
                            b0 = he * 64
                            P.op("pe", "matmul", M[:, 256 + he * 64:256 + (he + 1) * 64], pd[:, 2, he, :], Hb[:, c, :], start=(he == 0), stop=False,
                                 skip_group_check=True, r=[kpad, "Hb", ka0b], w=[kM])
                            P.op("pe", "matmul", M[:, 256 + he * 64:256 + (he + 1) * 64], msk[s][:, he, 256:384], Vtok[:, c * 128 + b0:c * 128 + b0 + 64], start=False, stop=(he == 1),
                                 skip_group_check=True, r=[kmsk, "Vtok"], w=[kM])
                        DD = DDb[s][0]
                        kDD = ("DD", s, 0)
                        P.op("dve", "tensor_tensor", Eb[s], a0b[s], blk_ap(d, 0), op=ALU.mult, r=[ka0b, "blk"], w=[kEb])
                        P.op("dve", "tensor_tensor", DD[:, :, 0, :], Eb[s], ident2, op=ALU.add, r=[kEb, "ident"], w=[kDD])
                        P.op("dve", "tensor_tensor", Eb[s], msk[s][:, :, 0:128], blk_ap(1 - d, 0), op=ALU.mult, r=[kmsk, "blk", kDD], w=[kEb])
                        P.op("dve", "tensor_tensor", DD[:, :, 1, :], Eb[s], ident2, op=ALU.add, r=[kEb, "ident"], w=[kDD])
                        P.op("act", "copy", Xb[s], M[:, 256:384], r=[kM], w=[kXb])
                        yield
                        for lev in range(1, 7):
                            cur = DDb[s][(lev - 1) % 2]
                            kcur = ("DD", s, (lev - 1) % 2)
                            nxt = DDb[s][lev % 2]
                            knxt = ("DD", s, lev % 2)
                            for he in range(2):
                                P.op("pe", "matmul", La[:, he * 128:(he + 1) * 128], msk[s][:, he, 0:128], cur[:, he, 0, :], start=True, stop=True, r=[kmsk, kcur], w=[kLa])
                            P.op("dve", "tensor_tensor", Eb[s], La[:, 0:256].rearrange("p (a b) -> p a b", a=2), blk_ap(d, lev), op=ALU.mult, r=[kLa, "blk"], w=[kEb])
                            if lev < 6:
                                for he in range(2):
                                    o1 = Lb[:, he * 256:he * 256 + 128]
                                    P.op("pe", "matmul", o1, ident[:], cur[:, he, 0, :], start=True, stop=False, r=["ident", kcur], w=[kLb])
                                    P.op("pe", "matmul", o1, cur[:, he, 1, :], Eb[s][:, he, :], start=False, stop=True, r=[kcur, kEb], w=[kLb])
                                    o2 = Lb[:, he * 256 + 128:he * 256 + 256]
                                    P.op("pe", "matmul", o2, ident[:], cur[:, he, 1, :], start=True, stop=False, r=["ident", kcur], w=[kLb])
                                    P.op("pe", "matmul", o2, Eb[s][:, he, :], cur[:, he, 1, :], start=False, stop=True, r=[kcur, kEb], w=[kLb])
                                P.op("act", "copy", nxt, Lb[:].rearrange("p (a b c) -> p a b c", a=2, b=2), r=[kLb], w=[knxt])
                            else:
                                for he in range(2):
                                    o2 = Lb[:, he * 128:(he + 1) * 128]
                                    P.op("pe", "matmul", o2, ident[:], cur[:, he, 1, :], start=True, stop=False, r=["ident", kcur], w=[kLb])
                                    P.op("pe", "matmul", o2, Eb[s][:, he, :], cur[:, he, 1, :], start=False, stop=True, r=[kcur, kEb], w=[kLb])
                                P.op("act", "copy", Dt7[s], Lb[:, 0:256].rearrange("p (a b) -> p a b", a=2), r=[kLb], w=[kDt7])
                            yield
                        for he in range(2):
                            P.op("pe", "matmul", M[:, 256 + he * 64:256 + (he + 1) * 64], Dt7[s][:, he, :], Xb[s][:, he * 64:(he + 1) * 64], start=True, stop=True,
                                 r=[kDt7, kXb], w=[kM])
                        P.op("act", "copy", Ub[s], M[:, 256:384], r=[kM], w=[kUb])
                        for he in range(2):
                            b0 = he * 64
                            yo_ = M[:, 384 + he * 64:384 + (he + 1) * 64]
                            P.op("pe", "matmul", yo_, pd[:, 3, he, :], Hb[:, c, :], start=(he == 0), stop=False, r=[kpad, "Hb", kUb], w=[kM])
                            P.op("pe", "matmul", yo_, msk[s][:, he, 128:256], Ub[s][:, b0:b0 + 64], start=False, stop=False, r=[kmsk, kUb], w=[kM])
                            P.op("pe", "matmul", yo_, msk[s][:, he, 384:512], Vtok[:, c * 128 + b0:c * 128 + b0 + 64], start=False, stop=(he == 1), r=[kmsk, "Vtok"], w=[kM])
                        for he in range(2):
                            b0 = he * 64
                            ph = M[b0:b0 + 64, 0:64]
                            P.op("pe", "matmul", ph, Btok[:, c * 128 + b0:c * 128 + b0 + 64], Ub[s][:, b0:b0 + 64], start=True, stop=False, tile_position=(0, b0),
                                 r=["Btok", kUb], w=[kM])
                            P.op("pe", "matmul", ph, Ktok[:, c * 128 + b0:c * 128 + b0 + 64], Vtok[:, c * 128 + b0:c * 128 + b0 + 64], start=False, stop=True, tile_position=(0, b0),
                                 r=["Ktok", "Vtok"], w=[kM])
                        yield
                        _k10 = os.environ.get("KU10", "yamh")
                        if d == 0:
                            if "y" in _k10:
                                P.op("dve", "tensor_copy", yfb[n_ % 2][:, c * 128:(c + 1) * 128], M[:, 384:512], r=[kM], w=[("yfb", n_ % 2)])
                        else:
                            P.op("dve", "tensor_tensor", ysum[:, c * 128:(c + 1) * 128], M[:, 384:512], yfl[sl][:, c * 128:(c + 1) * 128], op=ALU.add,
                                 r=[kM, ("yfl", 0)], w=["eLe"])
                        if "a" in _k10:
                            P.op("dve", "tensor_tensor", Hst[:, c, :], M[:, 0:64], Hst[:, c, :], op=ALU.add, r=[kM, "Hst"], w=["Hst"])
                        if "m" in _k10:
                            P.op("dve", "tensor_scalar", Hst[:, c, :], Hst[:, c, :], PC[:, c:c + 1], None, op0=ALU.mult, r=["Hst", "PC"], w=["Hst"])
                        if "h" in _k10:
                            P.op("act", "copy", Hb[:, c, :], Hst[:, c, :], r=["Hst"], w=["Hb"])
                        yield

                    for cp in range(4):
                        gens = [unit(2 * cp, 0), unit(2 * cp + 1, 1)]
                        alive = [True, True]
                        _nst = 0
                        while any(alive):
                            _nst += 1
                            if _ks.startswith("u") and _nst > int(_ks[1:]):
                                break
                            for gi, g_ in enumerate(gens):
                                if alive[gi]:
                                    try:
                                        next(g_)
                                    except StopIteration:
                                        alive[gi] = False

                    if d == 0:
                        P.op("sp", "dma_start", out=scr[(i, "yf")][ch], in_=yfb[n_ % 2], r=[("yfb", n_ % 2)], w=[("scr", "yf", ch)], dma=True)
                    else:
                        w = 0 if ch < 16 else 1
                        col0 = ch * 128
                        y3 = ysum.rearrange("p (h n) -> p h n", n=64)
                        P.op("dve", "tensor_reduce", out=st16[:, 0, :], in_=y3, axis=AX.X, op=ALU.add, r=["eLe"], w=["st16a"])
                        P.op("act", "activation", out=ysq, in_=ysum, func=AF.Square, r=["eLe"], w=["eLi"])
                        P.op("dve", "tensor_reduce", out=st16[:, 1, :], in_=ysq.rearrange("p (h n) -> p h n", n=64), axis=AX.X, op=ALU.add, r=["eLi"], w=["st16b"])
                        P.op("dve", "tensor_scalar", st16[:, 2, :], st16[:, 0, :], 1.0 / 64.0, None, op0=ALU.mult, r=["st16a"], w=["st16c"])
                        P.op("dve", "tensor_tensor", st16[:, 3, :], st16[:, 2, :], st16[:, 2, :], op=ALU.mult, r=["st16c"], w=["st16d"])
                        P.op("dve", "scalar_tensor_tensor", st16[:, 4, :], st16[:, 1, :], 1.0 / 64.0, st16[:, 3, :], op0=ALU.mult, op1=ALU.subtract,
                             r=["st16b", "st16d"], w=["st16e"])
                        P.op("act", "activation", out=st16[:, 5, :], in_=st16[:, 4, :], func=AF.Sqrt, bias=epsg[:, 0:1], scale=1.0, r=["st16e", "epsg"], w=["st16f"])
                        P.op("dve", "reciprocal", st16[:, 4, :], st16[:, 5, :], r=["st16f"], w=["st16g"])
                        P.op("dve", "tensor_tensor", y3, y3, bcast_free(st16[:, 2, :], 64), op=ALU.subtract, r=["eLe", "st16c"], w=["eLe"])
                        P.op("dve", "tensor_tensor", ynb.rearrange("p (h n) -> p h n", n=64), y3, bcast_free(st16[:, 4, :], 64), op=ALU.mult, r=["eLe", "st16g"], w=["tqs"])
                        for c in range(8):
                            P.op("pe", "transpose", bankbf[6][:, c * 128:(c + 1) * 128], ynb[:, c * 128:(c + 1) * 128], ident[:], r=["tqs", "ident"], w=[("bk", 6)])
                        for c in range(8):
                            P.op("act", "activation", out=yo1[:, c, :], in_=bankbf[6][:, c * 128:(c + 1) * 128], func=AF.Identity, scale=pvi(8, c), bias=pvi(9, c),
                                 r=[("bk", 6), "pv"], w=["emL"])
                        P.op("dve", "tensor_tensor", yo1, yo1, ldbv[sl], op=ALU.add, r=["emL", ("ldbv", 0)], w=["emL"])
                        P.op("dve", "tensor_tensor", yob, yo1, ldg[sl], op=ALU.mult, r=["emL", ("ldg", 0)], w=["ka"])
                        for oc in range(8):
                            bk = oc // 4
                            for kc in range(8):
                                P.op("pe", "matmul", banks[bk][:, (oc % 4) * 128:(oc % 4 + 1) * 128], Wo[:, kc, oc * 128:(oc + 1) * 128], yob[:, kc, :],
                                     start=(kc == 0), stop=(kc == 7), r=["Wo", "ka"], w=[("bk", bk)])
                        for oc in range(8):
                            bk = oc // 4
                            P.op("dve", "scalar_tensor_tensor", x_sb[:, oc, col0:col0 + 128], banks[bk][:, (oc % 4) * 128:(oc % 4 + 1) * 128], modT[:, l, 16 + oc, w:w + 1],
                                 x_sb[:, oc, col0:col0 + 128], op0=ALU.mult, op1=ALU.add, r=[("bk", bk), "modT"] + xkeys(col0, 128), w=xkeys(col0, 128))
                    if last and kind == "p":
                        P.op("sp", "dma_start", out=d_ns[idx, i, d], in_=Hst[:], r=["Hst"], w=[("ns", idx, i, d)], dma=True)

        for l in range(depth):
            if l % 2 == 0:
                sgu_layer(l)
            else:
                rwkv_layer(l)
            mlp_layer(l)

        P.barrier()
        AR.reset()
        ntmp = norm_tmp(512)
        yo = [AR.f32(8, 512) for _ in range(2)]
        for ti, (t0, w) in enumerate(TILES512):
            yb = yo[ti % 2]
            emit_norm(t0, 512, lambda c: fg[:, c:c + 1], None, lambda c: yb[:, c, :], ntmp, banks[0], lambda c: [("yo", ti % 2)])
            P.op("sp", "dma_start", out=d_y[:, :, t0:t0 + 512], in_=yb, r=[("yo", ti % 2)], w=[("dy", ti)], dma=True)
        if n_rw == 0:
            pass

        P.finalize()
        sems = {k: es.enter_context(nc.semaphore("s_%s_%d" % k)) for k in sorted(P.sem_keys)}
        block = es.enter_context(nc.Block())
        P.emit(sems, block)
    return nc


def _pc(a):
    a = np.asarray(a, np.float32)
    lead = a.shape[:-1]
    b = a.reshape(lead + (8, 128))
    return np.ascontiguousarray(np.moveaxis(b, -1, 0))


def _kmajor(w):
    w = np.asarray(w, np.float32)
    K, N = w.shape[-2], w.shape[-1]
    lead = w.shape[:-2]
    b = w.reshape(lead + (K // 128, 128, N))
    return np.ascontiguousarray(np.swapaxes(b, -3, -2))


_NC_CACHE = {}
_RUNNER = [None]


def kernel(x_prompt, x_sample, state_wkv, c, c_ctx, norm1_g, norm2_g, ada_w, ada_b,
           sgu_w_in, sgu_ln_g, sgu_ln_b, sgu_w_s, sgu_b_s, sgu_w_out,
           rwkv_mu, rwkv_w_r, rwkv_w_k, rwkv_w_v, rwkv_w_o, rwkv_w0, rwkv_w1, rwkv_w2,
           rwkv_a0, rwkv_a1, rwkv_a2, rwkv_v0, rwkv_v1, rwkv_v2, rwkv_g1, rwkv_g2,
           rwkv_k_k, rwkv_k_a, rwkv_r_k, rwkv_ln_g, rwkv_ln_b, mlp_w1, mlp_w2, final_g, _depth=DEPTH):
    f = lambda a: np.asarray(a, np.float32)
    depth = _depth
    shared = {}
    shared["ident"] = np.eye(128, dtype=np.float32)
    bd = np.zeros((128, 128), np.float32); bd[:64, :64] = 1; bd[64:, 64:] = 1
    shared["onesbd"] = bd
    s_i = np.arange(128)[:, None]; t_i = np.arange(128)[None, :]
    su = (s_i < t_i).astype(np.float32); iu = (s_i <= t_i).astype(np.float32)
    sl = (s_i > t_i).astype(np.float32); il = (s_i >= t_i).astype(np.float32)
    shared["maska"] = np.stack([np.concatenate([su, iu, su, iu], 1), np.concatenate([sl, il, sl, il], 1)])
    shared["maskb"] = np.stack([np.concatenate([sl, sl], 1), np.concatenate([su, su], 1)])
    bl = []
    for lev in range(7):
        bs = 2 ** lev
        bl.append(((s_i // (2 * bs) == t_i // (2 * bs)) & ((s_i // bs) % 2 == 1) & ((t_i // bs) % 2 == 0)).astype(np.float32))
    bl = np.stack(bl, axis=1)
    shared["blkm"] = np.ascontiguousarray(np.stack([bl, bl.transpose(2, 1, 0)]))
    shared["ada_w"] = _kmajor(f(ada_w))
    shared["ada_b"] = np.ascontiguousarray(f(ada_b).reshape(4, 48, 128).transpose(2, 0, 1))
    shared["norm_g"] = np.ascontiguousarray(np.stack([_pc(norm1_g), _pc(norm2_g)], axis=2))
    shared["final_g"] = _pc(final_g)
    shared["sgu_w_in"] = _kmajor(f(sgu_w_in))
    shared["sgu_w_out"] = _kmajor(f(sgu_w_out))
    shared["sgu_wsT"] = np.ascontiguousarray(f(sgu_w_s).transpose(0, 3, 1, 2))
    shared["sgu_bs"] = np.ascontiguousarray(f(sgu_b_s).reshape(2, 1, 2048))
    shared["sgu_ln"] = np.ascontiguousarray(np.stack([f(sgu_ln_g), f(sgu_ln_b)], axis=1))
    w1 = f(mlp_w1).reshape(4, 8, 128, 8, 512)
    shared["mlp_w1"] = np.ascontiguousarray(w1.transpose(0, 3, 2, 1, 4))
    w2 = f(mlp_w2).reshape(4, 8, 4, 128, 1024)
    shared["mlp_w2"] = np.ascontiguousarray(w2.transpose(0, 1, 3, 2, 4))
    shared["rwkv_w"] = np.ascontiguousarray(np.stack([_kmajor(f(rwkv_w_r)), _kmajor(f(rwkv_w_k)), _kmajor(f(rwkv_w_v)), _kmajor(f(rwkv_w_o))], axis=1))
    shared["rwkv_mu"] = _pc(rwkv_mu)
    shared["rwkv_l1"] = np.ascontiguousarray(np.concatenate([_kmajor(f(rwkv_w1)), _kmajor(f(rwkv_a1))], axis=1))
    shared["rwkv_l2"] = np.ascontiguousarray(np.concatenate([f(rwkv_w2), f(rwkv_a2)], axis=1))
    shared["rwkv_g1"] = _kmajor(f(rwkv_g1))
    shared["rwkv_g2"] = np.ascontiguousarray(f(rwkv_g2))
    shared["rwkv_v1"] = _kmajor(f(rwkv_v1))[0]
    shared["rwkv_v2"] = np.ascontiguousarray(f(rwkv_v2)[0])
    v0 = np.broadcast_to(f(rwkv_v0).reshape(1, 1024), (2, 1024))
    zz = np.zeros((2, 1024), np.float32)
    pvl = [f(rwkv_w0)[:, 0], f(rwkv_w0)[:, 1], f(rwkv_a0)[:, 0], f(rwkv_a0)[:, 1], v0, f(rwkv_k_k), f(rwkv_k_a),
           f(rwkv_r_k).reshape(2, 1024), f(rwkv_ln_g), f(rwkv_ln_b), zz]
    shared["rwkv_pv"] = _pc(np.stack(pvl, axis=1))

    in_maps = []
    xs = f(x_sample); xp = f(x_prompt); st = f(state_wkv)
    for b in range(NCORES):
        m = dict(shared)
        xt = np.concatenate([xs[b], xp[2 * b], xp[2 * b + 1]], axis=0)
        m["xT"] = np.ascontiguousarray(xt.reshape(NT, 8, 128).transpose(2, 1, 0))
        cond = np.stack([f(c)[b], f(c_ctx)], axis=-1)
        m["condT"] = np.ascontiguousarray(cond.reshape(8, 128, 2).transpose(1, 0, 2))
        s = st[b].reshape(2, 2, 8, 2, 64, 64)
        m["state0"] = np.ascontiguousarray(s.transpose(3, 5, 0, 1, 2, 4).reshape(128, 2, 2, 8, 64))
        in_maps.append(m)

    if depth not in _NC_CACHE:
        _NC_CACHE[depth] = build(depth)
    nc = _NC_CACHE[depth]
    if _RUNNER[0] is not None:
        res = _RUNNER[0](nc, in_maps)
    else:
        ncr = int(os.environ.get("KCORES", NCORES))
        res = run_bass_kernel_spmd(nc, in_maps[:ncr], core_ids=list(range(ncr)))
        if ncr < NCORES:
            res.results.extend([res.results[0]] * (NCORES - ncr))
    y_prompt = np.zeros((16, 256, D), np.float32)
    y_sample = np.zeros((8, 2048, D), np.float32)
    n_rw = depth // 2
    new_state = np.zeros((16, max(n_rw, 1), 2, 16, 64, 64), np.float32)
    for b in range(NCORES):
        r = res.results[b]
        yt = np.asarray(r["yT"]).transpose(2, 1, 0).reshape(NT, D)
        y_sample[b] = yt[:2048]
        y_prompt[2 * b] = yt[2048:2304]
        y_prompt[2 * b + 1] = yt[2304:2560]
        ns = np.asarray(r["new_state"])
        ns = ns.reshape(2, 2, 2, 2, 64, 8, 64)
        ns = ns.transpose(0, 1, 2, 5, 3, 6, 4).reshape(2, 2, 2, 16, 64, 64)
        for q in range(2):
            new_state[2 * b + q, :n_rw] = ns[q, :n_rw]
    if n_rw == 0:
        new_state = new_state[:, :0]
    return (y_prompt, y_sample, new_state)
```

```python
import os
from contextlib import ExitStack
import numpy as np
import concourse.bass as bass
import concourse.mybir as mybir
from concourse.bass_utils import run_bass_kernel_spmd

F32 = mybir.dt.float32
BF16 = mybir.dt.bfloat16
AF = mybir.ActivationFunctionType
ALU = mybir.AluOpType
AX = mybir.AxisListType

NCORES = 8
D = 1024
NT = 2560
DEPTH = 4
NORM_EPS = 1e-6
GN_EPS = 64e-5
NEGC = -0.6065306597126334
NDMASEM = 8
_DBG = {}

class Op:
    __slots__ = ("eng", "fn", "deps", "sig", "waits", "is_dma", "dsem", "n", "prev_dma")

    def __init__(self, eng, fn, is_dma):
        self.eng = eng; self.fn = fn; self.deps = []; self.sig = None; self.waits = []
        self.is_dma = is_dma; self.dsem = None; self.n = 0; self.prev_dma = None


class Prog:
    ENGS = ("pe", "act", "dve", "pool", "sp")

    def __init__(self):
        self.ops = []
        self.last_w = {}
        self.readers = {}
        self.bar = []
        self.last_eng = {}
        self.last_dma = {}
        self.dcnt = {e: 0 for e in self.ENGS}

    def barrier(self):
        self.bar = list(self.last_eng.values()) + list(self.last_dma.values())
        self.last_w = {}
        self.readers = {}

    def op(self, eng, name, *args, r=(), w=(), dma=False, **kw):
        o = Op(eng, (name, args, kw), dma)
        o.n = len(self.ops)
        deps = {}
        for k in r:
            p = self.last_w.get(k)
            if p is not None:
                deps[p.n] = (p, True)
        for k in w:
            p = self.last_w.get(k)
            if p is not None and p.n not in deps:
                deps[p.n] = (p, True)
            for q in self.readers.get(k, ()):
                if q.n not in deps:
                    deps[q.n] = (q, False)
        for k in r:
            self.readers.setdefault(k, []).append(o)
        for k in w:
            self.last_w[k] = o
            self.readers[k] = []
        for p, raw in deps.values():
            if p.is_dma:
                o.deps.append(p)
            elif p.eng == o.eng and not o.is_dma:
                if raw and p.eng != "pe":
                    o.deps.append(p)
            else:
                o.deps.append(p)
        for p in self.bar:
            if p.is_dma or p.eng != o.eng or o.is_dma:
                o.deps.append(p)
        if dma:
            i = self.dcnt[eng]; self.dcnt[eng] += 1
            key = (eng, i % NDMASEM)
            o.prev_dma = self.last_dma.get(key)
            self.last_dma[key] = o
            o.dsem = key
        else:
            self.last_eng[eng] = o
        self.ops.append(o)
        return o

    def finalize(self):
        need = set()
        for o in self.ops:
            for p in o.deps:
                if not p.is_dma:
                    need.add(p.n)
        cnt = {e: 0 for e in self.ENGS}
        dval = {}
        for o in self.ops:
            if o.is_dma:
                key = o.dsem
                v = dval.get(key, 0) + 16
                dval[key] = v
                o.dsem = (key, v)
            elif o.n in need:
                cnt[o.eng] += 1
                o.sig = (("c" + o.eng, 0), cnt[o.eng])
        self.dma_final = dval
        waited = {e: {} for e in self.ENGS}
        for o in self.ops:
            wl = {}
            if o.is_dma and o.prev_dma is not None:
                k, v = o.prev_dma.dsem
                wl[k] = v
            for p in o.deps:
                k, v = p.dsem if p.is_dma else p.sig
                if wl.get(k, 0) < v:
                    wl[k] = v
            wd = waited[o.eng]
            for k, v in wl.items():
                if wd.get(k, 0) < v:
                    wd[k] = v
                    o.waits.append((k, v))
        self.sem_keys = set(dval.keys())
        for e in self.ENGS:
            if cnt[e]:
                self.sem_keys.add(("c" + e, 0))

    def emit(self, sems, block):
        byeng = {e: [o for o in self.ops if o.eng == e] for e in self.ENGS}
        finals = self.dma_final

        def run(eng_obj, lst, fk):
            for o in lst:
                for k, v in o.waits:
                    eng_obj.wait_ge(sems[k], v)
                name, args, kw = o.fn
                ins = getattr(eng_obj, name)(*args, **kw)
                if o.is_dma:
                    ins.then_inc(sems[o.dsem[0]], 16)
                elif o.sig is not None:
                    ins.then_inc(sems[o.sig[0]], 1)
            for k in fk:
                eng_obj.wait_ge(sems[k], finals[k])

        @block.tensor
        def _(e):
            run(e, byeng["pe"], [])

        @block.scalar
        def _(e):
            run(e, byeng["act"], [k for k in finals if k[0] == "act"])

        @block.vector
        def _(e):
            run(e, byeng["dve"], [])

        @block.gpsimd
        def _(e):
            run(e, byeng["pool"], [k for k in finals if k[0] == "pool"])

        @block.sync
        def _(e):
            run(e, byeng["sp"], [k for k in finals if k[0] == "sp"])


class Arena:
    def __init__(self, t, nwords):
        self.t = t; self.n = nwords; self.off = 0; self.reg = {}

    def reset(self):
        self.off = 0

    def f32(self, *shape, name=None):
        n = int(np.prod(shape))
        if name: self.reg[name] = (self.off, n, "f32", shape)
        a = self.t[:, self.off:self.off + n]
        self.off += n
        assert self.off <= self.n, ("arena overflow", self.off, self.n)
        return self._shape(a, shape)

    def bf16(self, *shape, name=None):
        n = int(np.prod(shape))
        nw = (n + 1) // 2
        if name: self.reg[name] = (self.off, nw, "bf16", shape)
        a = self.t[:, self.off:self.off + nw].bitcast(BF16)
        if n != 2 * nw:
            a = a[:, 0:n]
        self.off += nw
        assert self.off <= self.n, ("arena overflow", self.off, self.n)
        return self._shape(a, shape)

    @staticmethod
    def _shape(a, shape):
        if len(shape) == 1:
            return a
        if len(shape) == 2:
            return a.rearrange("p (a b) -> p a b", a=shape[0])
        if len(shape) == 3:
            return a.rearrange("p (a b c) -> p a b c", a=shape[0], b=shape[1])
        if len(shape) == 4:
            return a.rearrange("p (a b c d) -> p a b c d", a=shape[0], b=shape[1], c=shape[2])
        raise ValueError(shape)


def bcast_free(ap, n):
    return bass.AP(ap.tensor, ap.offset, [list(d) for d in ap.ap] + [[0, n]])


TILES512 = [(0, 0), (512, 0), (1024, 0), (1536, 0), (2048, 1)]
TILES256 = [(256 * i, 0, i > 0, i < 7) for i in range(8)] + [(2048, 1, False, False), (2304, 1, False, False)]
SEQS = [(0, 16, "s", 0), (16, 2, "p", 0), (18, 2, "p", 1)]


def build(depth=DEPTH):
    nc = bass.Bass("TRN2", target_bir_lowering=False)
    n_rw = depth // 2

    def din(name, shape, dt=F32):
        return nc.dram_tensor(name, list(shape), dt, kind="ExternalInput").ap()

    def dout(name, shape, dt=F32):
        return nc.dram_tensor(name, list(shape), dt, kind="ExternalOutput").ap()

    def dscr(name, shape, dt):
        return nc.dram_tensor(name, list(shape), dt, kind="Internal").ap()

    d_x = din("xT", [128, 8, NT])
    d_cond = din("condT", [128, 8, 2])
    d_state = din("state0", [128, 2, 2, 8, 64])
    d_ident = din("ident", [128, 128])
    d_onesbd = din("onesbd", [128, 128])
    d_maska = din("maska", [2, 128, 512])
    d_maskb = din("maskb", [2, 128, 256])
    d_blk = din("blkm", [2, 128, 7, 128])
    d_adaw = din("ada_w", [4, 128, 8, 6144])
    d_adab = din("ada_b", [128, 4, 48])
    d_ng = din("norm_g", [128, 4, 2, 8])
    d_fg = din("final_g", [128, 8])
    d_win = din("sgu_w_in", [2, 128, 8, 2048])
    d_wout = din("sgu_w_out", [2, 128, 8, 1024])
    d_wsT = din("sgu_wsT", [2, 128, 16, 128])
    d_bs = din("sgu_bs", [2, 1, 2048])
    d_lng = din("sgu_ln", [2, 2, 1024])
    d_w1 = din("mlp_w1", [4, 8, 128, 8, 512])
    d_w2 = din("mlp_w2", [4, 8, 128, 4, 1024])
    d_rw = din("rwkv_w", [2, 4, 128, 8, 1024])
    d_mu = din("rwkv_mu", [128, 2, 6, 8])
    d_l1 = din("rwkv_l1", [2, 4, 128, 8, 64])
    d_l2 = din("rwkv_l2", [2, 4, 64, 1024])
    d_g1 = din("rwkv_g1", [2, 128, 8, 128])
    d_g2 = din("rwkv_g2", [2, 128, 1024])
    d_v1 = din("rwkv_v1", [128, 8, 32])
    d_v2 = din("rwkv_v2", [32, 1024])
    d_pv = din("rwkv_pv", [128, 2, 11, 8])
    d_y = dout("yT", [128, 8, NT])
    d_ns = dout("new_state", [2, 2, 2, 128, 8, 64])
    scr = {}
    for i in range(n_rw):
        for nm in ("r", "k", "v", "kk", "af", "ab", "g", "bv"):
            scr[(i, nm)] = dscr("scr_%s_%d" % (nm, i), [20, 128, 8, 128], BF16)
        for nm in ("sf", "sb"):
            scr[(i, nm)] = dscr("scr_%s_%d" % (nm, i), [20, 128, 8, 128], F32)
        scr[(i, "yf")] = dscr("scr_yf_%d" % i, [20, 128, 1024], BF16)

    P = Prog()
    with ExitStack() as es:
        def sbt(name, shape, dt):
            return es.enter_context(nc.sbuf_tensor(name, list(shape), dt))

        x_sb = sbt("x_sb", [128, 8, NT], F32)
        modT = sbt("modT", [128, 4, 48, 2], F32)
        gsc = sbt("gsc", [128, 4, 2, 8, 2], F32)
        adab = sbt("adab", [128, 4, 48], F32)
        ng = sbt("ng", [128, 4, 2, 8], F32)
        fg = sbt("fg", [128, 8], F32)
        ident = sbt("ident_sb", [128, 128], BF16)
        onesbd = sbt("onesbdb", [128, 128], BF16)
        onesm = sbt("onesm", [128, 128], BF16)
        onesf = sbt("onesf", [128, 128], F32)
        maska = sbt("maska_sb", [128, 2, 512], BF16)
        maskb = sbt("maskb_sb", [128, 2, 256], BF16)
        blkm = sbt("blk_sb", [128, 2, 7, 128], BF16)
        epsn = sbt("epsn", [128, 1], F32)
        epsg = sbt("epsg", [128, 1], F32)
        epsk = sbt("epsk", [128, 1], F32)
        hm = sbt("hm", [128, 2], F32)
        pv = sbt("pv", [128, 2, 11, 8], F32)
        mu = sbt("mu", [128, 2, 6, 8], F32)
        omm = sbt("omm", [128, 2, 6, 8], F32)
        hmu = sbt("hmu", [128, 2, 6, 8], F32)
        omka = sbt("omka", [128, 2, 8], F32)
        tmka = sbt("tmka", [128, 2, 8], F32)
        Hst = sbt("Hst", [128, 8, 64], F32)
        Hb = sbt("Hb", [128, 8, 64], BF16)
        condb = sbt("condb", [128, 8, 2], BF16)
        ARW = 27500
        arena_t = sbt("arena", [128, ARW], F32)
        AR = Arena(arena_t, ARW)
        _DBG['AR'] = AR
        banks = [es.enter_context(nc.psum_tensor("bank%d" % i, [128, 512], F32)) for i in range(8)]
        bankbf = [b[:].bitcast(BF16) for b in banks]

        P.op("sp", "dma_start", out=adab[:], in_=d_adab[:, :, :], w=["adab"], dma=True)
        P.op("sp", "dma_start", out=ng[:], in_=d_ng[:, :, :, :], w=["ng"], dma=True)
        P.op("sp", "dma_start", out=fg[:], in_=d_fg[:, :], w=["fg"], dma=True)
        P.op("sp", "dma_start", out=pv[:], in_=d_pv[:, :, :, :], w=["pv"], dma=True)
        P.op("sp", "dma_start", out=mu[:], in_=d_mu[:, :, :, :], w=["mu"], dma=True)
        P.op("pool", "dma_start", out=ident[:], in_=d_ident[:, :], w=["ident"], dma=True)
        P.op("pool", "dma_start", out=onesbd[:], in_=d_onesbd[:, :], w=["onesbd"], dma=True)
        for dd in range(2):
            P.op("pool", "dma_start", out=maska[:, dd, :], in_=d_maska[dd], w=["maska"], dma=True)
            P.op("pool", "dma_start", out=maskb[:, dd, :], in_=d_maskb[dd], w=["maskb"], dma=True)
            P.op("pool", "dma_start", out=blkm[:, dd, :, :], in_=d_blk[dd], w=["blk"], dma=True)
        P.op("dve", "memset", onesm[:], 1.0 / 1024.0, w=["onesm"])
        P.op("dve", "memset", onesf[:], 1.0, w=["onesf"])
        P.op("dve", "memset", epsn[:], NORM_EPS, w=["epsn"])
        P.op("dve", "memset", epsg[:], GN_EPS, w=["epsg"])
        P.op("dve", "memset", epsk[:], 1e-24, w=["epsk"])
        P.op("dve", "memset", hm[0:64, 0:1], 1.0, w=["hm00"])
        P.op("dve", "memset", hm[64:128, 0:1], 0.0, w=["hm10"])
        P.op("dve", "memset", hm[0:64, 1:2], 0.0, w=["hm01"])
        P.op("dve", "memset", hm[64:128, 1:2], 1.0, w=["hm11"])
        P.op("dve", "tensor_scalar", omm[:], mu[:], -1.0, 1.0, op0=ALU.mult, op1=ALU.add, r=["mu"], w=["omm"])
        P.op("dve", "tensor_scalar", hmu[:], mu[:], 0.5, None, op0=ALU.mult, r=["mu"], w=["hmu"])
        P.op("dve", "tensor_scalar", omka[:], pv[:, :, 6, :], -1.0, 1.0, op0=ALU.mult, op1=ALU.add, r=["pv"], w=["omka"])
        P.op("dve", "tensor_scalar", tmka[:], pv[:, :, 6, :], -2.0, 2.0, op0=ALU.mult, op1=ALU.add, r=["pv"], w=["tmka"])

        def xkeys(t0, n):
            return [("x", k) for k in range(t0 // 128, (t0 + n + 127) // 128)]

        for (t0, _) in TILES512:
            P.op("sp", "dma_start", out=x_sb[:, :, t0:t0 + 512], in_=d_x[:, :, t0:t0 + 512], w=xkeys(t0, 512), dma=True)

        AR.reset()
        condf = AR.f32(8, 2)
        adaw = [AR.bf16(8, 512) for _ in range(2)]
        P.op("sp", "dma_start", out=condf, in_=d_cond[:, :, :], w=["condf"], dma=True)
        P.op("act", "activation", out=condb[:], in_=condf, func=AF.Silu, r=["condf"], w=["condb"])
        pm = banks[0][:, 0:96]
        nblk = 0
        for l in range(depth):
            for blk in range(12):
                buf = adaw[nblk % 2]
                bkey = ("adaw", nblk % 2)
                nblk += 1
                P.op("pool", "dma_start", out=buf, in_=d_adaw[l, :, :, blk * 512:(blk + 1) * 512], w=[bkey], dma=True)
                for m in range(4):
                    n = blk * 4 + m
                    for kc in range(8):
                        P.op("pe", "matmul", pm[:, 2 * n:2 * n + 2], buf[:, kc, m * 128:(m + 1) * 128], condb[:, kc, :],
                             start=(kc == 0), stop=(kc == 7), r=[bkey, "condb"], w=["pm"])
            P.op("dve", "tensor_tensor", modT[:, l, :, :], pm.rearrange("p (n w) -> p n w", w=2), bcast_free(adab[:, l, :], 2), op=ALU.add,
                 r=["pm", "adab"], w=["modT"])
            for j in range(2):
                P.op("dve", "tensor_scalar", gsc[:, l, j, :, :], modT[:, l, (3 * j + 1) * 8:(3 * j + 2) * 8, :], 1.0, None, op0=ALU.add,
                     r=["modT"], w=["gsc"])
                P.op("dve", "tensor_tensor", gsc[:, l, j, :, :], gsc[:, l, j, :, :], bcast_free(ng[:, l, j, :], 2), op=ALU.mult,
                     r=["gsc", "ng"], w=["gsc"])

        def emit_norm(t0, n, scale_ap, bias_ap, out_ap, tmp, ps_bank, out_keys):
            sq, sd, rstd, tmpn = tmp
            ps = ps_bank[:, 0:n]
            for c in range(8):
                s = sq[c % 2]
                P.op("act", "activation", out=s[:, 0:n], in_=x_sb[:, c, t0:t0 + n], func=AF.Square, r=xkeys(t0, n), w=[("nsq", c % 2)])
                P.op("pe", "matmul", ps, onesm[:], s[:, 0:n], start=(c == 0), stop=(c == 7), r=[("nsq", c % 2), "onesm"], w=["nps"])
            P.op("act", "activation", out=sd[:, 0:n], in_=ps, func=AF.Sqrt, bias=epsn[:, 0:1], scale=1.0, r=["nps", "epsn"], w=["nsd"])
            P.op("dve", "reciprocal", rstd[:, 0:n], sd[:, 0:n], r=["nsd"], w=["nrstd"])
            for c in range(8):
                tn = tmpn[c % 2]
                P.op("dve", "tensor_tensor", tn[:, 0:n], x_sb[:, c, t0:t0 + n], rstd[:, 0:n], op=ALU.mult,
                     r=xkeys(t0, n) + ["nrstd"], w=[("ntmp", c % 2)])
                if bias_ap is not None:
                    P.op("act", "activation", out=out_ap(c), in_=tn[:, 0:n], func=AF.Identity, scale=scale_ap(c), bias=bias_ap(c),
                         r=[("ntmp", c % 2), "gsc", "modT", "fg"], w=out_keys(c))
                else:
                    P.op("act", "activation", out=out_ap(c), in_=tn[:, 0:n], func=AF.Identity, scale=scale_ap(c),
                         r=[("ntmp", c % 2), "gsc", "modT", "fg"], w=out_keys(c))

        def norm_tmp(n):
            return ([AR.bf16(n) for _ in range(2)], AR.f32(n), AR.f32(n), [AR.f32(n) for _ in range(2)])

        def sgu_layer(l):
            i = l // 2
            P.barrier()
            AR.reset()
            w_in = AR.bf16(8, 2048)
            w_out = AR.bf16(8, 1024)
            wsT = AR.bf16(16, 128)
            lnG = AR.f32(1024)
            lnB = AR.f32(1024)
            bsrow = AR.bf16(2048)
            onesrow = AR.bf16(64)
            ntmp = norm_tmp(512)
            h1 = AR.bf16(8, 512)
            u = AR.bf16(8, 512)
            vt = AR.f32(1024)
            vsq = AR.f32(1024)
            vn = AR.bf16(4, 1024)
            st = AR.f32(8)
            for q in range(4):
                P.op("pool", "dma_start", out=w_in[:, :, q * 512:(q + 1) * 512], in_=d_win[i, :, :, q * 512:(q + 1) * 512], w=[("w_in", q)], dma=True)
            P.op("pool", "dma_start", out=wsT, in_=d_wsT[i], w=["wsT"], dma=True)
            P.op("pool", "dma_start", out=bsrow[0:1, :], in_=d_bs[i], w=["bsrow"], dma=True)
            for q in range(2):
                P.op("pool", "dma_start", out=w_out[:, :, q * 512:(q + 1) * 512], in_=d_wout[i, :, :, q * 512:(q + 1) * 512], w=[("w_out", q)], dma=True)
            P.op("sp", "dma_start", out=lnG, in_=bass.AP(d_lng.tensor, d_lng[i, 0].offset, [[0, 128], [1, 1024]]), w=["lnG"], dma=True)
            P.op("sp", "dma_start", out=lnB, in_=bass.AP(d_lng.tensor, d_lng[i, 1].offset, [[0, 128], [1, 1024]]), w=["lnB"], dma=True)
            P.op("dve", "memset", onesrow[0:1, :], 1.0, w=["onesrow"])
            for (t0, w) in TILES512:
                emit_norm(t0, 512, lambda c: gsc[:, l, 0, c, w:w + 1], lambda c: modT[:, l, 0 + c, w:w + 1],
                          lambda c: h1[:, c, :], ntmp, banks[0], lambda c: ["h1"])
                for oc in range(8):
                    ps = banks[1 + oc % 2]
                    pk = ("bk", 1 + oc % 2)
                    for kc in range(8):
                        P.op("pe", "matmul", ps[:], w_in[:, kc, oc * 128:(oc + 1) * 128], h1[:, kc, :], start=(kc == 0), stop=(kc == 7),
                             r=[("w_in", oc // 4), "h1"], w=[pk])
                    P.op("act", "activation", out=u[:, oc, :], in_=ps[:], func=AF.Gelu_apprx_tanh, r=[pk], w=[("u", oc)])
                for q in range(4):
                    for nb in range(2):
                        ps = banks[3 + nb]
                        pk = ("bk", 3 + nb)
                        for kc in range(8):
                            P.op("pe", "matmul", ps[:], h1[:, kc, q * 128:(q + 1) * 128], w_in[:, kc, 1024 + nb * 512:1024 + (nb + 1) * 512],
                                 start=(kc == 0), stop=(kc == 7), r=[("w_in", 2 + nb), "h1"], w=[pk])
                        P.op("act", "activation", out=vt[:, nb * 512:(nb + 1) * 512], in_=ps[:], func=AF.Gelu_apprx_tanh, r=[pk], w=["vt"])
                    P.op("dve", "tensor_reduce", out=st[:, 0:1], in_=vt, axis=AX.X, op=ALU.add, r=["vt"], w=["st0"])
                    P.op("act", "activation", out=vsq, in_=vt, func=AF.Square, r=["vt"], w=["vsq"])
                    P.op("dve", "tensor_reduce", out=st[:, 1:2], in_=vsq, axis=AX.X, op=ALU.add, r=["vsq"], w=["st1"])
                    P.op("dve", "tensor_scalar", st[:, 2:3], st[:, 0:1], 1.0 / 1024.0, None, op0=ALU.mult, r=["st0"], w=["st2"])
                    P.op("dve", "tensor_tensor", st[:, 3:4], st[:, 2:3], st[:, 2:3], op=ALU.mult, r=["st2"], w=["st3"])
                    P.op("dve", "scalar_tensor_tensor", st[:, 4:5], st[:, 1:2], 1.0 / 1024.0, st[:, 3:4], op0=ALU.mult, op1=ALU.subtract,
                         r=["st1", "st3"], w=["st4"])
                    P.op("act", "activation", out=st[:, 5:6], in_=st[:, 4:5], func=AF.Sqrt, bias=epsn[:, 0:1], scale=1.0, r=["st4", "epsn"], w=["st5"])
                    P.op("dve", "reciprocal", st[:, 6:7], st[:, 5:6], r=["st5"], w=["st6"])
                    P.op("dve", "tensor_scalar", vt, vt, st[:, 2:3], st[:, 6:7], op0=ALU.subtract, op1=ALU.mult, r=["vt", "st2", "st6"], w=["vt"])
                    P.op("dve", "tensor_tensor", vt, vt, lnG, op=ALU.mult, r=["vt", "lnG"], w=["vt"])
                    P.op("dve", "tensor_tensor", vn[:, q, :], vt, lnB, op=ALU.add, r=["vt", "lnB"], w=[("vn", q)])
                for c in range(8):
                    ps = banks[5 + c % 2]
                    pk = ("bk", 5 + c % 2)
                    for q in range(4):
                        for he in range(2):
                            g = 2 * c + he
                            P.op("pe", "matmul", ps[he * 64:(he + 1) * 64, q * 128:(q + 1) * 128], vn[:, q, g * 64:(g + 1) * 64], wsT[:, g, :],
                                 start=True, stop=False, tile_position=(0, he * 64), r=[("vn", q), "wsT"], w=[pk])
                            P.op("pe", "matmul", ps[he * 64:(he + 1) * 64, q * 128:(q + 1) * 128], onesrow[0:1, :], bsrow[0:1, g * 128:(g + 1) * 128],
                                 start=False, stop=True, tile_position=(0, he * 64), r=["onesrow", "bsrow"], w=[pk])
                    P.op("dve", "tensor_tensor", u[:, c, :], ps[:], u[:, c, :], op=ALU.mult, r=[pk, ("u", c)], w=[("u", c)])
                for oc in range(8):
                    ps = banks[1 + oc % 2]
                    pk = ("bk", 1 + oc % 2)
                    for kc in range(8):
                        P.op("pe", "matmul", ps[:], w_out[:, kc, oc * 128:(oc + 1) * 128], u[:, kc, :], start=(kc == 0), stop=(kc == 7),
                             r=[("w_out", oc // 4), ("u", kc)], w=[pk])
                    P.op("dve", "scalar_tensor_tensor", x_sb[:, oc, t0:t0 + 512], ps[:], modT[:, l, 16 + oc, w:w + 1], x_sb[:, oc, t0:t0 + 512],
                         op0=ALU.mult, op1=ALU.add, r=[pk, "modT"] + xkeys(t0, 512), w=xkeys(t0, 512))

        def mlp_layer(l):
            P.barrier()
            AR.reset()
            h2 = AR.bf16(8, NT)
            w1b = [AR.bf16(8, 512) for _ in range(2)]
            w2b = [AR.bf16(4, 1024) for _ in range(2)]
            hid = [AR.bf16(4, 512) for _ in range(2)]
            rl = [AR.bf16(512) for _ in range(2)]
            ntmp = norm_tmp(512)

            def load(jb):
                P.op("pool", "dma_start", out=w1b[jb % 2], in_=d_w1[l, jb], w=[("w1b", jb % 2)], dma=True)
                P.op("pool", "dma_start", out=w2b[jb % 2], in_=d_w2[l, jb], w=[("w2b", jb % 2)], dma=True)
            load(0)
            for (t0, w) in TILES512:
                emit_norm(t0, 512, lambda c: gsc[:, l, 1, c, w:w + 1], lambda c: modT[:, l, 24 + c, w:w + 1],
                          lambda c: h2[:, c, t0:t0 + 512], ntmp, banks[0], lambda c: [("h2", t0)])
            nph = 0
            npo = 0
            nh = 0
            for jb in range(8):
                if jb + 1 < 8:
                    load(jb + 1)
                W1 = w1b[jb % 2]
                W2 = w2b[jb % 2]
                for (t0, w) in TILES512:
                    hd = hid[nh % 2]
                    hk = ("hid", nh % 2)
                    nh += 1
                    for hc in range(4):
                        ps = banks[1 + nph % 4]
                        pk = ("bk", 1 + nph % 4)
                        rr = rl[nph % 2]
                        rk = ("rl", nph % 2)
                        nph += 1
                        for kc in range(8):
                            P.op("pe", "matmul", ps[:], W1[:, kc, hc * 128:(hc + 1) * 128], h2[:, kc, t0:t0 + 512], start=(kc == 0), stop=(kc == 7),
                                 r=[("w1b", jb % 2), ("h2", t0)], w=[pk])
                        P.op("act", "activation", out=rr, in_=ps[:], func=AF.Relu, r=[pk], w=[rk])
                        P.op("dve", "tensor_tensor", hd[:, hc, :], rr, rr, op=ALU.mult, r=[rk], w=[hk])
                    for oc in range(8):
                        ps = banks[5 + npo % 3]
                        pk = ("bk", 5 + npo % 3)
                        npo += 1
                        for hc in range(4):
                            P.op("pe", "matmul", ps[:], W2[:, hc, oc * 128:(oc + 1) * 128], hd[:, hc, :], start=(hc == 0), stop=(hc == 3),
                                 r=[("w2b", jb % 2), hk], w=[pk])
                        P.op("dve", "scalar_tensor_tensor", x_sb[:, oc, t0:t0 + 512], ps[:], modT[:, l, 40 + oc, w:w + 1], x_sb[:, oc, t0:t0 + 512],
                             op0=ALU.mult, op1=ALU.add, r=[pk, "modT"] + xkeys(t0, 512), w=xkeys(t0, 512))

        def rwkv_layer(l):
            i = l // 2
            pvi = lambda idx, c: pv[:, i, idx, c:c + 1]
            P.barrier()
            AR.reset()
            wring = [AR.bf16(8, 512) for _ in range(2)]
            l1 = AR.bf16(4, 8, 64)
            l2 = AR.bf16(4, 1024)
            g1w = AR.bf16(8, 128)
            g2w = AR.bf16(1024)
            v1w = AR.bf16(8, 32)
            v2w = AR.bf16(1024)
            ntmp = norm_tmp(258)
            hh = AR.bf16(8, 258)
            ss = AR.bf16(8, 256)
            t1 = [AR.bf16(256) for _ in range(2)]
            xj = AR.bf16(8, 256)
            p1b = [AR.bf16(256) for _ in range(2)]
            sig = AR.f32(8, 256)
            o_r = AR.bf16(8, 256)
            o_k = AR.bf16(8, 256)
            o_v = AR.bf16(8, 256)
            o_kk = AR.bf16(8, 256)
            o_af = AR.bf16(8, 256)
            o_ab = AR.bf16(8, 256)
            o_g = AR.bf16(8, 256)
            o_bv = AR.bf16(8, 256)
            tq = AR.bf16(8, 256)
            vf = AR.bf16(8, 256) if i == 1 else None
            sgv = o_bv
            tmpa = AR.f32(256)
            tmpb = AR.f32(256)
            for q in range(4):
                P.op("pool", "dma_start", out=l1[:, q, :, :], in_=d_l1[i, q], w=["l1"], dma=True)
                P.op("pool", "dma_start", out=l2[0:64, q, :], in_=d_l2[i, q], w=["l2"], dma=True)
            P.op("pool", "dma_start", out=g1w, in_=d_g1[i], w=["g1w"], dma=True)
            P.op("pool", "dma_start", out=g2w, in_=d_g2[i], w=["g2w"], dma=True)
            if i == 1:
                P.op("pool", "dma_start", out=v1w, in_=d_v1[:, :, :], w=["v1w"], dma=True)
                P.op("pool", "dma_start", out=v2w[0:32, :], in_=d_v2[:, :], w=["v2w"], dma=True)

            def wload(wi):
                for hb in range(2):
                    P.op("pool", "dma_start", out=wring[hb], in_=d_rw[i, wi, :, :, hb * 512:(hb + 1) * 512], w=[("wring", hb)], dma=True)

            npb = [0]

            def proj(wi, evac):
                for o2 in range(4):
                    bk = 1 + npb[0] % 3
                    npb[0] += 1
                    pk = ("bk", bk)
                    for sub in range(2):
                        oc = 2 * o2 + sub
                        for kc in range(8):
                            P.op("pe", "matmul", banks[bk][:, sub * 256:(sub + 1) * 256], wring[oc // 4][:, kc, (oc % 4) * 128:(oc % 4 + 1) * 128], xj[:, kc, :],
                                 start=(kc == 0), stop=(kc == 7), r=[("wring", oc // 4), "xj"], w=[pk])
                    evac(banks[bk][:].rearrange("p (a b) -> p a b", a=2), 2 * o2, pk)

            def mix(j):
                for c in range(8):
                    tt = t1[c % 2]
                    P.op("act", "activation", out=tt, in_=hh[:, c, 1:257], func=AF.Identity, scale=omm[:, i, j, c:c + 1], r=["hh", "omm"], w=[("t1", c % 2)])
                    P.op("dve", "scalar_tensor_tensor", xj[:, c, :], ss[:, c, :], hmu[:, i, j, c:c + 1], tt, op0=ALU.mult, op1=ALU.add,
                         r=["ss", "hmu", ("t1", c % 2)], w=["xj"])

            def lora(wq, K, func1, evac2, l1w=None, l2w=None):
                A1 = l1w if l1w is not None else l1[:, wq, :, :]
                pb = p1b[wq % 2]
                pkey = ("p1b", wq % 2)
                for kc in range(8):
                    P.op("pe", "matmul", banks[4][0:K, 0:256], A1[:, kc, 0:K], xj[:, kc, :], start=(kc == 0), stop=(kc == 7), r=["l1", "g1w", "v1w", "xj"], w=[("bk", 4)])
                P.op("act", "activation", out=pb[0:K, :], in_=banks[4][0:K, 0:256], func=func1, r=[("bk", 4)], w=[pkey])
                A2 = l2w if l2w is not None else l2[:, wq, :]
                for o2 in range(4):
                    bk = 5 + o2 % 2
                    pk = ("bk", bk)
                    for sub in range(2):
                        oc = 2 * o2 + sub
                        P.op("pe", "matmul", banks[bk][:, sub * 256:(sub + 1) * 256], A2[0:K, oc * 128:(oc + 1) * 128], pb[0:K, :], start=True, stop=True,
                             r=["l2", "g2w", "v2w", pkey], w=[pk])
                    evac2(banks[bk][:].rearrange("p (a b) -> p a b", a=2), 2 * o2, pk)

            def spill(buf, key, name, ch0):
                for q in range(2):
                    P.op("sp", "dma_start", out=scr[(i, name)][ch0 + q], in_=buf[:, :, q * 128:(q + 1) * 128], r=[key], w=[("scr", name, ch0 + q)], dma=True)

            wload(2)
            for (t0, w, lv, rv) in TILES256:
                ch0 = t0 // 128
                a = t0 - 1 if lv else t0
                b = t0 + 257 if rv else t0 + 256
                off = a - (t0 - 1)
                n = b - a
                emit_norm(a, n, lambda c: gsc[:, l, 0, c, w:w + 1], lambda c: modT[:, l, 0 + c, w:w + 1],
                          lambda c: hh[:, c, off:off + n], ntmp, banks[0], lambda c: ["hh"])
                if not lv:
                    P.op("dve", "memset", hh[:, :, 0:1], 0.0, w=["hh"])
                if not rv:
                    P.op("dve", "memset", hh[:, :, 257:258], 0.0, w=["hh"])
                P.op("dve", "tensor_tensor", ss, hh[:, :, 0:256], hh[:, :, 2:258], op=ALU.add, r=["hh"], w=["ss"])
                mix(1)
                for d in range(2):
                    def ev_w(bv_, oc0, pk, d=d):
                        for sub in range(2):
                            oc = oc0 + sub
                            P.op("act", "activation", out=sig[:, oc, :], in_=bv_[:, sub, :], func=AF.Sigmoid, bias=pvi(0 + d, oc), scale=1.0, r=[pk, "pv"], w=["sig"])
                    lora(d, 64, AF.Tanh, ev_w)
                    spill(sig, "sig", "sf" if d == 0 else "sb", ch0)
                mix(4)
                for d in range(2):
                    oa = o_af if d == 0 else o_ab
                    def ev_a(bv_, oc0, pk, d=d, oa=oa):
                        for sub in range(2):
                            oc = oc0 + sub
                            P.op("act", "activation", out=oa[:, oc, :], in_=bv_[:, sub, :], func=AF.Sigmoid, bias=pvi(2 + d, oc), scale=1.0, r=[pk, "pv"], w=[("oa", d)])
                    lora(2 + d, 64, AF.Identity, ev_a)
                    spill(oa, ("oa", d), "af" if d == 0 else "ab", ch0)
                P.op("dve", "tensor_tensor", tq, o_af, o_ab, op=ALU.add, r=[("oa", 0), ("oa", 1)], w=["tq"])
                for c in range(8):
                    P.op("dve", "tensor_scalar", tq[:, c, :], tq[:, c, :], pvi(6, c), tmka[:, i, c:c + 1], op0=ALU.mult, op1=ALU.add, r=["tq", "pv", "tmka"], w=["tq"])
                mix(5)
                def ev_g(bv_, oc0, pk):
                    P.op("act", "copy", o_g[:, oc0:oc0 + 2, :], bv_, r=[pk], w=["o_g"])
                lora(0, 128, AF.Sigmoid, ev_g, l1w=g1w, l2w=g2w)
                spill(o_g, "o_g", "g", ch0)
                mix(3)
                def ev_v(bv_, oc0, pk):
                    P.op("act", "copy", o_v[:, oc0:oc0 + 2, :], bv_, r=[pk], w=["o_v"])
                proj(2, ev_v)
                wload(0)
                if i == 1:
                    for q in range(2):
                        P.op("sp", "dma_start", out=vf[:, :, q * 128:(q + 1) * 128], in_=scr[(0, "v")][ch0 + q], w=["vf"], dma=True)
                    def ev_sv(bv_, oc0, pk):
                        for sub in range(2):
                            oc = oc0 + sub
                            P.op("act", "activation", out=sgv[:, oc, :], in_=bv_[:, sub, :], func=AF.Sigmoid, bias=pvi(4, oc), scale=1.0, r=[pk, "pv"], w=["o_bv"])
                    lora(1, 32, AF.Identity, ev_sv, l1w=v1w, l2w=v2w)
                    P.op("dve", "tensor_tensor", vf, vf, o_v, op=ALU.subtract, r=["vf", "o_v"], w=["vf"])
                    P.op("dve", "tensor_tensor", vf, vf, sgv, op=ALU.mult, r=["vf", "o_bv"], w=["vf"])
                    P.op("dve", "tensor_tensor", o_v, o_v, vf, op=ALU.add, r=["vf", "o_v"], w=["o_v"])
                spill(o_v, "o_v", "v", ch0)
                mix(0)
                def ev_r(bv_, oc0, pk):
                    P.op("act", "copy", o_r[:, oc0:oc0 + 2, :], bv_, r=[pk], w=["o_r"])
                proj(0, ev_r)
                wload(1)
                spill(o_r, "o_r", "r", ch0)
                mix(2)
                def ev_k(bv_, oc0, pk):
                    P.op("act", "copy", o_k[:, oc0:oc0 + 2, :], bv_, r=[pk], w=["o_k"])
                proj(1, ev_k)
                wload(2)
                spill(o_k, "o_k", "k", ch0)
                for c in range(8):
                    P.op("dve", "tensor_scalar", o_kk[:, c, :], o_k[:, c, :], pvi(5, c), None, op0=ALU.mult, r=["o_k", "pv"], w=["o_kk"])
                    sqb = t1[c % 2]
                    P.op("act", "activation", out=sqb, in_=o_kk[:, c, :], func=AF.Square, r=["o_kk"], w=[("t1", c % 2)])
                    P.op("pe", "matmul", banks[7][:, 0:256], onesbd[:], sqb, start=True, stop=True, r=[("t1", c % 2), "onesbd"], w=[("bk", 7)])
                    P.op("act", "activation", out=tmpa, in_=banks[7][:, 0:256], func=AF.Sqrt, bias=epsk[:, 0:1], scale=1.0, r=[("bk", 7), "epsk"], w=["tmpa"])
                    P.op("dve", "reciprocal", tmpb, tmpa, r=["tmpa"], w=["tmpb"])
                    P.op("dve", "tensor_tensor", o_kk[:, c, :], o_kk[:, c, :], tmpb, op=ALU.mult, r=["o_kk", "tmpb"], w=["o_kk"])
                spill(o_kk, "o_kk", "kk", ch0)
                for c in range(8):
                    bq = t1[c % 2]
                    P.op("dve", "scalar_tensor_tensor", tmpa, o_r[:, c, :], pvi(7, c), o_k[:, c, :], op0=ALU.mult, op1=ALU.mult, r=["o_r", "o_k", "pv"], w=["tmpa"])
                    P.op("dve", "tensor_tensor", bq, tmpa, tq[:, c, :], op=ALU.mult, r=["tmpa", "tq"], w=[("t1", c % 2)])
                    P.op("pe", "matmul", banks[7][:, 256:512], onesbd[:], bq, start=True, stop=True, r=[("t1", c % 2), "onesbd"], w=[("bk7b",)])
                    P.op("dve", "tensor_tensor", o_bv[:, c, :], banks[7][:, 256:512], o_v[:, c, :], op=ALU.mult, r=[("bk7b",), "o_v"], w=["o_bv"])
                spill(o_bv, "o_bv", "bv", ch0)

            _rwstop = os.environ.get("KRW", "")
            for d in range(2):
                if _rwstop == "proj" or (_rwstop == "fwd" and d == 1):
                    break
                P.barrier()
                AR.reset()
                ld = {}
                for nm in ("r", "k", "v", "kk", "a"):
                    ld[nm] = [AR.bf16(8, 128)] * 2
                ld["s"] = [AR.f32(8, 128)] * 2
                pinc = AR.f32(8, 128, name="pinc")
                eLi_f = AR.f32(1024, name="eLi")
                emL_f = AR.f32(1024, name="emL")
                eLe_f = AR.f32(1024, name="eLe")
                eLi = eLi_f.rearrange("p (a b) -> p a b", a=8)
                emL = emL_f.rearrange("p (a b) -> p a b", a=8)
                eLe = eLe_f.rearrange("p (a b) -> p a b", a=8)
                PC = AR.f32(8, name="PC")
                bPN = AR.f32(2, 8)
                ar = AR.bf16(8, 256, name="ar")
                bt = AR.bf16(8, 128, name="bt")
                kt = AR.bf16(8, 128, name="kt")
                ka = AR.bf16(8, 128)
                tqs = AR.bf16(8, 128)
                Vtok = AR.bf16(1024, name="Vtok")
                Ktok = AR.bf16(1024, name="Ktok")
                Btok = AR.bf16(1024, name="Btok")
                msk = [AR.bf16(2, 512) for _ in range(4)]
                DDb = [[AR.bf16(2, 2, 128) for _ in range(2)] for _ in range(4)]
                Eb = [AR.bf16(2, 128) for _ in range(4)]
                Dt7 = [AR.bf16(2, 128) for _ in range(4)]
                Xb = [AR.bf16(128) for _ in range(4)]
                Ub = [AR.bf16(128) for _ in range(4)]
                pad = [AR.bf16(4, 2, 128) for _ in range(4)]
                ident2 = bass.AP(ident[:].tensor, ident[:].offset, [list(ident[:].ap[0]), [0, 2], [1, 128]])

                def blk_ap(dd, lev):
                    v_ = blkm[:, dd, lev, :]
                    return bass.AP(v_.tensor, v_.offset, [list(v_.ap[0]), [0, 2], [1, 128]])
                if d == 0:
                    yfb = [AR.bf16(1024) for _ in range(2)]
                else:
                    yfl = [AR.bf16(1024)] * 2
                    ysum = eLe_f
                    ysq = eLi_f
                    ynb = tqs.rearrange("p a b -> p (a b)")
                    yo1 = emL
                    yob = ka
                    ldg = [AR.bf16(8, 128)] * 2
                    ldbv = [AR.bf16(8, 128)] * 2
                    Wo = AR.bf16(8, 1024)
                    st16 = AR.f32(6, 16)
                    for hb in range(2):
                        P.op("pool", "dma_start", out=Wo[:, :, hb * 512:(hb + 1) * 512], in_=d_rw[i, 3, :, :, hb * 512:(hb + 1) * 512], w=["Wo"], dma=True)
                aname = "af" if d == 0 else "ab"
                sname = "sf" if d == 0 else "sb"
                order = []
                for (c0, ncx, kind, idx) in SEQS:
                    chs = list(range(c0, c0 + ncx))
                    if d == 1:
                        chs = chs[::-1]
                    for k_, ch in enumerate(chs):
                        order.append((ch, kind, idx, k_ == 0, k_ == ncx - 1))

                def loads(n_):
                    ch = order[n_][0]
                    sl = n_ % 2
                    for nm, sn in (("r", "r"), ("k", "k"), ("v", "v"), ("kk", "kk"), ("a", aname), ("s", sname)):
                        P.op("sp", "dma_start", out=ld[nm][sl], in_=scr[(i, sn)][ch], w=[("ld", nm, 0)], dma=True)

                loads(0)
                for n_, (ch, kind, idx, first, last) in enumerate(order):
                    sl = n_ % 2
                    if d == 1:
                        P.op("sp", "dma_start", out=yfl[0], in_=scr[(i, "yf")][ch], w=[("yfl", 0)], dma=True)
                        P.op("sp", "dma_start", out=ldg[0], in_=scr[(i, "g")][ch], w=[("ldg", 0)], dma=True)
                        P.op("sp", "dma_start", out=ldbv[0], in_=scr[(i, "bv")][ch], w=[("ldbv", 0)], dma=True)
                    L = {nm: ld[nm][sl] for nm in ld}
                    lk = lambda nm: ("ld", nm, 0)
                    if first:
                        if kind == "s":
                            P.op("sp", "dma_start", out=Hst[:], in_=d_state[:, i, d], w=[("Hst", c_) for c_ in range(8)], dma=True)
                        else:
                            P.op("dve", "memset", Hst[:], 0.0, w=[("Hst", c_) for c_ in range(8)])
                        P.op("act", "copy", Hb[:], Hst[:], r=[("Hst", c_) for c_ in range(8)], w=[("Hb", c_) for c_ in range(8)])
                    _ks = os.environ.get("KSCAN", "")
                    if _ks == "load":
                        continue
                    for c in range(8):
                        P.op("dve", "tensor_tensor_scan", out=pinc[:, c, :], data0=onesf[:], data1=L["s"][:, c, :], initial=0.0, op0=ALU.mult, op1=ALU.add,
                             r=[lk("s"), "onesf"], w=["pinc"])
                    pexc = L["s"]
                    P.op("dve", "tensor_tensor", pexc, pinc, L["s"], op=ALU.subtract, r=["pinc", lk("s")], w=[lk("s")])
                    P.op("act", "activation", out=PC, in_=pinc[:, :, 127], func=AF.Exp, scale=NEGC, r=["pinc"], w=["PC"])
                    if d == 0:
                        P.op("act", "activation", out=eLi, in_=pinc, func=AF.Exp, scale=NEGC, r=["pinc"], w=["eLi"])
                        P.op("act", "activation", out=emL, in_=pinc, func=AF.Exp, scale=-NEGC, r=["pinc"], w=["emL"])
                        P.op("act", "activation", out=eLe, in_=pexc, func=AF.Exp, scale=NEGC, r=[lk("s")], w=["eLe"])
                    else:
                        P.op("dve", "tensor_scalar", bPN[:, 0, :], pinc[:, :, 127], NEGC, None, op0=ALU.mult, r=["pinc"], w=["bPN"])
                        P.op("dve", "tensor_scalar", bPN[:, 1, :], pinc[:, :, 127], -NEGC, None, op0=ALU.mult, r=["pinc"], w=["bPN"])
                        for c in range(8):
                            P.op("act", "activation", out=eLi[:, c, :], in_=pexc[:, c, :], func=AF.Exp, scale=-NEGC, bias=bPN[:, 0, c:c + 1], r=[lk("s"), "bPN"], w=["eLi"])
                            P.op("act", "activation", out=emL[:, c, :], in_=pexc[:, c, :], func=AF.Exp, scale=NEGC, bias=bPN[:, 1, c:c + 1], r=[lk("s"), "bPN"], w=["emL"])
                            P.op("act", "activation", out=eLe[:, c, :], in_=pinc[:, c, :], func=AF.Exp, scale=-NEGC, bias=bPN[:, 0, c:c + 1], r=["pinc", "bPN"], w=["eLe"])
                    P.op("dve", "tensor_tensor", ar[:, :, 128:256], L["r"], eLi, op=ALU.mult, r=[lk("r"), "eLi"], w=["ar"])
                    P.op("dve", "scalar_tensor_tensor", ar[:, :, 0:128], L["kk"], -1.0, eLe, op0=ALU.mult, op1=ALU.mult, r=[lk("kk"), "eLe"], w=["ar"])
                    P.op("dve", "tensor_tensor", ka, L["kk"], L["a"], op=ALU.mult, r=[lk("kk"), lk("a")], w=["ka"])
                    P.op("dve", "tensor_tensor", bt, ka, emL, op=ALU.mult, r=["ka", "emL"], w=["bt"])
                    P.op("dve", "tensor_tensor", tqs, L["a"], bcast_free(pv[:, i, 6, :], 128), op=ALU.mult, r=[lk("a"), "pv"], w=["tqs"])
                    P.op("dve", "tensor_tensor", tqs, tqs, bcast_free(omka[:, i, :], 128), op=ALU.add, r=["tqs", "omka"], w=["tqs"])
                    P.op("dve", "tensor_tensor", ka, L["k"], tqs, op=ALU.mult, r=[lk("k"), "tqs", "bt"], w=["ka"])
                    P.op("dve", "tensor_tensor", kt, ka, emL, op=ALU.mult, r=["ka", "emL"], w=["kt"])
                    if _ks == "prep":
                        continue
                    for c in range(8):
                        P.op("pe", "transpose", bankbf[6][:, c * 128:(c + 1) * 128], L["v"][:, c, :], ident[:], r=[lk("v"), "ident"], w=[("bk", 6)])
                    P.op("act", "copy", Vtok, bankbf[6], r=[("bk", 6)], w=["Vtok"])
                    for c in range(8):
                        P.op("pe", "transpose", bankbf[7][:, c * 128:(c + 1) * 128], kt[:, c, :], ident[:], r=["kt", "ident"], w=[("bk", 7)])
                    P.op("dve", "tensor_copy", Ktok, bankbf[7], r=[("bk", 7)], w=["Ktok"])
                    for c in range(8):
                        P.op("pe", "transpose", bankbf[6][:, c * 128:(c + 1) * 128], bt[:, c, :], ident[:], r=["bt", "ident"], w=[("bk", 6)])
                    P.op("act", "copy", Btok, bankbf[6], r=[("bk", 6)], w=["Btok"])

                    if n_ + 1 < len(order):
                        loads(n_ + 1)
                    def unit(c, s):
                        A, B = banks[2 * s], banks[2 * s + 1]
                        kA, kB = ("bk", 2 * s), ("bk", 2 * s + 1)
                        kmsk, kUb, kEb, kXb, kDt7, kpad = ("msk", s), ("Ub", s), ("Eb", s), ("Xb", s), ("Dt7", s), ("pad", s)
                        pd = pad[s]
                        hmk = ["hm00", "hm10", "hm01", "hm11"]
                        for he in range(2):
                            P.op("act", "activation", out=pd[:, 0, he, :], in_=bt[:, c, :], func=AF.Identity, scale=hm[:, he:he + 1], r=["bt"] + hmk, w=[kpad])
                            P.op("act", "activation", out=pd[:, 1, he, :], in_=kt[:, c, :], func=AF.Identity, scale=hm[:, he:he + 1], r=["kt"] + hmk, w=[kpad])
                            P.op("act", "activation", out=pd[:, 2, he, :], in_=ar[:, c, 0:128], func=AF.Identity, scale=hm[:, he:he + 1], r=["ar"] + hmk, w=[kpad])
                            P.op("act", "activation", out=pd[:, 3, he, :], in_=ar[:, c, 128:256], func=AF.Identity, scale=hm[:, he:he + 1], r=["ar"] + hmk, w=[kpad])
                        for he, bank, kb in ((0, A, kA), (1, B, kB)):
                            P.op("pe", "matmul", bank[:, 0:256], pd[:, 0, he, :], ar[:, c, :], start=True, stop=True, r=[kpad, "ar"], w=[kb])
                            P.op("pe", "matmul", bank[:, 256:512], pd[:, 1, he, :], ar[:, c, :], start=True, stop=True, r=[kpad, "ar"], w=[kb])
                        P.op("dve", "tensor_tensor", msk[s][:, 0, :], A[:], maska[:, d, :], op=ALU.mult, r=[kA, "maska"], w=[kmsk])
                        P.op("dve", "tensor_tensor", msk[s][:, 1, :], B[:], maska[:, d, :], op=ALU.mult, r=[kB, "maska"], w=[kmsk])
                        yield
                        for he in range(2):
                            b0 = he * 64
                            P.op("pe", "matmul", A[:, 256 + he * 64:256 + (he + 1) * 64], pd[:, 2, he, :], Hb[:, c, :], start=(he == 0), stop=False,
                                 skip_group_check=True, r=[kpad, ("Hb", c), kmsk], w=[kA])
                            P.op("pe", "matmul", A[:, 256 + he * 64:256 + (he + 1) * 64], msk[s][:, he, 256:384], Vtok[:, c * 128 + b0:c * 128 + b0 + 64], start=False, stop=(he == 1),
                                 skip_group_check=True, r=[kmsk, "Vtok"], w=[kA])
                        P.op("act", "copy", Xb[s], A[:, 256:384], r=[kA], w=[kXb])
                        yield
                        curD = [ident[:], ident[:]]
                        curT = [ident[:], ident[:]]
                        curv = bass.AP(ident[:].tensor, ident[:].offset, [list(ident[:].ap[0]), [0, 2], [0, 2], [1, 128]])
                        kcur = "ident"
                        for lev in range(7):
                            for he in range(2):
                                P.op("pe", "matmul", A[:, he * 128:(he + 1) * 128], msk[s][:, he, 0:128], curD[he], start=True, stop=True, r=[kmsk, kcur], w=[kA])
                            P.op("dve", "tensor_tensor", Eb[s], A[:, 0:256].rearrange("p (a b) -> p a b", a=2), blk_ap(d, lev), op=ALU.mult, r=[kA, "blk"], w=[kEb])
                            if lev < 6:
                                nxt = DDb[s][lev % 2]
                                knxt = ("DD", s, lev % 2)
                                for he in range(2):
                                    P.op("pe", "matmul", B[:, he * 256:he * 256 + 128], curT[he], Eb[s][:, he, :], start=True, stop=True, r=[kcur, kEb], w=[kB])
                                    P.op("pe", "matmul", B[:, he * 256 + 128:he * 256 + 256], Eb[s][:, he, :], curT[he], start=True, stop=True, r=[kcur, kEb], w=[kB])
                                P.op("dve", "tensor_tensor", nxt, B[:].rearrange("p (a b c) -> p a b c", a=2, b=2), curv, op=ALU.add, r=[kB, kcur], w=[knxt])
                                curD = [nxt[:, he, 0, :] for he in range(2)]
                                curT = [nxt[:, he, 1, :] for he in range(2)]
                                curv = nxt
                                kcur = knxt
                            else:
                                for he in range(2):
                                    P.op("pe", "matmul", B[:, he * 128:(he + 1) * 128], Eb[s][:, he, :], curT[he], start=True, stop=True, r=[kcur, kEb], w=[kB])
                                P.op("dve", "tensor_tensor", Dt7[s], B[:, 0:256].rearrange("p (a b) -> p a b", a=2), curv[:, :, 1, :], op=ALU.add, r=[kB, kcur], w=[kDt7])
                            yield
                        for he in range(2):
                            P.op("pe", "matmul", A[:, 256 + he * 64:256 + (he + 1) * 64], Dt7[s][:, he, :], Xb[s][:, he * 64:(he + 1) * 64], start=True, stop=True,
                                 r=[kDt7, kXb], w=[kA])
                        P.op("act", "copy", Ub[s], A[:, 256:384], r=[kA], w=[kUb])
                        for he in range(2):
                            b0 = he * 64
                            yo_ = A[:, 384 + he * 64:384 + (he + 1) * 64]
                            P.op("pe", "matmul", yo_, pd[:, 3, he, :], Hb[:, c, :], start=(he == 0), stop=False, r=[kpad, ("Hb", c), kUb], w=[kA])
                            P.op("pe", "matmul", yo_, msk[s][:, he, 128:256], Ub[s][:, b0:b0 + 64], start=False, stop=False, r=[kmsk, kUb], w=[kA])
                            P.op("pe", "matmul", yo_, msk[s][:, he, 384:512], Vtok[:, c * 128 + b0:c * 128 + b0 + 64], start=False, stop=(he == 1), r=[kmsk, "Vtok"], w=[kA])
                        for he in range(2):
                            b0 = he * 64
                            ph = A[b0:b0 + 64, 0:64]
                            P.op("pe", "matmul", ph, Btok[:, c * 128 + b0:c * 128 + b0 + 64], Ub[s][:, b0:b0 + 64], start=True, stop=False, tile_position=(0, b0),
                                 r=["Btok", kUb], w=[kA])
                            P.op("pe", "matmul", ph, Ktok[:, c * 128 + b0:c * 128 + b0 + 64], Vtok[:, c * 128 + b0:c * 128 + b0 + 64], start=False, stop=True, tile_position=(0, b0),
                                 r=["Ktok", "Vtok"], w=[kA])
                        yield
                        if d == 0:
                            P.op("dve", "tensor_copy", yfb[n_ % 2][:, c * 128:(c + 1) * 128], A[:, 384:512], r=[kA], w=[("yfb", n_ % 2)])
                        else:
                            P.op("dve", "tensor_tensor", ysum[:, c * 128:(c + 1) * 128], A[:, 384:512], yfl[0][:, c * 128:(c + 1) * 128], op=ALU.add,
                                 r=[kA, ("yfl", 0)], w=["eLe"])
                        P.op("dve", "tensor_tensor", Hst[:, c, :], A[:, 0:64], Hst[:, c, :], op=ALU.add, r=[kA, ("Hst", c)], w=[("Hst", c)])
                        P.op("dve", "tensor_scalar", Hst[:, c, :], Hst[:, c, :], PC[:, c:c + 1], None, op0=ALU.mult, r=[("Hst", c), "PC"], w=[("Hst", c)])
                        P.op("act", "copy", Hb[:, c, :], Hst[:, c, :], r=[("Hst", c)], w=[("Hb", c)])
                        yield

                    for cg in range(2):
                        gens = [unit(4 * cg + q, q) for q in range(4)]
                        alive = [True] * 4
                        while any(alive):
                            for gi, g_ in enumerate(gens):
                                if alive[gi]:
                                    try:
                                        next(g_)
                                    except StopIteration:
                                        alive[gi] = False

                    if d == 0:
                        P.op("sp", "dma_start", out=scr[(i, "yf")][ch], in_=yfb[n_ % 2], r=[("yfb", n_ % 2)], w=[("scr", "yf", ch)], dma=True)
                    else:
                        w = 0 if ch < 16 else 1
                        col0 = ch * 128
                        y3 = ysum.rearrange("p (h n) -> p h n", n=64)
                        P.op("dve", "tensor_reduce", out=st16[:, 0, :], in_=y3, axis=AX.X, op=ALU.add, r=["eLe"], w=["st16a"])
                        P.op("act", "activation", out=ysq, in_=ysum, func=AF.Square, r=["eLe"], w=["eLi"])
                        P.op("dve", "tensor_reduce", out=st16[:, 1, :], in_=ysq.rearrange("p (h n) -> p h n", n=64), axis=AX.X, op=ALU.add, r=["eLi"], w=["st16b"])
                        P.op("dve", "tensor_scalar", st16[:, 2, :], st16[:, 0, :], 1.0 / 64.0, None, op0=ALU.mult, r=["st16a"], w=["st16c"])
                        P.op("dve", "tensor_tensor", st16[:, 3, :], st16[:, 2, :], st16[:, 2, :], op=ALU.mult, r=["st16c"], w=["st16d"])
                        P.op("dve", "scalar_tensor_tensor", st16[:, 4, :], st16[:, 1, :], 1.0 / 64.0, st16[:, 3, :], op0=ALU.mult, op1=ALU.subtract,
                             r=["st16b", "st16d"], w=["st16e"])
                        P.op("act", "activation", out=st16[:, 5, :], in_=st16[:, 4, :], func=AF.Sqrt, bias=epsg[:, 0:1], scale=1.0, r=["st16e", "epsg"], w=["st16f"])
                        P.op("dve", "reciprocal", st16[:, 4, :], st16[:, 5, :], r=["st16f"], w=["st16g"])
                        P.op("dve", "tensor_tensor", y3, y3, bcast_free(st16[:, 2, :], 64), op=ALU.subtract, r=["eLe", "st16c"], w=["eLe"])
                        P.op("dve", "tensor_tensor", ynb.rearrange("p (h n) -> p h n", n=64), y3, bcast_free(st16[:, 4, :], 64), op=ALU.mult, r=["eLe", "st16g"], w=["tqs"])
                        for c in range(8):
                            P.op("pe", "transpose", bankbf[6][:, c * 128:(c + 1) * 128], ynb[:, c * 128:(c + 1) * 128], ident[:], r=["tqs", "ident"], w=[("bk", 6)])
                        for c in range(8):
                            P.op("act", "activation", out=yo1[:, c, :], in_=bankbf[6][:, c * 128:(c + 1) * 128], func=AF.Identity, scale=pvi(8, c), bias=pvi(9, c),
                                 r=[("bk", 6), "pv"], w=["emL"])
                        P.op("dve", "tensor_tensor", yo1, yo1, ldbv[sl], op=ALU.add, r=["emL", ("ldbv", 0)], w=["emL"])
                        P.op("dve", "tensor_tensor", yob, yo1, ldg[sl], op=ALU.mult, r=["emL", ("ldg", 0)], w=["ka"])
                        for oc in range(8):
                            bk = oc // 4
                            for kc in range(8):
                                P.op("pe", "matmul", banks[bk][:, (oc % 4) * 128:(oc % 4 + 1) * 128], Wo[:, kc, oc * 128:(oc + 1) * 128], yob[:, kc, :],
                                     start=(kc == 0), stop=(kc == 7), r=["Wo", "ka"], w=[("bk", bk)])
                        for oc in range(8):
                            bk = oc // 4
                            P.op("dve", "scalar_tensor_tensor", x_sb[:, oc, col0:col0 + 128], banks[bk][:, (oc % 4) * 128:(oc % 4 + 1) * 128], modT[:, l, 16 + oc, w:w + 1],
                                 x_sb[:, oc, col0:col0 + 128], op0=ALU.mult, op1=ALU.add, r=[("bk", bk), "modT"] + xkeys(col0, 128), w=xkeys(col0, 128))
                    if last and kind == "p":
                        P.op("sp", "dma_start", out=d_ns[idx, i, d], in_=Hst[:], r=[("Hst", c_) for c_ in range(8)], w=[("ns", idx, i, d)], dma=True)

        for l in range(depth):
            if l % 2 == 0:
                sgu_layer(l)
            else:
                rwkv_layer(l)
            mlp_layer(l)

        P.barrier()
        AR.reset()
        ntmp = norm_tmp(512)
        yo = [AR.f32(8, 512) for _ in range(2)]
        for ti, (t0, w) in enumerate(TILES512):
            yb = yo[ti % 2]
            emit_norm(t0, 512, lambda c: fg[:, c:c + 1], None, lambda c: yb[:, c, :], ntmp, banks[0], lambda c: [("yo", ti % 2)])
            P.op("sp", "dma_start", out=d_y[:, :, t0:t0 + 512], in_=yb, r=[("yo", ti % 2)], w=[("dy", ti)], dma=True)
        if n_rw == 0:
            pass

        P.finalize()
        sems = {k: es.enter_context(nc.semaphore("s_%s_%d" % k)) for k in sorted(P.sem_keys)}
        block = es.enter_context(nc.Block())
        P.emit(sems, block)
    return nc


def _pc(a):
    a = np.asarray(a, np.float32)
    lead = a.shape[:-1]
    b = a.reshape(lead + (8, 128))
    return np.ascontiguousarray(np.moveaxis(b, -1, 0))


def _kmajor(w):
    w = np.asarray(w, np.float32)
    K, N = w.shape[-2], w.shape[-1]
    lead = w.shape[:-2]
    b = w.reshape(lead + (K // 128, 128, N))
    return np.ascontiguousarray(np.swapaxes(b, -3, -2))


_NC_CACHE = {}
_RUNNER = [None]


def kernel(x_prompt, x_sample, state_wkv, c, c_ctx, norm1_g, norm2_g, ada_w, ada_b,
           sgu_w_in, sgu_ln_g, sgu_ln_b, sgu_w_s, sgu_b_s, sgu_w_out,
           rwkv_mu, rwkv_w_r, rwkv_w_k, rwkv_w_v, rwkv_w_o, rwkv_w0, rwkv_w1, rwkv_w2,
           rwkv_a0, rwkv_a1, rwkv_a2, rwkv_v0, rwkv_v1, rwkv_v2, rwkv_g1, rwkv_g2,
           rwkv_k_k, rwkv_k_a, rwkv_r_k, rwkv_ln_g, rwkv_ln_b, mlp_w1, mlp_w2, final_g, _depth=DEPTH):
    f = lambda a: np.asarray(a, np.float32)
    depth = _depth
    shared = {}
    shared["ident"] = np.eye(128, dtype=np.float32)
    bd = np.zeros((128, 128), np.float32); bd[:64, :64] = 1; bd[64:, 64:] = 1
    shared["onesbd"] = bd
    s_i = np.arange(128)[:, None]; t_i = np.arange(128)[None, :]
    su = (s_i < t_i).astype(np.float32); iu = (s_i <= t_i).astype(np.float32)
    sl = (s_i > t_i).astype(np.float32); il = (s_i >= t_i).astype(np.float32)
    shared["maska"] = np.stack([np.concatenate([su, iu, su, iu], 1), np.concatenate([sl, il, sl, il], 1)])
    shared["maskb"] = np.stack([np.concatenate([sl, sl], 1), np.concatenate([su, su], 1)])
    bl = []
    for lev in range(7):
        bs = 2 ** lev
        bl.append(((s_i // (2 * bs) == t_i // (2 * bs)) & ((s_i // bs) % 2 == 1) & ((t_i // bs) % 2 == 0)).astype(np.float32))
    bl = np.stack(bl, axis=1)
    shared["blkm"] = np.ascontiguousarray(np.stack([bl, bl.transpose(2, 1, 0)]))
    shared["ada_w"] = _kmajor(f(ada_w))
    shared["ada_b"] = np.ascontiguousarray(f(ada_b).reshape(4, 48, 128).transpose(2, 0, 1))
    shared["norm_g"] = np.ascontiguousarray(np.stack([_pc(norm1_g), _pc(norm2_g)], axis=2))
    shared["final_g"] = _pc(final_g)
    shared["sgu_w_in"] = _kmajor(f(sgu_w_in))
    shared["sgu_w_out"] = _kmajor(f(sgu_w_out))
    shared["sgu_wsT"] = np.ascontiguousarray(f(sgu_w_s).transpose(0, 3, 1, 2))
    shared["sgu_bs"] = np.ascontiguousarray(f(sgu_b_s).reshape(2, 1, 2048))
    shared["sgu_ln"] = np.ascontiguousarray(np.stack([f(sgu_ln_g), f(sgu_ln_b)], axis=1))
    w1 = f(mlp_w1).reshape(4, 8, 128, 8, 512)
    shared["mlp_w1"] = np.ascontiguousarray(w1.transpose(0, 3, 2, 1, 4))
    w2 = f(mlp_w2).reshape(4, 8, 4, 128, 1024)
    shared["mlp_w2"] = np.ascontiguousarray(w2.transpose(0, 1, 3, 2, 4))
    shared["rwkv_w"] = np.ascontiguousarray(np.stack([_kmajor(f(rwkv_w_r)), _kmajor(f(rwkv_w_k)), _kmajor(f(rwkv_w_v)), _kmajor(f(rwkv_w_o))], axis=1))
    shared["rwkv_mu"] = _pc(rwkv_mu)
    shared["rwkv_l1"] = np.ascontiguousarray(np.concatenate([_kmajor(f(rwkv_w1)), _kmajor(f(rwkv_a1))], axis=1))
    shared["rwkv_l2"] = np.ascontiguousarray(np.concatenate([f(rwkv_w2), f(rwkv_a2)], axis=1))
    shared["rwkv_g1"] = _kmajor(f(rwkv_g1))
    shared["rwkv_g2"] = np.ascontiguousarray(f(rwkv_g2))
    shared["rwkv_v1"] = _kmajor(f(rwkv_v1))[0]
    shared["rwkv_v2"] = np.ascontiguousarray(f(rwkv_v2)[0])
    v0 = np.broadcast_to(f(rwkv_v0).reshape(1, 1024), (2, 1024))
    zz = np.zeros((2, 1024), np.float32)
    pvl = [f(rwkv_w0)[:, 0], f(rwkv_w0)[:, 1], f(rwkv_a0)[:, 0], f(rwkv_a0)[:, 1], v0, f(rwkv_k_k), f(rwkv_k_a),
           f(rwkv_r_k).reshape(2, 1024), f(rwkv_ln_g), f(rwkv_ln_b), zz]
    shared["rwkv_pv"] = _pc(np.stack(pvl, axis=1))

    in_maps = []
    xs = f(x_sample); xp = f(x_prompt); st = f(state_wkv)
    for b in range(NCORES):
        m = dict(shared)
        xt = np.concatenate([xs[b], xp[2 * b], xp[2 * b + 1]], axis=0)
        m["xT"] = np.ascontiguousarray(xt.reshape(NT, 8, 128).transpose(2, 1, 0))
        cond = np.stack([f(c)[b], f(c_ctx)], axis=-1)
        m["condT"] = np.ascontiguousarray(cond.reshape(8, 128, 2).transpose(1, 0, 2))
        s = st[b].reshape(2, 2, 8, 2, 64, 64)
        m["state0"] = np.ascontiguousarray(s.transpose(3, 5, 0, 1, 2, 4).reshape(128, 2, 2, 8, 64))
        in_maps.append(m)

    if depth not in _NC_CACHE:
        _NC_CACHE[depth] = build(depth)
    nc = _NC_CACHE[depth]
    if _RUNNER[0] is not None:
        res = _RUNNER[0](nc, in_maps)
    else:
        ncr = int(os.environ.get("KCORES", NCORES))
        res = run_bass_kernel_spmd(nc, in_maps[:ncr], core_ids=list(range(ncr)))
        if ncr < NCORES:
            res.results.extend([res.results[0]] * (NCORES - ncr))
    y_prompt = np.zeros((16, 256, D), np.float32)
    y_sample = np.zeros((8, 2048, D), np.float32)
    n_rw = depth // 2
    new_state = np.zeros((16, max(n_rw, 1), 2, 16, 64, 64), np.float32)
    for b in range(NCORES):
        r = res.results[b]
        yt = np.asarray(r["yT"]).transpose(2, 1, 0).reshape(NT, D)
        y_sample[b] = yt[:2048]
        y_prompt[2 * b] = yt[2048:2304]
        y_prompt[2 * b + 1] = yt[2304:2560]
        ns = np.asarray(r["new_state"])
        ns = ns.reshape(2, 2, 2, 2, 64, 8, 64)
        ns = ns.transpose(0, 1, 2, 5, 3, 6, 4).reshape(2, 2, 2, 16, 64, 64)
        for q in range(2):
            new_state[2 * b + q, :n_rw] = ns[q, :n_rw]
    if n_rw == 0:
        new_state = new_state[:, :0]
    return (y_prompt, y_sample, new_state)
```

```python
import os
from contextlib import ExitStack
import numpy as np
import concourse.bass as bass
import concourse.mybir as mybir
from concourse.bass_utils import run_bass_kernel_spmd

F32 = mybir.dt.float32
BF16 = mybir.dt.bfloat16
AF = mybir.ActivationFunctionType
ALU = mybir.AluOpType
AX = mybir.AxisListType

NCORES = 8
D = 1024
NT = 2560
DEPTH = 4
NORM_EPS = 1e-6
GN_EPS = 64e-5
NEGC = -0.6065306597126334
NDMASEM = 8
_DBG = {}

class Op:
    __slots__ = ("eng", "fn", "deps", "sig", "waits", "is_dma", "dsem", "n", "prev_dma")

    def __init__(self, eng, fn, is_dma):
        self.eng = eng; self.fn = fn; self.deps = []; self.sig = None; self.waits = []
        self.is_dma = is_dma; self.dsem = None; self.n = 0; self.prev_dma = None


class Prog:
    ENGS = ("pe", "act", "dve", "pool", "sp")

    def __init__(self):
        self.ops = []
        self.last_w = {}
        self.readers = {}
        self.bar = []
        self.last_eng = {}
        self.last_dma = {}
        self.dcnt = {e: 0 for e in self.ENGS}

    def barrier(self):
        self.bar = list(self.last_eng.values()) + list(self.last_dma.values())
        self.last_w = {}
        self.readers = {}

    def op(self, eng, name, *args, r=(), w=(), dma=False, **kw):
        o = Op(eng, (name, args, kw), dma)
        o.n = len(self.ops)
        deps = {}
        for k in r:
            p = self.last_w.get(k)
            if p is not None:
                deps[p.n] = (p, True)
        for k in w:
            p = self.last_w.get(k)
            if p is not None and p.n not in deps:
                deps[p.n] = (p, True)
            for q in self.readers.get(k, ()):
                if q.n not in deps:
                    deps[q.n] = (q, False)
        for k in r:
            self.readers.setdefault(k, []).append(o)
        for k in w:
            self.last_w[k] = o
            self.readers[k] = []
        for p, raw in deps.values():
            if p.is_dma:
                o.deps.append(p)
            elif p.eng == o.eng and not o.is_dma:
                if raw and p.eng != "pe":
                    o.deps.append(p)
            else:
                o.deps.append(p)
        for p in self.bar:
            if p.is_dma or p.eng != o.eng or o.is_dma:
                o.deps.append(p)
        if dma:
            i = self.dcnt[eng]; self.dcnt[eng] += 1
            key = (eng, i % NDMASEM)
            o.prev_dma = self.last_dma.get(key)
            self.last_dma[key] = o
            o.dsem = key
        else:
            self.last_eng[eng] = o
        self.ops.append(o)
        return o

    def finalize(self):
        need = set()
        for o in self.ops:
            for p in o.deps:
                if not p.is_dma:
                    need.add(p.n)
        cnt = {e: 0 for e in self.ENGS}
        dval = {}
        for o in self.ops:
            if o.is_dma:
                key = o.dsem
                v = dval.get(key, 0) + 16
                dval[key] = v
                o.dsem = (key, v)
            elif o.n in need:
                cnt[o.eng] += 1
                o.sig = (("c" + o.eng, 0), cnt[o.eng])
        self.dma_final = dval
        waited = {e: {} for e in self.ENGS}
        for o in self.ops:
            wl = {}
            if o.is_dma and o.prev_dma is not None:
                k, v = o.prev_dma.dsem
                wl[k] = v
            for p in o.deps:
                k, v = p.dsem if p.is_dma else p.sig
                if wl.get(k, 0) < v:
                    wl[k] = v
            wd = waited[o.eng]
            for k, v in wl.items():
                if wd.get(k, 0) < v:
                    wd[k] = v
                    o.waits.append((k, v))
        self.sem_keys = set(dval.keys())
        for e in self.ENGS:
            if cnt[e]:
                self.sem_keys.add(("c" + e, 0))

    def emit(self, sems, block):
        byeng = {e: [o for o in self.ops if o.eng == e] for e in self.ENGS}
        finals = self.dma_final

        def run(eng_obj, lst, fk):
            for o in lst:
                for k, v in o.waits:
                    eng_obj.wait_ge(sems[k], v)
                name, args, kw = o.fn
                ins = getattr(eng_obj, name)(*args, **kw)
                if o.is_dma:
                    ins.then_inc(sems[o.dsem[0]], 16)
                elif o.sig is not None:
                    ins.then_inc(sems[o.sig[0]], 1)
            for k in fk:
                eng_obj.wait_ge(sems[k], finals[k])

        @block.tensor
        def _(e):
            run(e, byeng["pe"], [])

        @block.scalar
        def _(e):
            run(e, byeng["act"], [k for k in finals if k[0] == "act"])

        @block.vector
        def _(e):
            run(e, byeng["dve"], [])

        @block.gpsimd
        def _(e):
            run(e, byeng["pool"], [k for k in finals if k[0] == "pool"])

        @block.sync
        def _(e):
            run(e, byeng["sp"], [k for k in finals if k[0] == "sp"])


class Arena:
    def __init__(self, t, nwords):
        self.t = t; self.n = nwords; self.off = 0; self.reg = {}

    def reset(self):
        self.off = 0

    def f32(self, *shape, name=None):
        n = int(np.prod(shape))
        if name: self.reg[name] = (self.off, n, "f32", shape)
        a = self.t[:, self.off:self.off + n]
        self.off += n
        assert self.off <= self.n, ("arena overflow", self.off, self.n)
        return self._shape(a, shape)

    def bf16(self, *shape, name=None):
        n = int(np.prod(shape))
        nw = (n + 1) // 2
        if name: self.reg[name] = (self.off, nw, "bf16", shape)
        a = self.t[:, self.off:self.off + nw].bitcast(BF16)
        if n != 2 * nw:
            a = a[:, 0:n]
        self.off += nw
        assert self.off <= self.n, ("arena overflow", self.off, self.n)
        return self._shape(a, shape)

    @staticmethod
    def _shape(a, shape):
        if len(shape) == 1:
            return a
        if len(shape) == 2:
            return a.rearrange("p (a b) -> p a b", a=shape[0])
        if len(shape) == 3:
            return a.rearrange("p (a b c) -> p a b c", a=shape[0], b=shape[1])
        if len(shape) == 4:
            return a.rearrange("p (a b c d) -> p a b c d", a=shape[0], b=shape[1], c=shape[2])
        raise ValueError(shape)


def bcast_free(ap, n):
    return bass.AP(ap.tensor, ap.offset, [list(d) for d in ap.ap] + [[0, n]])


TILES512 = [(0, 0), (512, 0), (1024, 0), (1536, 0), (2048, 1)]
TILES256 = [(256 * i, 0, i > 0, i < 7) for i in range(8)] + [(2048, 1, False, False), (2304, 1, False, False)]
SEQS = [(0, 16, "s", 0), (16, 2, "p", 0), (18, 2, "p", 1)]


def build(depth=DEPTH):
    nc = bass.Bass("TRN2", target_bir_lowering=False)
    n_rw = depth // 2

    def din(name, shape, dt=F32):
        return nc.dram_tensor(name, list(shape), dt, kind="ExternalInput").ap()

    def dout(name, shape, dt=F32):
        return nc.dram_tensor(name, list(shape), dt, kind="ExternalOutput").ap()

    def dscr(name, shape, dt):
        return nc.dram_tensor(name, list(shape), dt, kind="Internal").ap()

    d_x = din("xT", [128, 8, NT])
    d_cond = din("condT", [128, 8, 2])
    d_state = din("state0", [128, 2, 2, 8, 64])
    d_ident = din("ident", [128, 128])
    d_onesbd = din("onesbd", [128, 128])
    d_maska = din("maska", [2, 128, 512])
    d_maskb = din("maskb", [2, 128, 256])
    d_blk = din("blkm", [2, 128, 7, 128])
    d_adaw = din("ada_w", [4, 128, 8, 6144])
    d_adab = din("ada_b", [128, 4, 48])
    d_ng = din("norm_g", [128, 4, 2, 8])
    d_fg = din("final_g", [128, 8])
    d_win = din("sgu_w_in", [2, 128, 8, 2048])
    d_wout = din("sgu_w_out", [2, 128, 8, 1024])
    d_wsT = din("sgu_wsT", [2, 128, 16, 128])
    d_bs = din("sgu_bs", [2, 1, 2048])
    d_lng = din("sgu_ln", [2, 2, 1024])
    d_w1 = din("mlp_w1", [4, 8, 128, 8, 512])
    d_w2 = din("mlp_w2", [4, 8, 128, 4, 1024])
    d_rw = din("rwkv_w", [2, 4, 128, 8, 1024])
    d_mu = din("rwkv_mu", [128, 2, 6, 8])
    d_l1 = din("rwkv_l1", [2, 4, 128, 8, 64])
    d_l2 = din("rwkv_l2", [2, 4, 64, 1024])
    d_g1 = din("rwkv_g1", [2, 128, 8, 128])
    d_g2 = din("rwkv_g2", [2, 128, 1024])
    d_v1 = din("rwkv_v1", [128, 8, 32])
    d_v2 = din("rwkv_v2", [32, 1024])
    d_pv = din("rwkv_pv", [128, 2, 11, 8])
    d_y = dout("yT", [128, 8, NT])
    d_ns = dout("new_state", [2, 2, 2, 128, 8, 64])
    scr = {}
    for i in range(n_rw):
        for nm in ("r", "k", "v", "kk", "af", "ab", "g", "bv"):
            scr[(i, nm)] = dscr("scr_%s_%d" % (nm, i), [20, 128, 8, 128], BF16)
        for nm in ("sf", "sb"):
            scr[(i, nm)] = dscr("scr_%s_%d" % (nm, i), [20, 128, 8, 128], F32)
        scr[(i, "yf")] = dscr("scr_yf_%d" % i, [20, 128, 1024], BF16)

    P = Prog()
    with ExitStack() as es:
        def sbt(name, shape, dt):
            return es.enter_context(nc.sbuf_tensor(name, list(shape), dt))

        x_sb = sbt("x_sb", [128, 8, NT], F32)
        modT = sbt("modT", [128, 4, 48, 2], F32)
        gsc = sbt("gsc", [128, 4, 2, 8, 2], F32)
        adab = sbt("adab", [128, 4, 48], F32)
        ng = sbt("ng", [128, 4, 2, 8], F32)
        fg = sbt("fg", [128, 8], F32)
        ident = sbt("ident_sb", [128, 128], BF16)
        onesbd = sbt("onesbdb", [128, 128], BF16)
        onesm = sbt("onesm", [128, 128], BF16)
        onesf = sbt("onesf", [128, 128], F32)
        maska = sbt("maska_sb", [128, 2, 512], BF16)
        maskb = sbt("maskb_sb", [128, 2, 256], BF16)
        blkm = sbt("blk_sb", [128, 2, 7, 128], BF16)
        epsn = sbt("epsn", [128, 1], F32)
        epsg = sbt("epsg", [128, 1], F32)
        epsk = sbt("epsk", [128, 1], F32)
        hm = sbt("hm", [128, 2], F32)
        pv = sbt("pv", [128, 2, 11, 8], F32)
        mu = sbt("mu", [128, 2, 6, 8], F32)
        omm = sbt("omm", [128, 2, 6, 8], F32)
        hmu = sbt("hmu", [128, 2, 6, 8], F32)
        omka = sbt("omka", [128, 2, 8], F32)
        tmka = sbt("tmka", [128, 2, 8], F32)
        Hst = sbt("Hst", [128, 8, 64], F32)
        Hb = sbt("Hb", [128, 8, 64], BF16)
        condb = sbt("condb", [128, 8, 2], BF16)
        ARW = 27500
        arena_t = sbt("arena", [128, ARW], F32)
        AR = Arena(arena_t, ARW)
        _DBG['AR'] = AR
        banks = [es.enter_context(nc.psum_tensor("bank%d" % i, [128, 512], F32)) for i in range(8)]
        bankbf = [b[:].bitcast(BF16) for b in banks]

        P.op("sp", "dma_start", out=adab[:], in_=d_adab[:, :, :], w=["adab"], dma=True)
        P.op("sp", "dma_start", out=ng[:], in_=d_ng[:, :, :, :], w=["ng"], dma=True)
        P.op("sp", "dma_start", out=fg[:], in_=d_fg[:, :], w=["fg"], dma=True)
        P.op("sp", "dma_start", out=pv[:], in_=d_pv[:, :, :, :], w=["pv"], dma=True)
        P.op("sp", "dma_start", out=mu[:], in_=d_mu[:, :, :, :], w=["mu"], dma=True)
        P.op("pool", "dma_start", out=ident[:], in_=d_ident[:, :], w=["ident"], dma=True)
        P.op("pool", "dma_start", out=onesbd[:], in_=d_onesbd[:, :], w=["onesbd"], dma=True)
        for dd in range(2):
            P.op("pool", "dma_start", out=maska[:, dd, :], in_=d_maska[dd], w=["maska"], dma=True)
            P.op("pool", "dma_start", out=maskb[:, dd, :], in_=d_maskb[dd], w=["maskb"], dma=True)
            P.op("pool", "dma_start", out=blkm[:, dd, :, :], in_=d_blk[dd], w=["blk"], dma=True)
        P.op("dve", "memset", onesm[:], 1.0 / 1024.0, w=["onesm"])
        P.op("dve", "memset", onesf[:], 1.0, w=["onesf"])
        P.op("dve", "memset", epsn[:], NORM_EPS, w=["epsn"])
        P.op("dve", "memset", epsg[:], GN_EPS, w=["epsg"])
        P.op("dve", "memset", epsk[:], 1e-24, w=["epsk"])
        P.op("dve", "memset", hm[0:64, 0:1], 1.0, w=["hm00"])
        P.op("dve", "memset", hm[64:128, 0:1], 0.0, w=["hm10"])
        P.op("dve", "memset", hm[0:64, 1:2], 0.0, w=["hm01"])
        P.op("dve", "memset", hm[64:128, 1:2], 1.0, w=["hm11"])
        P.op("dve", "tensor_scalar", omm[:], mu[:], -1.0, 1.0, op0=ALU.mult, op1=ALU.add, r=["mu"], w=["omm"])
        P.op("dve", "tensor_scalar", hmu[:], mu[:], 0.5, None, op0=ALU.mult, r=["mu"], w=["hmu"])
        P.op("dve", "tensor_scalar", omka[:], pv[:, :, 6, :], -1.0, 1.0, op0=ALU.mult, op1=ALU.add, r=["pv"], w=["omka"])
        P.op("dve", "tensor_scalar", tmka[:], pv[:, :, 6, :], -2.0, 2.0, op0=ALU.mult, op1=ALU.add, r=["pv"], w=["tmka"])

        def xkeys(t0, n):
            return [("x", k) for k in range(t0 // 128, (t0 + n + 127) // 128)]

        for (t0, _) in TILES512:
            P.op("sp", "dma_start", out=x_sb[:, :, t0:t0 + 512], in_=d_x[:, :, t0:t0 + 512], w=xkeys(t0, 512), dma=True)

        AR.reset()
        condf = AR.f32(8, 2)
        adaw = [AR.bf16(8, 512) for _ in range(2)]
        P.op("sp", "dma_start", out=condf, in_=d_cond[:, :, :], w=["condf"], dma=True)
        P.op("act", "activation", out=condb[:], in_=condf, func=AF.Silu, r=["condf"], w=["condb"])
        pm = banks[0][:, 0:96]
        nblk = 0
        for l in range(depth):
            for blk in range(12):
                buf = adaw[nblk % 2]
                bkey = ("adaw", nblk % 2)
                nblk += 1
                P.op("pool", "dma_start", out=buf, in_=d_adaw[l, :, :, blk * 512:(blk + 1) * 512], w=[bkey], dma=True)
                for m in range(4):
                    n = blk * 4 + m
                    for kc in range(8):
                        P.op("pe", "matmul", pm[:, 2 * n:2 * n + 2], buf[:, kc, m * 128:(m + 1) * 128], condb[:, kc, :],
                             start=(kc == 0), stop=(kc == 7), r=[bkey, "condb"], w=["pm"])
            P.op("dve", "tensor_tensor", modT[:, l, :, :], pm.rearrange("p (n w) -> p n w", w=2), bcast_free(adab[:, l, :], 2), op=ALU.add,
                 r=["pm", "adab"], w=["modT"])
            for j in range(2):
                P.op("dve", "tensor_scalar", gsc[:, l, j, :, :], modT[:, l, (3 * j + 1) * 8:(3 * j + 2) * 8, :], 1.0, None, op0=ALU.add,
                     r=["modT"], w=["gsc"])
                P.op("dve", "tensor_tensor", gsc[:, l, j, :, :], gsc[:, l, j, :, :], bcast_free(ng[:, l, j, :], 2), op=ALU.mult,
                     r=["gsc", "ng"], w=["gsc"])

        def emit_norm(t0, n, scale_ap, bias_ap, out_ap, tmp, ps_bank, out_keys):
            sq, sd, rstd, tmpn = tmp
            ps = ps_bank[:, 0:n]
            for c in range(8):
                s = sq[c % 2]
                P.op("act", "activation", out=s[:, 0:n], in_=x_sb[:, c, t0:t0 + n], func=AF.Square, r=xkeys(t0, n), w=[("nsq", c % 2)])
                P.op("pe", "matmul", ps, onesm[:], s[:, 0:n], start=(c == 0), stop=(c == 7), r=[("nsq", c % 2), "onesm"], w=["nps"])
            P.op("act", "activation", out=sd[:, 0:n], in_=ps, func=AF.Sqrt, bias=epsn[:, 0:1], scale=1.0, r=["nps", "epsn"], w=["nsd"])
            P.op("dve", "reciprocal", rstd[:, 0:n], sd[:, 0:n], r=["nsd"], w=["nrstd"])
            for c in range(8):
                tn = tmpn[c % 2]
                P.op("dve", "tensor_tensor", tn[:, 0:n], x_sb[:, c, t0:t0 + n], rstd[:, 0:n], op=ALU.mult,
                     r=xkeys(t0, n) + ["nrstd"], w=[("ntmp", c % 2)])
                if bias_ap is not None:
                    P.op("act", "activation", out=out_ap(c), in_=tn[:, 0:n], func=AF.Identity, scale=scale_ap(c), bias=bias_ap(c),
                         r=[("ntmp", c % 2), "gsc", "modT", "fg"], w=out_keys(c))
                else:
                    P.op("act", "activation", out=out_ap(c), in_=tn[:, 0:n], func=AF.Identity, scale=scale_ap(c),
                         r=[("ntmp", c % 2), "gsc", "modT", "fg"], w=out_keys(c))

        def norm_tmp(n):
            return ([AR.bf16(n) for _ in range(2)], AR.f32(n), AR.f32(n), [AR.f32(n) for _ in range(2)])

        def sgu_layer(l):
            i = l // 2
            P.barrier()
            AR.reset()
            w_in = AR.bf16(8, 2048)
            w_out = AR.bf16(8, 1024)
            wsT = AR.bf16(16, 128)
            lnG = AR.f32(1024)
            lnB = AR.f32(1024)
            bsrow = AR.bf16(2048)
            onesrow = AR.bf16(64)
            ntmp = norm_tmp(512)
            h1 = AR.bf16(8, 512)
            u = AR.bf16(8, 512)
            vt = AR.f32(1024)
            vsq = AR.f32(1024)
            vn = AR.bf16(4, 1024)
            st = AR.f32(8)
            for q in range(4):
                P.op("pool", "dma_start", out=w_in[:, :, q * 512:(q + 1) * 512], in_=d_win[i, :, :, q * 512:(q + 1) * 512], w=[("w_in", q)], dma=True)
            P.op("pool", "dma_start", out=wsT, in_=d_wsT[i], w=["wsT"], dma=True)
            P.op("pool", "dma_start", out=bsrow[0:1, :], in_=d_bs[i], w=["bsrow"], dma=True)
            for q in range(2):
                P.op("pool", "dma_start", out=w_out[:, :, q * 512:(q + 1) * 512], in_=d_wout[i, :, :, q * 512:(q + 1) * 512], w=[("w_out", q)], dma=True)
            P.op("sp", "dma_start", out=lnG, in_=bass.AP(d_lng.tensor, d_lng[i, 0].offset, [[0, 128], [1, 1024]]), w=["lnG"], dma=True)
            P.op("sp", "dma_start", out=lnB, in_=bass.AP(d_lng.tensor, d_lng[i, 1].offset, [[0, 128], [1, 1024]]), w=["lnB"], dma=True)
            P.op("dve", "memset", onesrow[0:1, :], 1.0, w=["onesrow"])
            for (t0, w) in TILES512:
                emit_norm(t0, 512, lambda c: gsc[:, l, 0, c, w:w + 1], lambda c: modT[:, l, 0 + c, w:w + 1],
                          lambda c: h1[:, c, :], ntmp, banks[0], lambda c: ["h1"])
                for oc in range(8):
                    ps = banks[1 + oc % 2]
                    pk = ("bk", 1 + oc % 2)
                    for kc in range(8):
                        P.op("pe", "matmul", ps[:], w_in[:, kc, oc * 128:(oc + 1) * 128], h1[:, kc, :], start=(kc == 0), stop=(kc == 7),
                             r=[("w_in", oc // 4), "h1"], w=[pk])
                    P.op("act", "activation", out=u[:, oc, :], in_=ps[:], func=AF.Gelu_apprx_tanh, r=[pk], w=[("u", oc)])
                for q in range(4):
                    for nb in range(2):
                        ps = banks[3 + nb]
                        pk = ("bk", 3 + nb)
                        for kc in range(8):
                            P.op("pe", "matmul", ps[:], h1[:, kc, q * 128:(q + 1) * 128], w_in[:, kc, 1024 + nb * 512:1024 + (nb + 1) * 512],
                                 start=(kc == 0), stop=(kc == 7), r=[("w_in", 2 + nb), "h1"], w=[pk])
                        P.op("act", "activation", out=vt[:, nb * 512:(nb + 1) * 512], in_=ps[:], func=AF.Gelu_apprx_tanh, r=[pk], w=["vt"])
                    P.op("dve", "tensor_reduce", out=st[:, 0:1], in_=vt, axis=AX.X, op=ALU.add, r=["vt"], w=["st0"])
                    P.op("act", "activation", out=vsq, in_=vt, func=AF.Square, r=["vt"], w=["vsq"])
                    P.op("dve", "tensor_reduce", out=st[:, 1:2], in_=vsq, axis=AX.X, op=ALU.add, r=["vsq"], w=["st1"])
                    P.op("dve", "tensor_scalar", st[:, 2:3], st[:, 0:1], 1.0 / 1024.0, None, op0=ALU.mult, r=["st0"], w=["st2"])
                    P.op("dve", "tensor_tensor", st[:, 3:4], st[:, 2:3], st[:, 2:3], op=ALU.mult, r=["st2"], w=["st3"])
                    P.op("dve", "scalar_tensor_tensor", st[:, 4:5], st[:, 1:2], 1.0 / 1024.0, st[:, 3:4], op0=ALU.mult, op1=ALU.subtract,
                         r=["st1", "st3"], w=["st4"])
                    P.op("act", "activation", out=st[:, 5:6], in_=st[:, 4:5], func=AF.Sqrt, bias=epsn[:, 0:1], scale=1.0, r=["st4", "epsn"], w=["st5"])
                    P.op("dve", "reciprocal", st[:, 6:7], st[:, 5:6], r=["st5"], w=["st6"])
                    P.op("dve", "tensor_scalar", vt, vt, st[:, 2:3], st[:, 6:7], op0=ALU.subtract, op1=ALU.mult, r=["vt", "st2", "st6"], w=["vt"])
                    P.op("dve", "tensor_tensor", vt, vt, lnG, op=ALU.mult, r=["vt", "lnG"], w=["vt"])
                    P.op("dve", "tensor_tensor", vn[:, q, :], vt, lnB, op=ALU.add, r=["vt", "lnB"], w=[("vn", q)])
                for c in range(8):
                    ps = banks[5 + c % 2]
                    pk = ("bk", 5 + c % 2)
                    for q in range(4):
                        for he in range(2):
                            g = 2 * c + he
                            P.op("pe", "matmul", ps[he * 64:(he + 1) * 64, q * 128:(q + 1) * 128], vn[:, q, g * 64:(g + 1) * 64], wsT[:, g, :],
                                 start=True, stop=False, tile_position=(0, he * 64), r=[("vn", q), "wsT"], w=[pk])
                            P.op("pe", "matmul", ps[he * 64:(he + 1) * 64, q * 128:(q + 1) * 128], onesrow[0:1, :], bsrow[0:1, g * 128:(g + 1) * 128],
                                 start=False, stop=True, tile_position=(0, he * 64), r=["onesrow", "bsrow"], w=[pk])
                    P.op("dve", "tensor_tensor", u[:, c, :], ps[:], u[:, c, :], op=ALU.mult, r=[pk, ("u", c)], w=[("u", c)])
                for oc in range(8):
                    ps = banks[1 + oc % 2]
                    pk = ("bk", 1 + oc % 2)
                    for kc in range(8):
                        P.op("pe", "matmul", ps[:], w_out[:, kc, oc * 128:(oc + 1) * 128], u[:, kc, :], start=(kc == 0), stop=(kc == 7),
                             r=[("w_out", oc // 4), ("u", kc)], w=[pk])
                    P.op("dve", "scalar_tensor_tensor", x_sb[:, oc, t0:t0 + 512], ps[:], modT[:, l, 16 + oc, w:w + 1], x_sb[:, oc, t0:t0 + 512],
                         op0=ALU.mult, op1=ALU.add, r=[pk, "modT"] + xkeys(t0, 512), w=xkeys(t0, 512))

        def mlp_layer(l):
            P.barrier()
            AR.reset()
            h2 = AR.bf16(8, NT)
            w1b = [AR.bf16(8, 512) for _ in range(2)]
            w2b = [AR.bf16(4, 1024) for _ in range(2)]
            hid = [AR.bf16(4, 512) for _ in range(2)]
            rl = [AR.bf16(512) for _ in range(2)]
            ntmp = norm_tmp(512)

            def load(jb):
                P.op("pool", "dma_start", out=w1b[jb % 2], in_=d_w1[l, jb], w=[("w1b", jb % 2)], dma=True)
                P.op("pool", "dma_start", out=w2b[jb % 2], in_=d_w2[l, jb], w=[("w2b", jb % 2)], dma=True)
            load(0)
            for (t0, w) in TILES512:
                emit_norm(t0, 512, lambda c: gsc[:, l, 1, c, w:w + 1], lambda c: modT[:, l, 24 + c, w:w + 1],
                          lambda c: h2[:, c, t0:t0 + 512], ntmp, banks[0], lambda c: [("h2", t0)])
            nph = 0
            npo = 0
            nh = 0
            for jb in range(8):
                if jb + 1 < 8:
                    load(jb + 1)
                W1 = w1b[jb % 2]
                W2 = w2b[jb % 2]
                for (t0, w) in TILES512:
                    hd = hid[nh % 2]
                    hk = ("hid", nh % 2)
                    nh += 1
                    for hc in range(4):
                        ps = banks[1 + nph % 4]
                        pk = ("bk", 1 + nph % 4)
                        rr = rl[nph % 2]
                        rk = ("rl", nph % 2)
                        nph += 1
                        for kc in range(8):
                            P.op("pe", "matmul", ps[:], W1[:, kc, hc * 128:(hc + 1) * 128], h2[:, kc, t0:t0 + 512], start=(kc == 0), stop=(kc == 7),
                                 r=[("w1b", jb % 2), ("h2", t0)], w=[pk])
                        P.op("act", "activation", out=rr, in_=ps[:], func=AF.Relu, r=[pk], w=[rk])
                        P.op("dve", "tensor_tensor", hd[:, hc, :], rr, rr, op=ALU.mult, r=[rk], w=[hk])
                    for oc in range(8):
                        ps = banks[5 + npo % 3]
                        pk = ("bk", 5 + npo % 3)
                        npo += 1
                        for hc in range(4):
                            P.op("pe", "matmul", ps[:], W2[:, hc, oc * 128:(oc + 1) * 128], hd[:, hc, :], start=(hc == 0), stop=(hc == 3),
                                 r=[("w2b", jb % 2), hk], w=[pk])
                        P.op("dve", "scalar_tensor_tensor", x_sb[:, oc, t0:t0 + 512], ps[:], modT[:, l, 40 + oc, w:w + 1], x_sb[:, oc, t0:t0 + 512],
                             op0=ALU.mult, op1=ALU.add, r=[pk, "modT"] + xkeys(t0, 512), w=xkeys(t0, 512))

        def rwkv_layer(l):
            i = l // 2
            pvi = lambda idx, c: pv[:, i, idx, c:c + 1]
            P.barrier()
            AR.reset()
            wring = [AR.bf16(8, 512) for _ in range(2)]
            l1 = AR.bf16(4, 8, 64)
            l2 = AR.bf16(4, 1024)
            g1w = AR.bf16(8, 128)
            g2w = AR.bf16(1024)
            v1w = AR.bf16(8, 32)
            v2w = AR.bf16(1024)
            ntmp = norm_tmp(258)
            hh = AR.bf16(8, 258)
            ss = AR.bf16(8, 256)
            t1 = [AR.bf16(256) for _ in range(2)]
            xj = AR.bf16(8, 256)
            p1b = [AR.bf16(256) for _ in range(2)]
            sig = AR.f32(8, 256)
            o_r = AR.bf16(8, 256)
            o_k = AR.bf16(8, 256)
            o_v = AR.bf16(8, 256)
            o_kk = AR.bf16(8, 256)
            o_af = AR.bf16(8, 256)
            o_ab = AR.bf16(8, 256)
            o_g = AR.bf16(8, 256)
            o_bv = AR.bf16(8, 256)
            tq = AR.bf16(8, 256)
            vf = AR.bf16(8, 256) if i == 1 else None
            sgv = o_bv
            tmpa = AR.f32(256)
            tmpb = AR.f32(256)
            for q in range(4):
                P.op("pool", "dma_start", out=l1[:, q, :, :], in_=d_l1[i, q], w=["l1"], dma=True)
                P.op("pool", "dma_start", out=l2[0:64, q, :], in_=d_l2[i, q], w=["l2"], dma=True)
            P.op("pool", "dma_start", out=g1w, in_=d_g1[i], w=["g1w"], dma=True)
            P.op("pool", "dma_start", out=g2w, in_=d_g2[i], w=["g2w"], dma=True)
            if i == 1:
                P.op("pool", "dma_start", out=v1w, in_=d_v1[:, :, :], w=["v1w"], dma=True)
                P.op("pool", "dma_start", out=v2w[0:32, :], in_=d_v2[:, :], w=["v2w"], dma=True)

            def wload(wi):
                for hb in range(2):
                    P.op("pool", "dma_start", out=wring[hb], in_=d_rw[i, wi, :, :, hb * 512:(hb + 1) * 512], w=[("wring", hb)], dma=True)

            npb = [0]

            def proj(wi, evac):
                for o2 in range(4):
                    bk = 1 + npb[0] % 3
                    npb[0] += 1
                    pk = ("bk", bk)
                    for sub in range(2):
                        oc = 2 * o2 + sub
                        for kc in range(8):
                            P.op("pe", "matmul", banks[bk][:, sub * 256:(sub + 1) * 256], wring[oc // 4][:, kc, (oc % 4) * 128:(oc % 4 + 1) * 128], xj[:, kc, :],
                                 start=(kc == 0), stop=(kc == 7), r=[("wring", oc // 4), "xj"], w=[pk])
                    evac(banks[bk][:].rearrange("p (a b) -> p a b", a=2), 2 * o2, pk)

            def mix(j):
                for c in range(8):
                    tt = t1[c % 2]
                    P.op("act", "activation", out=tt, in_=hh[:, c, 1:257], func=AF.Identity, scale=omm[:, i, j, c:c + 1], r=["hh", "omm"], w=[("t1", c % 2)])
                    P.op("dve", "scalar_tensor_tensor", xj[:, c, :], ss[:, c, :], hmu[:, i, j, c:c + 1], tt, op0=ALU.mult, op1=ALU.add,
                         r=["ss", "hmu", ("t1", c % 2)], w=["xj"])

            def lora(wq, K, func1, evac2, l1w=None, l2w=None):
                A1 = l1w if l1w is not None else l1[:, wq, :, :]
                pb = p1b[wq % 2]
                pkey = ("p1b", wq % 2)
                for kc in range(8):
                    P.op("pe", "matmul", banks[4][0:K, 0:256], A1[:, kc, 0:K], xj[:, kc, :], start=(kc == 0), stop=(kc == 7), r=["l1", "g1w", "v1w", "xj"], w=[("bk", 4)])
                P.op("act", "activation", out=pb[0:K, :], in_=banks[4][0:K, 0:256], func=func1, r=[("bk", 4)], w=[pkey])
                A2 = l2w if l2w is not None else l2[:, wq, :]
                for o2 in range(4):
                    bk = 5 + o2 % 2
                    pk = ("bk", bk)
                    for sub in range(2):
                        oc = 2 * o2 + sub
                        P.op("pe", "matmul", banks[bk][:, sub * 256:(sub + 1) * 256], A2[0:K, oc * 128:(oc + 1) * 128], pb[0:K, :], start=True, stop=True,
                             r=["l2", "g2w", "v2w", pkey], w=[pk])
                    evac2(banks[bk][:].rearrange("p (a b) -> p a b", a=2), 2 * o2, pk)

            def spill(buf, key, name, ch0):
                for q in range(2):
                    P.op("sp", "dma_start", out=scr[(i, name)][ch0 + q], in_=buf[:, :, q * 128:(q + 1) * 128], r=[key], w=[("scr", name, ch0 + q)], dma=True)

            wload(2)
            for (t0, w, lv, rv) in TILES256:
                ch0 = t0 // 128
                a = t0 - 1 if lv else t0
                b = t0 + 257 if rv else t0 + 256
                off = a - (t0 - 1)
                n = b - a
                emit_norm(a, n, lambda c: gsc[:, l, 0, c, w:w + 1], lambda c: modT[:, l, 0 + c, w:w + 1],
                          lambda c: hh[:, c, off:off + n], ntmp, banks[0], lambda c: ["hh"])
                if not lv:
                    P.op("dve", "memset", hh[:, :, 0:1], 0.0, w=["hh"])
                if not rv:
                    P.op("dve", "memset", hh[:, :, 257:258], 0.0, w=["hh"])
                P.op("dve", "tensor_tensor", ss, hh[:, :, 0:256], hh[:, :, 2:258], op=ALU.add, r=["hh"], w=["ss"])
                mix(1)
                for d in range(2):
                    def ev_w(bv_, oc0, pk, d=d):
                        for sub in range(2):
                            oc = oc0 + sub
                            P.op("act", "activation", out=sig[:, oc, :], in_=bv_[:, sub, :], func=AF.Sigmoid, bias=pvi(0 + d, oc), scale=1.0, r=[pk, "pv"], w=["sig"])
                    lora(d, 64, AF.Tanh, ev_w)
                    spill(sig, "sig", "sf" if d == 0 else "sb", ch0)
                mix(4)
                for d in range(2):
                    oa = o_af if d == 0 else o_ab
                    def ev_a(bv_, oc0, pk, d=d, oa=oa):
                        for sub in range(2):
                            oc = oc0 + sub
                            P.op("act", "activation", out=oa[:, oc, :], in_=bv_[:, sub, :], func=AF.Sigmoid, bias=pvi(2 + d, oc), scale=1.0, r=[pk, "pv"], w=[("oa", d)])
                    lora(2 + d, 64, AF.Identity, ev_a)
                    spill(oa, ("oa", d), "af" if d == 0 else "ab", ch0)
                P.op("dve", "tensor_tensor", tq, o_af, o_ab, op=ALU.add, r=[("oa", 0), ("oa", 1)], w=["tq"])
                for c in range(8):
                    P.op("dve", "tensor_scalar", tq[:, c, :], tq[:, c, :], pvi(6, c), tmka[:, i, c:c + 1], op0=ALU.mult, op1=ALU.add, r=["tq", "pv", "tmka"], w=["tq"])
                mix(5)
                def ev_g(bv_, oc0, pk):
                    P.op("act", "copy", o_g[:, oc0:oc0 + 2, :], bv_, r=[pk], w=["o_g"])
                lora(0, 128, AF.Sigmoid, ev_g, l1w=g1w, l2w=g2w)
                spill(o_g, "o_g", "g", ch0)
                mix(3)
                def ev_v(bv_, oc0, pk):
                    P.op("act", "copy", o_v[:, oc0:oc0 + 2, :], bv_, r=[pk], w=["o_v"])
                proj(2, ev_v)
                wload(0)
                if i == 1:
                    for q in range(2):
                        P.op("sp", "dma_start", out=vf[:, :, q * 128:(q + 1) * 128], in_=scr[(0, "v")][ch0 + q], w=["vf"], dma=True)
                    def ev_sv(bv_, oc0, pk):
                        for sub in range(2):
                            oc = oc0 + sub
                            P.op("act", "activation", out=sgv[:, oc, :], in_=bv_[:, sub, :], func=AF.Sigmoid, bias=pvi(4, oc), scale=1.0, r=[pk, "pv"], w=["o_bv"])
                    lora(1, 32, AF.Identity, ev_sv, l1w=v1w, l2w=v2w)
                    P.op("dve", "tensor_tensor", vf, vf, o_v, op=ALU.subtract, r=["vf", "o_v"], w=["vf"])
                    P.op("dve", "tensor_tensor", vf, vf, sgv, op=ALU.mult, r=["vf", "o_bv"], w=["vf"])
                    P.op("dve", "tensor_tensor", o_v, o_v, vf, op=ALU.add, r=["vf", "o_v"], w=["o_v"])
                spill(o_v, "o_v", "v", ch0)
                mix(0)
                def ev_r(bv_, oc0, pk):
                    P.op("act", "copy", o_r[:, oc0:oc0 + 2, :], bv_, r=[pk], w=["o_r"])
                proj(0, ev_r)
                wload(1)
                spill(o_r, "o_r", "r", ch0)
                mix(2)
                def ev_k(bv_, oc0, pk):
                    P.op("act", "copy", o_k[:, oc0:oc0 + 2, :], bv_, r=[pk], w=["o_k"])
                proj(1, ev_k)
                wload(2)
                spill(o_k, "o_k", "k", ch0)
                for c in range(8):
                    P.op("dve", "tensor_scalar", o_kk[:, c, :], o_k[:, c, :], pvi(5, c), None, op0=ALU.mult, r=["o_k", "pv"], w=["o_kk"])
                    sqb = t1[c % 2]
                    P.op("act", "activation", out=sqb, in_=o_kk[:, c, :], func=AF.Square, r=["o_kk"], w=[("t1", c % 2)])
                    P.op("pe", "matmul", banks[7][:, 0:256], onesbd[:], sqb, start=True, stop=True, r=[("t1", c % 2), "onesbd"], w=[("bk", 7)])
                    P.op("act", "activation", out=tmpa, in_=banks[7][:, 0:256], func=AF.Sqrt, bias=epsk[:, 0:1], scale=1.0, r=[("bk", 7), "epsk"], w=["tmpa"])
                    P.op("dve", "reciprocal", tmpb, tmpa, r=["tmpa"], w=["tmpb"])
                    P.op("dve", "tensor_tensor", o_kk[:, c, :], o_kk[:, c, :], tmpb, op=ALU.mult, r=["o_kk", "tmpb"], w=["o_kk"])
                spill(o_kk, "o_kk", "kk", ch0)
                for c in range(8):
                    bq = t1[c % 2]
                    P.op("dve", "scalar_tensor_tensor", tmpa, o_r[:, c, :], pvi(7, c), o_k[:, c, :], op0=ALU.mult, op1=ALU.mult, r=["o_r", "o_k", "pv"], w=["tmpa"])
                    P.op("dve", "tensor_tensor", bq, tmpa, tq[:, c, :], op=ALU.mult, r=["tmpa", "tq"], w=[("t1", c % 2)])
                    P.op("pe", "matmul", banks[7][:, 256:512], onesbd[:], bq, start=True, stop=True, r=[("t1", c % 2), "onesbd"], w=[("bk7b",)])
                    P.op("dve", "tensor_tensor", o_bv[:, c, :], banks[7][:, 256:512], o_v[:, c, :], op=ALU.mult, r=[("bk7b",), "o_v"], w=["o_bv"])
                spill(o_bv, "o_bv", "bv", ch0)

            _rwstop = os.environ.get("KRW", "")
            for d in range(2):
                if _rwstop == "proj" or (_rwstop == "fwd" and d == 1):
                    break
                P.barrier()
                AR.reset()
                ld = {}
                for nm in ("r", "k", "v", "kk", "a"):
                    ld[nm] = [AR.bf16(8, 128)] * 2
                ld["s"] = [AR.f32(8, 128)] * 2
                pinc = AR.f32(8, 128, name="pinc")
                eLi_f = AR.f32(1024, name="eLi")
                emL_f = AR.f32(1024, name="emL")
                eLe_f = AR.f32(1024, name="eLe")
                eLi = eLi_f.rearrange("p (a b) -> p a b", a=8)
                emL = emL_f.rearrange("p (a b) -> p a b", a=8)
                eLe = eLe_f.rearrange("p (a b) -> p a b", a=8)
                PC = AR.f32(8, name="PC")
                bPN = AR.f32(2, 8)
                ar = AR.bf16(8, 256, name="ar")
                bt = AR.bf16(8, 128, name="bt")
                kt = AR.bf16(8, 128, name="kt")
                ka = AR.bf16(8, 128)
                tqs = AR.bf16(8, 128)
                Vtok = AR.bf16(1024, name="Vtok")
                Ktok = AR.bf16(1024, name="Ktok")
                Btok = AR.bf16(1024, name="Btok")
                msk = [AR.bf16(2, 512) for _ in range(4)]
                DDb = [[AR.bf16(2, 2, 128) for _ in range(2)] for _ in range(4)]
                Eb = [AR.bf16(2, 128) for _ in range(4)]
                Dt7 = [AR.bf16(2, 128) for _ in range(4)]
                Xb = [AR.bf16(128) for _ in range(4)]
                Ub = [AR.bf16(128) for _ in range(4)]
                pad = [AR.bf16(4, 2, 128) for _ in range(4)]
                ident2 = bass.AP(ident[:].tensor, ident[:].offset, [list(ident[:].ap[0]), [0, 2], [1, 128]])

                def blk_ap(dd, lev):
                    v_ = blkm[:, dd, lev, :]
                    return bass.AP(v_.tensor, v_.offset, [list(v_.ap[0]), [0, 2], [1, 128]])
                if d == 0:
                    yfb = [AR.bf16(1024) for _ in range(2)]
                else:
                    yfl = [AR.bf16(1024)] * 2
                    ysum = eLe_f
                    ysq = eLi_f
                    ynb = tqs.rearrange("p a b -> p (a b)")
                    yo1 = emL
                    yob = ka
                    ldg = [AR.bf16(8, 128)] * 2
                    ldbv = [AR.bf16(8, 128)] * 2
                    Wo = AR.bf16(8, 1024)
                    st16 = AR.f32(6, 16)
                    for hb in range(2):
                        P.op("pool", "dma_start", out=Wo[:, :, hb * 512:(hb + 1) * 512], in_=d_rw[i, 3, :, :, hb * 512:(hb + 1) * 512], w=["Wo"], dma=True)
                aname = "af" if d == 0 else "ab"
                sname = "sf" if d == 0 else "sb"
                order = []
                for (c0, ncx, kind, idx) in SEQS:
                    chs = list(range(c0, c0 + ncx))
                    if d == 1:
                        chs = chs[::-1]
                    for k_, ch in enumerate(chs):
                        order.append((ch, kind, idx, k_ == 0, k_ == ncx - 1))

                def loads(n_):
                    ch = order[n_][0]
                    sl = n_ % 2
                    for nm, sn in (("r", "r"), ("k", "k"), ("v", "v"), ("kk", "kk"), ("a", aname), ("s", sname)):
                        P.op("sp", "dma_start", out=ld[nm][sl], in_=scr[(i, sn)][ch], w=[("ld", nm, 0)], dma=True)

                loads(0)
                for n_, (ch, kind, idx, first, last) in enumerate(order):
                    sl = n_ % 2
                    if d == 1:
                        P.op("sp", "dma_start", out=yfl[0], in_=scr[(i, "yf")][ch], w=[("yfl", 0)], dma=True)
                        P.op("sp", "dma_start", out=ldg[0], in_=scr[(i, "g")][ch], w=[("ldg", 0)], dma=True)
                        P.op("sp", "dma_start", out=ldbv[0], in_=scr[(i, "bv")][ch], w=[("ldbv", 0)], dma=True)
                    L = {nm: ld[nm][sl] for nm in ld}
                    lk = lambda nm: ("ld", nm, 0)
                    if first:
                        if kind == "s":
                            P.op("sp", "dma_start", out=Hst[:], in_=d_state[:, i, d], w=[("Hst", c_) for c_ in range(8)], dma=True)
                        else:
                            P.op("dve", "memset", Hst[:], 0.0, w=[("Hst", c_) for c_ in range(8)])
                        P.op("act", "copy", Hb[:], Hst[:], r=[("Hst", c_) for c_ in range(8)], w=[("Hb", c_) for c_ in range(8)])
                    _ks = os.environ.get("KSCAN", "")
                    if _ks == "load":
                        continue
                    for c in range(8):
                        P.op("dve", "tensor_tensor_scan", out=pinc[:, c, :], data0=onesf[:], data1=L["s"][:, c, :], initial=0.0, op0=ALU.mult, op1=ALU.add,
                             r=[lk("s"), "onesf"], w=["pinc"])
                    pexc = L["s"]
                    P.op("dve", "tensor_tensor", pexc, pinc, L["s"], op=ALU.subtract, r=["pinc", lk("s")], w=[lk("s")])
                    P.op("act", "activation", out=PC, in_=pinc[:, :, 127], func=AF.Exp, scale=NEGC, r=["pinc"], w=["PC"])
                    if d == 0:
                        P.op("act", "activation", out=eLi, in_=pinc, func=AF.Exp, scale=NEGC, r=["pinc"], w=["eLi"])
                        P.op("act", "activation", out=emL, in_=pinc, func=AF.Exp, scale=-NEGC, r=["pinc"], w=["emL"])
                        P.op("act", "activation", out=eLe, in_=pexc, func=AF.Exp, scale=NEGC, r=[lk("s")], w=["eLe"])
                    else:
                        P.op("dve", "tensor_scalar", bPN[:, 0, :], pinc[:, :, 127], NEGC, None, op0=ALU.mult, r=["pinc"], w=["bPN"])
                        P.op("dve", "tensor_scalar", bPN[:, 1, :], pinc[:, :, 127], -NEGC, None, op0=ALU.mult, r=["pinc"], w=["bPN"])
                        for c in range(8):
                            P.op("act", "activation", out=eLi[:, c, :], in_=pexc[:, c, :], func=AF.Exp, scale=-NEGC, bias=bPN[:, 0, c:c + 1], r=[lk("s"), "bPN"], w=["eLi"])
                            P.op("act", "activation", out=emL[:, c, :], in_=pexc[:, c, :], func=AF.Exp, scale=NEGC, bias=bPN[:, 1, c:c + 1], r=[lk("s"), "bPN"], w=["emL"])
                            P.op("act", "activation", out=eLe[:, c, :], in_=pinc[:, c, :], func=AF.Exp, scale=-NEGC, bias=bPN[:, 0, c:c + 1], r=["pinc", "bPN"], w=["eLe"])
                    P.op("dve", "tensor_tensor", ar[:, :, 128:256], L["r"], eLi, op=ALU.mult, r=[lk("r"), "eLi"], w=["ar"])
                    P.op("dve", "scalar_tensor_tensor", ar[:, :, 0:128], L["kk"], -1.0, eLe, op0=ALU.mult, op1=ALU.mult, r=[lk("kk"), "eLe"], w=["ar"])
                    P.op("dve", "tensor_tensor", ka, L["kk"], L["a"], op=ALU.mult, r=[lk("kk"), lk("a")], w=["ka"])
                    P.op("dve", "tensor_tensor", bt, ka, emL, op=ALU.mult, r=["ka", "emL"], w=["bt"])
                    P.op("dve", "tensor_tensor", tqs, L["a"], bcast_free(pv[:, i, 6, :], 128), op=ALU.mult, r=[lk("a"), "pv"], w=["tqs"])
                    P.op("dve", "tensor_tensor", tqs, tqs, bcast_free(omka[:, i, :], 128), op=ALU.add, r=["tqs", "omka"], w=["tqs"])
                    P.op("dve", "tensor_tensor", ka, L["k"], tqs, op=ALU.mult, r=[lk("k"), "tqs", "bt"], w=["ka"])
                    P.op("dve", "tensor_tensor", kt, ka, emL, op=ALU.mult, r=["ka", "emL"], w=["kt"])
                    if _ks == "prep":
                        continue
                    for c in range(8):
                        P.op("pe", "transpose", bankbf[6][:, c * 128:(c + 1) * 128], L["v"][:, c, :], ident[:], r=[lk("v"), "ident"], w=[("bk", 6)])
                    P.op("act", "copy", Vtok, bankbf[6], r=[("bk", 6)], w=["Vtok"])
                    for c in range(8):
                        P.op("pe", "transpose", bankbf[7][:, c * 128:(c + 1) * 128], kt[:, c, :], ident[:], r=["kt", "ident"], w=[("bk", 7)])
                    P.op("dve", "tensor_copy", Ktok, bankbf[7], r=[("bk", 7)], w=["Ktok"])
                    for c in range(8):
                        P.op("pe", "transpose", bankbf[6][:, c * 128:(c + 1) * 128], bt[:, c, :], ident[:], r=["bt", "ident"], w=[("bk", 6)])
                    P.op("act", "copy", Btok, bankbf[6], r=[("bk", 6)], w=["Btok"])

                    if n_ + 1 < len(order):
                        loads(n_ + 1)
                    def unit(c, s):
                        A, B = banks[2 * s], banks[2 * s + 1]
                        kA, kB = ("bk", 2 * s), ("bk", 2 * s + 1)
                        kmsk, kUb, kEb, kXb, kDt7, kpad = ("msk", s), ("Ub", s), ("Eb", s), ("Xb", s), ("Dt7", s), ("pad", s)
                        pd = pad[s]
                        hmk = ["hm00", "hm10", "hm01", "hm11"]
                        for he in range(2):
                            P.op("act", "activation", out=pd[:, 0, he, :], in_=bt[:, c, :], func=AF.Identity, scale=hm[:, he:he + 1], r=["bt"] + hmk, w=[kpad])
                            P.op("act", "activation", out=pd[:, 1, he, :], in_=kt[:, c, :], func=AF.Identity, scale=hm[:, he:he + 1], r=["kt"] + hmk, w=[kpad])
                            P.op("act", "activation", out=pd[:, 2, he, :], in_=ar[:, c, 0:128], func=AF.Identity, scale=hm[:, he:he + 1], r=["ar"] + hmk, w=[kpad])
                            P.op("act", "activation", out=pd[:, 3, he, :], in_=ar[:, c, 128:256], func=AF.Identity, scale=hm[:, he:he + 1], r=["ar"] + hmk, w=[kpad])
                        yield
                        for he, bank, kb in ((0, A, kA), (1, B, kB)):
                            P.op("pe", "matmul", bank[:, 0:256], pd[:, 0, he, :], ar[:, c, :], start=True, stop=True, r=[kpad, "ar"], w=[kb])
                            P.op("pe", "matmul", bank[:, 256:512], pd[:, 1, he, :], ar[:, c, :], start=True, stop=True, r=[kpad, "ar"], w=[kb])
                        yield
                        P.op("dve", "tensor_tensor", msk[s][:, 0, :], A[:], maska[:, d, :], op=ALU.mult, r=[kA, "maska"], w=[kmsk])
                        P.op("dve", "tensor_tensor", msk[s][:, 1, :], B[:], maska[:, d, :], op=ALU.mult, r=[kB, "maska"], w=[kmsk])
                        yield
                        for he in range(2):
                            b0 = he * 64
                            P.op("pe", "matmul", A[:, 256 + he * 64:256 + (he + 1) * 64], pd[:, 2, he, :], Hb[:, c, :], start=(he == 0), stop=False,
                                 skip_group_check=True, r=[kpad, ("Hb", c), kmsk], w=[kA])
                            P.op("pe", "matmul", A[:, 256 + he * 64:256 + (he + 1) * 64], msk[s][:, he, 256:384], Vtok[:, c * 128 + b0:c * 128 + b0 + 64], start=False, stop=(he == 1),
                                 skip_group_check=True, r=[kmsk, "Vtok"], w=[kA])
                        yield
                        P.op("act", "copy", Xb[s], A[:, 256:384], r=[kA], w=[kXb])
                        yield
                        curD = [ident[:], ident[:]]
                        curT = [ident[:], ident[:]]
                        curv = bass.AP(ident[:].tensor, ident[:].offset, [list(ident[:].ap[0]), [0, 2], [0, 2], [1, 128]])
                        kcur = "ident"
                        for lev in range(7):
                            for he in range(2):
                                P.op("pe", "matmul", A[:, he * 128:(he + 1) * 128], msk[s][:, he, 0:128], curD[he], start=True, stop=True, r=[kmsk, kcur], w=[kA])
                            yield
                            P.op("dve", "tensor_tensor", Eb[s], A[:, 0:256].rearrange("p (a b) -> p a b", a=2), blk_ap(d, lev), op=ALU.mult, r=[kA, "blk"], w=[kEb])
                            yield
                            if lev < 6:
                                nxt = DDb[s][lev % 2]
                                knxt = ("DD", s, lev % 2)
                                for he in range(2):
                                    P.op("pe", "matmul", B[:, he * 256:he * 256 + 128], curT[he], Eb[s][:, he, :], start=True, stop=True, r=[kcur, kEb], w=[kB])
                                    P.op("pe", "matmul", B[:, he * 256 + 128:he * 256 + 256], Eb[s][:, he, :], curT[he], start=True, stop=True, r=[kcur, kEb], w=[kB])
                                yield
                                P.op("dve", "tensor_tensor", nxt, B[:].rearrange("p (a b c) -> p a b c", a=2, b=2), curv, op=ALU.add, r=[kB, kcur], w=[knxt])
                                curD = [nxt[:, he, 0, :] for he in range(2)]
                                curT = [nxt[:, he, 1, :] for he in range(2)]
                                curv = nxt
                                kcur = knxt
                            else:
                                for he in range(2):
                                    P.op("pe", "matmul", B[:, he * 128:(he + 1) * 128], Eb[s][:, he, :], curT[he], start=True, stop=True, r=[kcur, kEb], w=[kB])
                                yield
                                P.op("dve", "tensor_tensor", Dt7[s], B[:, 0:256].rearrange("p (a b) -> p a b", a=2), curv[:, :, 1, :], op=ALU.add, r=[kB, kcur], w=[kDt7])
                            yield
                        for he in range(2):
                            P.op("pe", "matmul", A[:, 256 + he * 64:256 + (he + 1) * 64], Dt7[s][:, he, :], Xb[s][:, he * 64:(he + 1) * 64], start=True, stop=True,
                                 r=[kDt7, kXb], w=[kA])
                        yield
                        P.op("act", "copy", Ub[s], A[:, 256:384], r=[kA], w=[kUb])
                        yield
                        for he in range(2):
                            b0 = he * 64
                            yo_ = A[:, 384 + he * 64:384 + (he + 1) * 64]
                            P.op("pe", "matmul", yo_, pd[:, 3, he, :], Hb[:, c, :], start=(he == 0), stop=False, r=[kpad, ("Hb", c), kUb], w=[kA])
                            P.op("pe", "matmul", yo_, msk[s][:, he, 128:256], Ub[s][:, b0:b0 + 64], start=False, stop=False, r=[kmsk, kUb], w=[kA])
                            P.op("pe", "matmul", yo_, msk[s][:, he, 384:512], Vtok[:, c * 128 + b0:c * 128 + b0 + 64], start=False, stop=(he == 1), r=[kmsk, "Vtok"], w=[kA])
                        for he in range(2):
                            b0 = he * 64
                            ph = A[b0:b0 + 64, 0:64]
                            P.op("pe", "matmul", ph, Btok[:, c * 128 + b0:c * 128 + b0 + 64], Ub[s][:, b0:b0 + 64], start=True, stop=False, tile_position=(0, b0),
                                 r=["Btok", kUb], w=[kA])
                            P.op("pe", "matmul", ph, Ktok[:, c * 128 + b0:c * 128 + b0 + 64], Vtok[:, c * 128 + b0:c * 128 + b0 + 64], start=False, stop=True, tile_position=(0, b0),
                                 r=["Ktok", "Vtok"], w=[kA])
                        yield
                        if d == 0:
                            P.op("dve", "tensor_copy", yfb[n_ % 2][:, c * 128:(c + 1) * 128], A[:, 384:512], r=[kA], w=[("yfb", n_ % 2)])
                        else:
                            P.op("dve", "tensor_tensor", ysum[:, c * 128:(c + 1) * 128], A[:, 384:512], yfl[0][:, c * 128:(c + 1) * 128], op=ALU.add,
                                 r=[kA, ("yfl", 0)], w=["eLe"])
                        P.op("dve", "tensor_tensor", Hst[:, c, :], A[:, 0:64], Hst[:, c, :], op=ALU.add, r=[kA, ("Hst", c)], w=[("Hst", c)])
                        P.op("dve", "tensor_scalar", Hst[:, c, :], Hst[:, c, :], PC[:, c:c + 1], None, op0=ALU.mult, r=[("Hst", c), "PC"], w=[("Hst", c)])
                        P.op("act", "copy", Hb[:, c, :], Hst[:, c, :], r=[("Hst", c)], w=[("Hb", c)])
                        yield

                    for cg in range(2):
                        gens = [unit(4 * cg + q, q) for q in range(4)]
                        alive = [True] * 4
                        while any(alive):
                            for gi, g_ in enumerate(gens):
                                if alive[gi]:
                                    try:
                                        next(g_)
                                    except StopIteration:
                                        alive[gi] = False

                    if d == 0:
                        P.op("sp", "dma_start", out=scr[(i, "yf")][ch], in_=yfb[n_ % 2], r=[("yfb", n_ % 2)], w=[("scr", "yf", ch)], dma=True)
                    else:
                        w = 0 if ch < 16 else 1
                        col0 = ch * 128
                        y3 = ysum.rearrange("p (h n) -> p h n", n=64)
                        P.op("dve", "tensor_reduce", out=st16[:, 0, :], in_=y3, axis=AX.X, op=ALU.add, r=["eLe"], w=["st16a"])
                        P.op("act", "activation", out=ysq, in_=ysum, func=AF.Square, r=["eLe"], w=["eLi"])
                        P.op("dve", "tensor_reduce", out=st16[:, 1, :], in_=ysq.rearrange("p (h n) -> p h n", n=64), axis=AX.X, op=ALU.add, r=["eLi"], w=["st16b"])
                        P.op("dve", "tensor_scalar", st16[:, 2, :], st16[:, 0, :], 1.0 / 64.0, None, op0=ALU.mult, r=["st16a"], w=["st16c"])
                        P.op("dve", "tensor_tensor", st16[:, 3, :], st16[:, 2, :], st16[:, 2, :], op=ALU.mult, r=["st16c"], w=["st16d"])
                        P.op("dve", "scalar_tensor_tensor", st16[:, 4, :], st16[:, 1, :], 1.0 / 64.0, st16[:, 3, :], op0=ALU.mult, op1=ALU.subtract,
                             r=["st16b", "st16d"], w=["st16e"])
                        P.op("act", "activation", out=st16[:, 5, :], in_=st16[:, 4, :], func=AF.Sqrt, bias=epsg[:, 0:1], scale=1.0, r=["st16e", "epsg"], w=["st16f"])
                        P.op("dve", "reciprocal", st16[:, 4, :], st16[:, 5, :], r=["st16f"], w=["st16g"])
                        P.op("dve", "tensor_tensor", y3, y3, bcast_free(st16[:, 2, :], 64), op=ALU.subtract, r=["eLe", "st16c"], w=["eLe"])
                        P.op("dve", "tensor_tensor", ynb.rearrange("p (h n) -> p h n", n=64), y3, bcast_free(st16[:, 4, :], 64), op=ALU.mult, r=["eLe", "st16g"], w=["tqs"])
                        for c in range(8):
                            P.op("pe", "transpose", bankbf[6][:, c * 128:(c + 1) * 128], ynb[:, c * 128:(c + 1) * 128], ident[:], r=["tqs", "ident"], w=[("bk", 6)])
                        for c in range(8):
                            P.op("act", "activation", out=yo1[:, c, :], in_=bankbf[6][:, c * 128:(c + 1) * 128], func=AF.Identity, scale=pvi(8, c), bias=pvi(9, c),
                                 r=[("bk", 6), "pv"], w=["emL"])
                        P.op("dve", "tensor_tensor", yo1, yo1, ldbv[sl], op=ALU.add, r=["emL", ("ldbv", 0)], w=["emL"])
                        P.op("dve", "tensor_tensor", yob, yo1, ldg[sl], op=ALU.mult, r=["emL", ("ldg", 0)], w=["ka"])
                        for oc in range(8):
                            bk = oc // 4
                            for kc in range(8):
                                P.op("pe", "matmul", banks[bk][:, (oc % 4) * 128:(oc % 4 + 1) * 128], Wo[:, kc, oc * 128:(oc + 1) * 128], yob[:, kc, :],
                                     start=(kc == 0), stop=(kc == 7), r=["Wo", "ka"], w=[("bk", bk)])
                        for oc in range(8):
                            bk = oc // 4
                            P.op("dve", "scalar_tensor_tensor", x_sb[:, oc, col0:col0 + 128], banks[bk][:, (oc % 4) * 128:(oc % 4 + 1) * 128], modT[:, l, 16 + oc, w:w + 1],
                                 x_sb[:, oc, col0:col0 + 128], op0=ALU.mult, op1=ALU.add, r=[("bk", bk), "modT"] + xkeys(col0, 128), w=xkeys(col0, 128))
                    if last and kind == "p":
                        P.op("sp", "dma_start", out=d_ns[idx, i, d], in_=Hst[:], r=[("Hst", c_) for c_ in range(8)], w=[("ns", idx, i, d)], dma=True)

        for l in range(depth):
            if l % 2 == 0:
                sgu_layer(l)
            else:
                rwkv_layer(l)
            mlp_layer(l)

        P.barrier()
        AR.reset()
        ntmp = norm_tmp(512)
        yo = [AR.f32(8, 512) for _ in range(2)]
        for ti, (t0, w) in enumerate(TILES512):
            yb = yo[ti % 2]
            emit_norm(t0, 512, lambda c: fg[:, c:c + 1], None, lambda c: yb[:, c, :], ntmp, banks[0], lambda c: [("yo", ti % 2)])
            P.op("sp", "dma_start", out=d_y[:, :, t0:t0 + 512], in_=yb, r=[("yo", ti % 2)], w=[("dy", ti)], dma=True)
        if n_rw == 0:
            pass

        P.finalize()
        sems = {k: es.enter_context(nc.semaphore("s_%s_%d" % k)) for k in sorted(P.sem_keys)}
        block = es.enter_context(nc.Block())
        P.emit(sems, block)
    return nc


def _pc(a):
    a = np.asarray(a, np.float32)
    lead = a.shape[:-1]
    b = a.reshape(lead + (8, 128))
    return np.ascontiguousarray(np.moveaxis(b, -1, 0))


def _kmajor(w):
    w = np.asarray(w, np.float32)
    K, N = w.shape[-2], w.shape[-1]
    lead = w.shape[:-2]
    b = w.reshape(lead + (K // 128, 128, N))
    return np.ascontiguousarray(np.swapaxes(b, -3, -2))


_NC_CACHE = {}
_RUNNER = [None]


def kernel(x_prompt, x_sample, state_wkv, c, c_ctx, norm1_g, norm2_g, ada_w, ada_b,
           sgu_w_in, sgu_ln_g, sgu_ln_b, sgu_w_s, sgu_b_s, sgu_w_out,
           rwkv_mu, rwkv_w_r, rwkv_w_k, rwkv_w_v, rwkv_w_o, rwkv_w0, rwkv_w1, rwkv_w2,
           rwkv_a0, rwkv_a1, rwkv_a2, rwkv_v0, rwkv_v1, rwkv_v2, rwkv_g1, rwkv_g2,
           rwkv_k_k, rwkv_k_a, rwkv_r_k, rwkv_ln_g, rwkv_ln_b, mlp_w1, mlp_w2, final_g, _depth=DEPTH):
    f = lambda a: np.asarray(a, np.float32)
    depth = _depth
    shared = {}
    shared["ident"] = np.eye(128, dtype=np.float32)
    bd = np.zeros((128, 128), np.float32); bd[:64, :64] = 1; bd[64:, 64:] = 1
    shared["onesbd"] = bd
    s_i = np.arange(128)[:, None]; t_i = np.arange(128)[None, :]
    su = (s_i < t_i).astype(np.float32); iu = (s_i <= t_i).astype(np.float32)
    sl = (s_i > t_i).astype(np.float32); il = (s_i >= t_i).astype(np.float32)
    shared["maska"] = np.stack([np.concatenate([su, iu, su, iu], 1), np.concatenate([sl, il, sl, il], 1)])
    shared["maskb"] = np.stack([np.concatenate([sl, sl], 1), np.concatenate([su, su], 1)])
    bl = []
    for lev in range(7):
        bs = 2 ** lev
        bl.append(((s_i // (2 * bs) == t_i // (2 * bs)) & ((s_i // bs) % 2 == 1) & ((t_i // bs) % 2 == 0)).astype(np.float32))
    bl = np.stack(bl, axis=1)
    shared["blkm"] = np.ascontiguousarray(np.stack([bl, bl.transpose(2, 1, 0)]))
    shared["ada_w"] = _kmajor(f(ada_w))
    shared["ada_b"] = np.ascontiguousarray(f(ada_b).reshape(4, 48, 128).transpose(2, 0, 1))
    shared["norm_g"] = np.ascontiguousarray(np.stack([_pc(norm1_g), _pc(norm2_g)], axis=2))
    shared["final_g"] = _pc(final_g)
    shared["sgu_w_in"] = _kmajor(f(sgu_w_in))
    shared["sgu_w_out"] = _kmajor(f(sgu_w_out))
    shared["sgu_wsT"] = np.ascontiguousarray(f(sgu_w_s).transpose(0, 3, 1, 2))
    shared["sgu_bs"] = np.ascontiguousarray(f(sgu_b_s).reshape(2, 1, 2048))
    shared["sgu_ln"] = np.ascontiguousarray(np.stack([f(sgu_ln_g), f(sgu_ln_b)], axis=1))
    w1 = f(mlp_w1).reshape(4, 8, 128, 8, 512)
    shared["mlp_w1"] = np.ascontiguousarray(w1.transpose(0, 3, 2, 1, 4))
    w2 = f(mlp_w2).reshape(4, 8, 4, 128, 1024)
    shared["mlp_w2"] = np.ascontiguousarray(w2.transpose(0, 1, 3, 2, 4))
    shared["rwkv_w"] = np.ascontiguousarray(np.stack([_kmajor(f(rwkv_w_r)), _kmajor(f(rwkv_w_k)), _kmajor(f(rwkv_w_v)), _kmajor(f(rwkv_w_o))], axis=1))
    shared["rwkv_mu"] = _pc(rwkv_mu)
    shared["rwkv_l1"] = np.ascontiguousarray(np.concatenate([_kmajor(f(rwkv_w1)), _kmajor(f(rwkv_a1))], axis=1))
    shared["rwkv_l2"] = np.ascontiguousarray(np.concatenate([f(rwkv_w2), f(rwkv_a2)], axis=1))
    shared["rwkv_g1"] = _kmajor(f(rwkv_g1))
    shared["rwkv_g2"] = np.ascontiguousarray(f(rwkv_g2))
    shared["rwkv_v1"] = _kmajor(f(rwkv_v1))[0]
    shared["rwkv_v2"] = np.ascontiguousarray(f(rwkv_v2)[0])
    v0 = np.broadcast_to(f(rwkv_v0).reshape(1, 1024), (2, 1024))
    zz = np.zeros((2, 1024), np.float32)
    pvl = [f(rwkv_w0)[:, 0], f(rwkv_w0)[:, 1], f(rwkv_a0)[:, 0], f(rwkv_a0)[:, 1], v0, f(rwkv_k_k), f(rwkv_k_a),
           f(rwkv_r_k).reshape(2, 1024), f(rwkv_ln_g), f(rwkv_ln_b), zz]
    shared["rwkv_pv"] = _pc(np.stack(pvl, axis=1))

    in_maps = []
    xs = f(x_sample); xp = f(x_prompt); st = f(state_wkv)
    for b in range(NCORES):
        m = dict(shared)
        xt = np.concatenate([xs[b], xp[2 * b], xp[2 * b + 1]], axis=0)
        m["xT"] = np.ascontiguousarray(xt.reshape(NT, 8, 128).transpose(2, 1, 0))
        cond = np.stack([f(c)[b], f(c_ctx)], axis=-1)
        m["condT"] = np.ascontiguousarray(cond.reshape(8, 128, 2).transpose(1, 0, 2))
        s = st[b].reshape(2, 2, 8, 2, 64, 64)
        m["state0"] = np.ascontiguousarray(s.transpose(3, 5, 0, 1, 2, 4).reshape(128, 2, 2, 8, 64))
        in_maps.append(m)

    if depth not in _NC_CACHE:
        _NC_CACHE[depth] = build(depth)
    nc = _NC_CACHE[depth]
    if _RUNNER[0] is not None:
        res = _RUNNER[0](nc, in_maps)
    else:
        ncr = int(os.environ.get("KCORES", NCORES))
        res = run_bass_kernel_spmd(nc, in_maps[:ncr], core_ids=list(range(ncr)))
        if ncr < NCORES:
            res.results.extend([res.results[0]] * (NCORES - ncr))
    y_prompt = np.zeros((16, 256, D), np.float32)
    y_sample = np.zeros((8, 2048, D), np.float32)
    n_rw = depth // 2
    new_state = np.zeros((16, max(n_rw, 1), 2, 16, 64, 64), np.float32)
    for b in range(NCORES):
        r = res.results[b]
        yt = np.asarray(r["yT"]).transpose(2, 1, 0).reshape(NT, D)
        y_sample[b] = yt[:2048]
        y_prompt[2 * b] = yt[2048:2304]
        y_prompt[2 * b + 1] = yt[2304:2560]
        ns = np.asarray(r["new_state"])
        ns = ns.reshape(2, 2, 2, 2, 64, 8, 64)
        ns = ns.transpose(0, 1, 2, 5, 3, 6, 4).reshape(2, 2, 2, 16, 64, 64)
        for q in range(2):
            new_state[2 * b + q, :n_rw] = ns[q, :n_rw]
    if n_rw == 0:
        new_state = new_state[:, :0]
    return (y_prompt, y_sample, new_state)
```

```python
import os
from contextlib import ExitStack
import numpy as np
import concourse.bass as bass
import concourse.mybir as mybir
from concourse.bass_utils import run_bass_kernel_spmd

F32 = mybir.dt.float32
BF16 = mybir.dt.bfloat16
AF = mybir.ActivationFunctionType
ALU = mybir.AluOpType
AX = mybir.AxisListType

NCORES = 8
D = 1024
NT = 2560
DEPTH = 4
NORM_EPS = 1e-6
GN_EPS = 64e-5
NEGC = -0.6065306597126334
NDMASEM = 8
_DBG = {}

class Op:
    __slots__ = ("eng", "fn", "deps", "sig", "waits", "is_dma", "dsem", "n", "prev_dma")

    def __init__(self, eng, fn, is_dma):
        self.eng = eng; self.fn = fn; self.deps = []; self.sig = None; self.waits = []
        self.is_dma = is_dma; self.dsem = None; self.n = 0; self.prev_dma = None


class Prog:
    ENGS = ("pe", "act", "dve", "pool", "sp")

    def __init__(self):
        self.ops = []
        self.last_w = {}
        self.readers = {}
        self.bar = []
        self.last_eng = {}
        self.last_dma = {}
        self.dcnt = {e: 0 for e in self.ENGS}

    def barrier(self):
        self.bar = list(self.last_eng.values()) + list(self.last_dma.values())
        self.last_w = {}
        self.readers = {}

    def op(self, eng, name, *args, r=(), w=(), dma=False, **kw):
        o = Op(eng, (name, args, kw), dma)
        o.n = len(self.ops)
        deps = {}
        for k in r:
            p = self.last_w.get(k)
            if p is not None:
                deps[p.n] = (p, True)
        for k in w:
            p = self.last_w.get(k)
            if p is not None and p.n not in deps:
                deps[p.n] = (p, True)
            for q in self.readers.get(k, ()):
                if q.n not in deps:
                    deps[q.n] = (q, False)
        for k in r:
            self.readers.setdefault(k, []).append(o)
        for k in w:
            self.last_w[k] = o
            self.readers[k] = []
        for p, raw in deps.values():
            if p.is_dma:
                o.deps.append(p)
            elif p.eng == o.eng and not o.is_dma:
                if raw and p.eng != "pe":
                    o.deps.append(p)
            else:
                o.deps.append(p)
        for p in self.bar:
            if p.is_dma or p.eng != o.eng or o.is_dma:
                o.deps.append(p)
        if dma:
            i = self.dcnt[eng]; self.dcnt[eng] += 1
            key = (eng, i % NDMASEM)
            o.prev_dma = self.last_dma.get(key)
            self.last_dma[key] = o
            o.dsem = key
        else:
            self.last_eng[eng] = o
        self.ops.append(o)
        return o

    def finalize(self):
        need = set()
        for o in self.ops:
            for p in o.deps:
                if not p.is_dma:
                    need.add(p.n)
        cnt = {e: 0 for e in self.ENGS}
        dval = {}
        for o in self.ops:
            if o.is_dma:
                key = o.dsem
                v = dval.get(key, 0) + 16
                dval[key] = v
                o.dsem = (key, v)
            elif o.n in need:
                cnt[o.eng] += 1
                o.sig = (("c" + o.eng, 0), cnt[o.eng])
        self.dma_final = dval
        waited = {e: {} for e in self.ENGS}
        for o in self.ops:
            wl = {}
            if o.is_dma and o.prev_dma is not None:
                k, v = o.prev_dma.dsem
                wl[k] = v
            for p in o.deps:
                k, v = p.dsem if p.is_dma else p.sig
                if wl.get(k, 0) < v:
                    wl[k] = v
            wd = waited[o.eng]
            for k, v in wl.items():
                if wd.get(k, 0) < v:
                    wd[k] = v
                    o.waits.append((k, v))
        self.sem_keys = set(dval.keys())
        for e in self.ENGS:
            if cnt[e]:
                self.sem_keys.add(("c" + e, 0))

    def emit(self, sems, block):
        byeng = {e: [o for o in self.ops if o.eng == e] for e in self.ENGS}
        finals = self.dma_final

        def run(eng_obj, lst, fk):
            for o in lst:
                for k, v in o.waits:
                    eng_obj.wait_ge(sems[k], v)
                name, args, kw = o.fn
                ins = getattr(eng_obj, name)(*args, **kw)
                if o.is_dma:
                    ins.then_inc(sems[o.dsem[0]], 16)
                elif o.sig is not None:
                    ins.then_inc(sems[o.sig[0]], 1)
            for k in fk:
                eng_obj.wait_ge(sems[k], finals[k])

        @block.tensor
        def _(e):
            run(e, byeng["pe"], [])

        @block.scalar
        def _(e):
            run(e, byeng["act"], [k for k in finals if k[0] == "act"])

        @block.vector
        def _(e):
            run(e, byeng["dve"], [])

        @block.gpsimd
        def _(e):
            run(e, byeng["pool"], [k for k in finals if k[0] == "pool"])

        @block.sync
        def _(e):
            run(e, byeng["sp"], [k for k in finals if k[0] == "sp"])


class Arena:
    def __init__(self, t, nwords):
        self.t = t; self.n = nwords; self.off = 0; self.reg = {}

    def reset(self):
        self.off = 0

    def f32(self, *shape, name=None):
        n = int(np.prod(shape))
        if name: self.reg[name] = (self.off, n, "f32", shape)
        a = self.t[:, self.off:self.off + n]
        self.off += n
        assert self.off <= self.n, ("arena overflow", self.off, self.n)
        return self._shape(a, shape)

    def bf16(self, *shape, name=None):
        n = int(np.prod(shape))
        nw = (n + 1) // 2
        if name: self.reg[name] = (self.off, nw, "bf16", shape)
        a = self.t[:, self.off:self.off + nw].bitcast(BF16)
        if n != 2 * nw:
            a = a[:, 0:n]
        self.off += nw
        assert self.off <= self.n, ("arena overflow", self.off, self.n)
        return self._shape(a, shape)

    @staticmethod
    def _shape(a, shape):
        if len(shape) == 1:
            return a
        if len(shape) == 2:
            return a.rearrange("p (a b) -> p a b", a=shape[0])
        if len(shape) == 3:
            return a.rearrange("p (a b c) -> p a b c", a=shape[0], b=shape[1])
        if len(shape) == 4:
            return a.rearrange("p (a b c d) -> p a b c d", a=shape[0], b=shape[1], c=shape[2])
        raise ValueError(shape)


def bcast_free(ap, n):
    return bass.AP(ap.tensor, ap.offset, [list(d) for d in ap.ap] + [[0, n]])


TILES512 = [(0, 0), (512, 0), (1024, 0), (1536, 0), (2048, 1)]
TILES256 = [(256 * i, 0, i > 0, i < 7) for i in range(8)] + [(2048, 1, False, False), (2304, 1, False, False)]
SEQS = [(0, 16, "s", 0), (16, 2, "p", 0), (18, 2, "p", 1)]


def build(depth=DEPTH):
    nc = bass.Bass("TRN2", target_bir_lowering=False)
    n_rw = depth // 2

    def din(name, shape, dt=F32):
        return nc.dram_tensor(name, list(shape), dt, kind="ExternalInput").ap()

    def dout(name, shape, dt=F32):
        return nc.dram_tensor(name, list(shape), dt, kind="ExternalOutput").ap()

    def dscr(name, shape, dt):
        return nc.dram_tensor(name, list(shape), dt, kind="Internal").ap()

    d_x = din("xT", [128, 8, NT])
    d_cond = din("condT", [128, 8, 2])
    d_state = din("state0", [128, 2, 2, 8, 64])
    d_ident = din("ident", [128, 128])
    d_onesbd = din("onesbd", [128, 128])
    d_maska = din("maska", [2, 128, 512])
    d_maskb = din("maskb", [2, 128, 256])
    d_blk = din("blkm", [2, 128, 7, 128])
    d_adaw = din("ada_w", [4, 128, 8, 6144])
    d_adab = din("ada_b", [128, 4, 48])
    d_ng = din("norm_g", [128, 4, 2, 8])
    d_fg = din("final_g", [128, 8])
    d_win = din("sgu_w_in", [2, 128, 8, 2048])
    d_wout = din("sgu_w_out", [2, 128, 8, 1024])
    d_wsT = din("sgu_wsT", [2, 128, 16, 128])
    d_bs = din("sgu_bs", [2, 1, 2048])
    d_lng = din("sgu_ln", [2, 2, 1024])
    d_w1 = din("mlp_w1", [4, 8, 128, 8, 512])
    d_w2 = din("mlp_w2", [4, 8, 128, 4, 1024])
    d_rw = din("rwkv_w", [2, 4, 128, 8, 1024])
    d_mu = din("rwkv_mu", [128, 2, 6, 8])
    d_l1 = din("rwkv_l1", [2, 4, 128, 8, 64])
    d_l2 = din("rwkv_l2", [2, 4, 64, 1024])
    d_g1 = din("rwkv_g1", [2, 128, 8, 128])
    d_g2 = din("rwkv_g2", [2, 128, 1024])
    d_v1 = din("rwkv_v1", [128, 8, 32])
    d_v2 = din("rwkv_v2", [32, 1024])
    d_pv = din("rwkv_pv", [128, 2, 11, 8])
    d_y = dout("yT", [128, 8, NT])
    d_ns = dout("new_state", [2, 2, 2, 128, 8, 64])
    scr = {}
    for i in range(n_rw):
        for nm in ("r", "k", "v", "kk", "af", "ab", "g", "bv"):
            scr[(i, nm)] = dscr("scr_%s_%d" % (nm, i), [20, 128, 8, 128], BF16)
        for nm in ("sf", "sb"):
            scr[(i, nm)] = dscr("scr_%s_%d" % (nm, i), [20, 128, 8, 128], F32)
        scr[(i, "yf")] = dscr("scr_yf_%d" % i, [20, 128, 1024], BF16)

    P = Prog()
    with ExitStack() as es:
        def sbt(name, shape, dt):
            return es.enter_context(nc.sbuf_tensor(name, list(shape), dt))

        x_sb = sbt("x_sb", [128, 8, NT], F32)
        modT = sbt("modT", [128, 4, 48, 2], F32)
        gsc = sbt("gsc", [128, 4, 2, 8, 2], F32)
        adab = sbt("adab", [128, 4, 48], F32)
        ng = sbt("ng", [128, 4, 2, 8], F32)
        fg = sbt("fg", [128, 8], F32)
        ident = sbt("ident_sb", [128, 128], BF16)
        onesbd = sbt("onesbdb", [128, 128], BF16)
        onesm = sbt("onesm", [128, 128], BF16)
        onesf = sbt("onesf", [128, 128], F32)
        maska = sbt("maska_sb", [128, 2, 512], BF16)
        maskb = sbt("maskb_sb", [128, 2, 256], BF16)
        blkm = sbt("blk_sb", [128, 2, 7, 128], BF16)
        epsn = sbt("epsn", [128, 1], F32)
        epsg = sbt("epsg", [128, 1], F32)
        epsk = sbt("epsk", [128, 1], F32)
        hm = sbt("hm", [128, 2], F32)
        pv = sbt("pv", [128, 2, 11, 8], F32)
        mu = sbt("mu", [128, 2, 6, 8], F32)
        omm = sbt("omm", [128, 2, 6, 8], F32)
        hmu = sbt("hmu", [128, 2, 6, 8], F32)
        omka = sbt("omka", [128, 2, 8], F32)
        tmka = sbt("tmka", [128, 2, 8], F32)
        Hst = sbt("Hst", [128, 8, 64], F32)
        Hb = sbt("Hb", [128, 8, 64], BF16)
        condb = sbt("condb", [128, 8, 2], BF16)
        ARW = 27560
        arena_t = sbt("arena", [128, ARW], F32)
        AR = Arena(arena_t, ARW)
        _DBG['AR'] = AR
        banks = [es.enter_context(nc.psum_tensor("bank%d" % i, [128, 512], F32)) for i in range(8)]
        bankbf = [b[:].bitcast(BF16) for b in banks]

        P.op("sp", "dma_start", out=adab[:], in_=d_adab[:, :, :], w=["adab"], dma=True)
        P.op("sp", "dma_start", out=ng[:], in_=d_ng[:, :, :, :], w=["ng"], dma=True)
        P.op("sp", "dma_start", out=fg[:], in_=d_fg[:, :], w=["fg"], dma=True)
        P.op("sp", "dma_start", out=pv[:], in_=d_pv[:, :, :, :], w=["pv"], dma=True)
        P.op("sp", "dma_start", out=mu[:], in_=d_mu[:, :, :, :], w=["mu"], dma=True)
        P.op("pool", "dma_start", out=ident[:], in_=d_ident[:, :], w=["ident"], dma=True)
        P.op("pool", "dma_start", out=onesbd[:], in_=d_onesbd[:, :], w=["onesbd"], dma=True)
        for dd in range(2):
            P.op("pool", "dma_start", out=maska[:, dd, :], in_=d_maska[dd], w=["maska"], dma=True)
            P.op("pool", "dma_start", out=maskb[:, dd, :], in_=d_maskb[dd], w=["maskb"], dma=True)
            P.op("pool", "dma_start", out=blkm[:, dd, :, :], in_=d_blk[dd], w=["blk"], dma=True)
        P.op("dve", "memset", onesm[:], 1.0 / 1024.0, w=["onesm"])
        P.op("dve", "memset", onesf[:], 1.0, w=["onesf"])
        P.op("dve", "memset", epsn[:], NORM_EPS, w=["epsn"])
        P.op("dve", "memset", epsg[:], GN_EPS, w=["epsg"])
        P.op("dve", "memset", epsk[:], 1e-24, w=["epsk"])
        P.op("dve", "memset", hm[0:64, 0:1], 1.0, w=["hm00"])
        P.op("dve", "memset", hm[64:128, 0:1], 0.0, w=["hm10"])
        P.op("dve", "memset", hm[0:64, 1:2], 0.0, w=["hm01"])
        P.op("dve", "memset", hm[64:128, 1:2], 1.0, w=["hm11"])
        P.op("dve", "tensor_scalar", omm[:], mu[:], -1.0, 1.0, op0=ALU.mult, op1=ALU.add, r=["mu"], w=["omm"])
        P.op("dve", "tensor_scalar", hmu[:], mu[:], 0.5, None, op0=ALU.mult, r=["mu"], w=["hmu"])
        P.op("dve", "tensor_scalar", omka[:], pv[:, :, 6, :], -1.0, 1.0, op0=ALU.mult, op1=ALU.add, r=["pv"], w=["omka"])
        P.op("dve", "tensor_scalar", tmka[:], pv[:, :, 6, :], -2.0, 2.0, op0=ALU.mult, op1=ALU.add, r=["pv"], w=["tmka"])

        def xkeys(t0, n):
            return [("x", k) for k in range(t0 // 128, (t0 + n + 127) // 128)]

        for (t0, _) in TILES512:
            P.op("sp", "dma_start", out=x_sb[:, :, t0:t0 + 512], in_=d_x[:, :, t0:t0 + 512], w=xkeys(t0, 512), dma=True)

        AR.reset()
        condf = AR.f32(8, 2)
        P.op("sp", "dma_start", out=condf, in_=d_cond[:, :, :], w=["condf"], dma=True)
        P.op("act", "activation", out=condb[:], in_=condf, func=AF.Silu, r=["condf"], w=["condb"])
        pm = banks[0][:, 0:96]

        cur_l = [0]

        def ada_layer(l):
            adaw = [AR.bf16(8, 256) for _ in range(2)]
            for blk in range(24):
                buf = adaw[blk % 2]
                bkey = ("adaw", blk % 2)
                P.op("pool", "dma_start", out=buf, in_=d_adaw[l, :, :, blk * 256:(blk + 1) * 256], w=[bkey], dma=True)
                for m in range(2):
                    n = blk * 2 + m
                    for kc in range(8):
                        P.op("pe", "matmul", pm[:, 2 * n:2 * n + 2], buf[:, kc, m * 128:(m + 1) * 128], condb[:, kc, :],
                             start=(kc == 0), stop=(kc == 7), r=[bkey, "condb"], w=["nps"])
            P.op("dve", "tensor_tensor", modT[:, l, :, :], pm.rearrange("p (n w) -> p n w", w=2), bcast_free(adab[:, l, :], 2), op=ALU.add,
                 r=["nps", "adab"], w=[("modT", l)])
            for j in range(2):
                P.op("dve", "tensor_scalar", gsc[:, l, j, :, :], modT[:, l, (3 * j + 1) * 8:(3 * j + 2) * 8, :], 1.0, None, op0=ALU.add,
                     r=[("modT", l)], w=[("gsc", l)])
                P.op("dve", "tensor_tensor", gsc[:, l, j, :, :], gsc[:, l, j, :, :], bcast_free(ng[:, l, j, :], 2), op=ALU.mult,
                     r=[("gsc", l), "ng"], w=[("gsc", l)])

        ada_layer(0)

        def emit_norm(t0, n, scale_ap, bias_ap, out_ap, tmp, ps_bank, out_keys):
            sq, sd, rstd, tmpn = tmp
            ps = ps_bank[:, 0:n]
            for c in range(8):
                s = sq[c % 2]
                P.op("act", "activation", out=s[:, 0:n], in_=x_sb[:, c, t0:t0 + n], func=AF.Square, r=xkeys(t0, n), w=[("nsq", c % 2)])
                P.op("pe", "matmul", ps, onesm[:], s[:, 0:n], start=(c == 0), stop=(c == 7), r=[("nsq", c % 2), "onesm"], w=["nps"])
            P.op("act", "activation", out=sd[:, 0:n], in_=ps, func=AF.Sqrt, bias=epsn[:, 0:1], scale=1.0, r=["nps", "epsn"], w=["nsd"])
            P.op("dve", "reciprocal", rstd[:, 0:n], sd[:, 0:n], r=["nsd"], w=["nrstd"])
            for c in range(8):
                tn = tmpn[c % 2]
                P.op("dve", "tensor_tensor", tn[:, 0:n], x_sb[:, c, t0:t0 + n], rstd[:, 0:n], op=ALU.mult,
                     r=xkeys(t0, n) + ["nrstd"], w=[("ntmp", c % 2)])
                if bias_ap is not None:
                    P.op("act", "activation", out=out_ap(c), in_=tn[:, 0:n], func=AF.Identity, scale=scale_ap(c), bias=bias_ap(c),
                         r=[("ntmp", c % 2), ("gsc", cur_l[0]), ("modT", cur_l[0]), "fg"], w=out_keys(c))
                else:
                    P.op("act", "activation", out=out_ap(c), in_=tn[:, 0:n], func=AF.Identity, scale=scale_ap(c),
                         r=[("ntmp", c % 2), ("gsc", cur_l[0]), ("modT", cur_l[0]), "fg"], w=out_keys(c))

        def norm_tmp(n):
            return ([AR.bf16(n) for _ in range(2)], AR.f32(n), AR.f32(n), [AR.f32(n) for _ in range(2)])

        def sgu_layer(l):
            cur_l[0] = l
            i = l // 2
            P.barrier()
            AR.reset()
            w_in = AR.bf16(8, 2048)
            w_out = AR.bf16(8, 1024)
            wsT = AR.bf16(16, 128)
            lnG = AR.f32(1024)
            lnB = AR.f32(1024)
            bsrow = AR.bf16(2048)
            onesrow = AR.bf16(64)
            ntmp = norm_tmp(512)
            h1 = AR.bf16(8, 512)
            u = AR.bf16(8, 512)
            vt = AR.f32(1024)
            vsq = AR.f32(1024)
            vn = AR.bf16(4, 1024)
            st = AR.f32(8)
            for q in range(4):
                P.op("pool", "dma_start", out=w_in[:, :, q * 512:(q + 1) * 512], in_=d_win[i, :, :, q * 512:(q + 1) * 512], w=[("w_in", q)], dma=True)
            P.op("pool", "dma_start", out=wsT, in_=d_wsT[i], w=["wsT"], dma=True)
            P.op("pool", "dma_start", out=bsrow[0:1, :], in_=d_bs[i], w=["bsrow"], dma=True)
            for q in range(2):
                P.op("pool", "dma_start", out=w_out[:, :, q * 512:(q + 1) * 512], in_=d_wout[i, :, :, q * 512:(q + 1) * 512], w=[("w_out", q)], dma=True)
            P.op("sp", "dma_start", out=lnG, in_=bass.AP(d_lng.tensor, d_lng[i, 0].offset, [[0, 128], [1, 1024]]), w=["lnG"], dma=True)
            P.op("sp", "dma_start", out=lnB, in_=bass.AP(d_lng.tensor, d_lng[i, 1].offset, [[0, 128], [1, 1024]]), w=["lnB"], dma=True)
            P.op("dve", "memset", onesrow[0:1, :], 1.0, w=["onesrow"])
            for (t0, w) in TILES512:
                emit_norm(t0, 512, lambda c: gsc[:, l, 0, c, w:w + 1], lambda c: modT[:, l, 0 + c, w:w + 1],
                          lambda c: h1[:, c, :], ntmp, banks[0], lambda c: ["h1"])
                for oc in range(8):
                    ps = banks[1 + oc % 2]
                    pk = ("bk", 1 + oc % 2)
                    for kc in range(8):
                        P.op("pe", "matmul", ps[:], w_in[:, kc, oc * 128:(oc + 1) * 128], h1[:, kc, :], start=(kc == 0), stop=(kc == 7),
                             r=[("w_in", oc // 4), "h1"], w=[pk])
                    P.op("act", "activation", out=u[:, oc, :], in_=ps[:], func=AF.Gelu_apprx_tanh, r=[pk], w=[("u", oc)])
                for q in range(4):
                    for nb in range(2):
                        ps = banks[3 + nb]
                        pk = ("bk", 3 + nb)
                        for kc in range(8):
                            P.op("pe", "matmul", ps[:], h1[:, kc, q * 128:(q + 1) * 128], w_in[:, kc, 1024 + nb * 512:1024 + (nb + 1) * 512],
                                 start=(kc == 0), stop=(kc == 7), r=[("w_in", 2 + nb), "h1"], w=[pk])
                        P.op("act", "activation", out=vt[:, nb * 512:(nb + 1) * 512], in_=ps[:], func=AF.Gelu_apprx_tanh, r=[pk], w=["vt"])
                    P.op("dve", "tensor_reduce", out=st[:, 0:1], in_=vt, axis=AX.X, op=ALU.add, r=["vt"], w=["st0"])
                    P.op("act", "activation", out=vsq, in_=vt, func=AF.Square, r=["vt"], w=["vsq"])
                    P.op("dve", "tensor_reduce", out=st[:, 1:2], in_=vsq, axis=AX.X, op=ALU.add, r=["vsq"], w=["st1"])
                    P.op("dve", "tensor_scalar", st[:, 2:3], st[:, 0:1], 1.0 / 1024.0, None, op0=ALU.mult, r=["st0"], w=["st2"])
                    P.op("dve", "tensor_tensor", st[:, 3:4], st[:, 2:3], st[:, 2:3], op=ALU.mult, r=["st2"], w=["st3"])
                    P.op("dve", "scalar_tensor_tensor", st[:, 4:5], st[:, 1:2], 1.0 / 1024.0, st[:, 3:4], op0=ALU.mult, op1=ALU.subtract,
                         r=["st1", "st3"], w=["st4"])
                    P.op("act", "activation", out=st[:, 5:6], in_=st[:, 4:5], func=AF.Sqrt, bias=epsn[:, 0:1], scale=1.0, r=["st4", "epsn"], w=["st5"])
                    P.op("dve", "reciprocal", st[:, 6:7], st[:, 5:6], r=["st5"], w=["st6"])
                    P.op("dve", "tensor_scalar", vt, vt, st[:, 2:3], st[:, 6:7], op0=ALU.subtract, op1=ALU.mult, r=["vt", "st2", "st6"], w=["vt"])
                    P.op("dve", "tensor_tensor", vt, vt, lnG, op=ALU.mult, r=["vt", "lnG"], w=["vt"])
                    P.op("dve", "tensor_tensor", vn[:, q, :], vt, lnB, op=ALU.add, r=["vt", "lnB"], w=[("vn", q)])
                for c in range(8):
                    ps = banks[5 + c % 2]
                    pk = ("bk", 5 + c % 2)
                    for q in range(4):
                        for he in range(2):
                            g = 2 * c + he
                            P.op("pe", "matmul", ps[he * 64:(he + 1) * 64, q * 128:(q + 1) * 128], vn[:, q, g * 64:(g + 1) * 64], wsT[:, g, :],
                                 start=True, stop=False, tile_position=(0, he * 64), r=[("vn", q), "wsT"], w=[pk])
                            P.op("pe", "matmul", ps[he * 64:(he + 1) * 64, q * 128:(q + 1) * 128], onesrow[0:1, :], bsrow[0:1, g * 128:(g + 1) * 128],
                                 start=False, stop=True, tile_position=(0, he * 64), r=["onesrow", "bsrow"], w=[pk])
                    P.op("dve", "tensor_tensor", u[:, c, :], ps[:], u[:, c, :], op=ALU.mult, r=[pk, ("u", c)], w=[("u", c)])
                for oc in range(8):
                    ps = banks[1 + oc % 2]
                    pk = ("bk", 1 + oc % 2)
                    for kc in range(8):
                        P.op("pe", "matmul", ps[:], w_out[:, kc, oc * 128:(oc + 1) * 128], u[:, kc, :], start=(kc == 0), stop=(kc == 7),
                             r=[("w_out", oc // 4), ("u", kc)], w=[pk])
                    P.op("dve", "scalar_tensor_tensor", x_sb[:, oc, t0:t0 + 512], ps[:], modT[:, l, 16 + oc, w:w + 1], x_sb[:, oc, t0:t0 + 512],
                         op0=ALU.mult, op1=ALU.add, r=[pk, ("modT", l)] + xkeys(t0, 512), w=xkeys(t0, 512))

        def mlp_layer(l):
            cur_l[0] = l
            P.barrier()
            AR.reset()
            h2 = AR.bf16(8, NT)
            w1b = [AR.bf16(8, 512) for _ in range(2)]
            w2b = [AR.bf16(4, 1024) for _ in range(2)]
            hid = [AR.bf16(4, 512) for _ in range(2)]
            rl = [AR.bf16(512) for _ in range(2)]
            ntmp = norm_tmp(512)

            def load(jb):
                P.op("pool", "dma_start", out=w1b[jb % 2], in_=d_w1[l, jb], w=[("w1b", jb % 2)], dma=True)
                P.op("pool", "dma_start", out=w2b[jb % 2], in_=d_w2[l, jb], w=[("w2b", jb % 2)], dma=True)
            load(0)
            for (t0, w) in TILES512:
                emit_norm(t0, 512, lambda c: gsc[:, l, 1, c, w:w + 1], lambda c: modT[:, l, 24 + c, w:w + 1],
                          lambda c: h2[:, c, t0:t0 + 512], ntmp, banks[0], lambda c: [("h2", t0)])
            if l + 1 < depth:
                ada_layer(l + 1)
            nph = 0
            npo = 0
            nh = 0
            for jb in range(8):
                if jb + 1 < 8:
                    load(jb + 1)
                W1 = w1b[jb % 2]
                W2 = w2b[jb % 2]
                for (t0, w) in TILES512:
                    hd = hid[nh % 2]
                    hk = ("hid", nh % 2)
                    nh += 1
                    for hc in range(4):
                        ps = banks[1 + nph % 4]
                        pk = ("bk", 1 + nph % 4)
                        rr = rl[nph % 2]
                        rk = ("rl", nph % 2)
                        nph += 1
                        for kc in range(8):
                            P.op("pe", "matmul", ps[:], W1[:, kc, hc * 128:(hc + 1) * 128], h2[:, kc, t0:t0 + 512], start=(kc == 0), stop=(kc == 7),
                                 r=[("w1b", jb % 2), ("h2", t0)], w=[pk])
                        P.op("act", "activation", out=rr, in_=ps[:], func=AF.Relu, r=[pk], w=[rk])
                        P.op("dve", "tensor_tensor", hd[:, hc, :], rr, rr, op=ALU.mult, r=[rk], w=[hk])
                    for oc in range(8):
                        ps = banks[5 + npo % 3]
                        pk = ("bk", 5 + npo % 3)
                        npo += 1
                        for hc in range(4):
                            P.op("pe", "matmul", ps[:], W2[:, hc, oc * 128:(oc + 1) * 128], hd[:, hc, :], start=(hc == 0), stop=(hc == 3),
                                 r=[("w2b", jb % 2), hk], w=[pk])
                        P.op("dve", "scalar_tensor_tensor", x_sb[:, oc, t0:t0 + 512], ps[:], modT[:, l, 40 + oc, w:w + 1], x_sb[:, oc, t0:t0 + 512],
                             op0=ALU.mult, op1=ALU.add, r=[pk, ("modT", l)] + xkeys(t0, 512), w=xkeys(t0, 512))

        def rwkv_layer(l):
            cur_l[0] = l
            i = l // 2
            pvi = lambda idx, c: pv[:, i, idx, c:c + 1]
            P.barrier()
            AR.reset()
            wring = [AR.bf16(8, 512) for _ in range(2)]
            l1 = AR.bf16(4, 8, 64)
            l2 = AR.bf16(4, 1024)
            g1w = AR.bf16(8, 128)
            g2w = AR.bf16(1024)
            v1w = AR.bf16(8, 32)
            v2w = AR.bf16(1024)
            ntmp = norm_tmp(258)
            hh = AR.bf16(8, 258)
            ss = AR.bf16(8, 256)
            t1 = [AR.bf16(256) for _ in range(2)]
            xjb = [AR.bf16(8, 256) for _ in range(2)]
            p1b = [AR.bf16(256) for _ in range(2)]
            sig = AR.f32(8, 256)
            o_r = AR.bf16(8, 256)
            o_k = AR.bf16(8, 256)
            o_v = AR.bf16(8, 256)
            o_kk = AR.bf16(8, 256)
            o_af = AR.bf16(8, 256)
            o_ab = AR.bf16(8, 256)
            o_g = AR.bf16(8, 256)
            o_bv = AR.bf16(8, 256)
            tq = AR.bf16(8, 256)
            vf = AR.bf16(8, 256) if i == 1 else None
            sgv = o_bv
            tmpa = AR.f32(256)
            tmpb = AR.f32(256)
            for q in range(4):
                P.op("pool", "dma_start", out=l1[:, q, :, :], in_=d_l1[i, q], w=["l1"], dma=True)
                P.op("pool", "dma_start", out=l2[0:64, q, :], in_=d_l2[i, q], w=["l2"], dma=True)
            P.op("pool", "dma_start", out=g1w, in_=d_g1[i], w=["g1w"], dma=True)
            P.op("pool", "dma_start", out=g2w, in_=d_g2[i], w=["g2w"], dma=True)
            if i == 1:
                P.op("pool", "dma_start", out=v1w, in_=d_v1[:, :, :], w=["v1w"], dma=True)
                P.op("pool", "dma_start", out=v2w[0:32, :], in_=d_v2[:, :], w=["v2w"], dma=True)

            def wload(wi):
                for hb in range(2):
                    P.op("pool", "dma_start", out=wring[hb], in_=d_rw[i, wi, :, :, hb * 512:(hb + 1) * 512], w=[("wring", hb)], dma=True)

            npb = [0]

            def proj(wi, evac, xp):
                for o2 in range(4):
                    bk = 1 + npb[0] % 3
                    npb[0] += 1
                    pk = ("bk", bk)
                    for sub in range(2):
                        oc = 2 * o2 + sub
                        for kc in range(8):
                            P.op("pe", "matmul", banks[bk][:, sub * 256:(sub + 1) * 256], wring[oc // 4][:, kc, (oc % 4) * 128:(oc % 4 + 1) * 128], xjb[xp][:, kc, :],
                                 start=(kc == 0), stop=(kc == 7), r=[("wring", oc // 4), ("xj", xp)], w=[pk])
                    evac(banks[bk][:].rearrange("p (a b) -> p a b", a=2), 2 * o2, pk)

            def mix(j, xp):
                for c in range(8):
                    tt = t1[c % 2]
                    P.op("act", "activation", out=tt, in_=hh[:, c, 1:257], func=AF.Identity, scale=omm[:, i, j, c:c + 1], r=["hh", "omm"], w=[("t1", c % 2)])
                    P.op("dve", "scalar_tensor_tensor", xjb[xp][:, c, :], ss[:, c, :], hmu[:, i, j, c:c + 1], tt, op0=ALU.mult, op1=ALU.add,
                         r=["ss", "hmu", ("t1", c % 2)], w=[("xj", xp)])

            def lora(wq, K, func1, evac2, xp, l1w=None, l2w=None):
                A1 = l1w if l1w is not None else l1[:, wq, :, :]
                pb = p1b[wq % 2]
                pkey = ("p1b", wq % 2)
                for kc in range(8):
                    P.op("pe", "matmul", banks[4][0:K, 0:256], A1[:, kc, 0:K], xjb[xp][:, kc, :], start=(kc == 0), stop=(kc == 7), r=["l1", "g1w", "v1w", ("xj", xp)], w=[("bk", 4)])
                P.op("act", "activation", out=pb[0:K, :], in_=banks[4][0:K, 0:256], func=func1, r=[("bk", 4)], w=[pkey])
                A2 = l2w if l2w is not None else l2[:, wq, :]
                for o2 in range(4):
                    bk = 5 + o2 % 2
                    pk = ("bk", bk)
                    for sub in range(2):
                        oc = 2 * o2 + sub
                        P.op("pe", "matmul", banks[bk][:, sub * 256:(sub + 1) * 256], A2[0:K, oc * 128:(oc + 1) * 128], pb[0:K, :], start=True, stop=True,
                             r=["l2", "g2w", "v2w", pkey], w=[pk])
                    evac2(banks[bk][:].rearrange("p (a b) -> p a b", a=2), 2 * o2, pk)

            def spill(buf, key, name, ch0):
                for q in range(2):
                    P.op("sp", "dma_start", out=scr[(i, name)][ch0 + q], in_=buf[:, :, q * 128:(q + 1) * 128], r=[key], w=[("scr", name, ch0 + q)], dma=True)

            wload(2)
            for (t0, w, lv, rv) in TILES256:
                ch0 = t0 // 128
                a = t0 - 1 if lv else t0
                b = t0 + 257 if rv else t0 + 256
                off = a - (t0 - 1)
                n = b - a
                emit_norm(a, n, lambda c: gsc[:, l, 0, c, w:w + 1], lambda c: modT[:, l, 0 + c, w:w + 1],
                          lambda c: hh[:, c, off:off + n], ntmp, banks[0], lambda c: ["hh"])
                if not lv:
                    P.op("dve", "memset", hh[:, :, 0:1], 0.0, w=["hh"])
                if not rv:
                    P.op("dve", "memset", hh[:, :, 257:258], 0.0, w=["hh"])
                P.op("dve", "tensor_tensor", ss, hh[:, :, 0:256], hh[:, :, 2:258], op=ALU.add, r=["hh"], w=["ss"])
                mix(1, 0)
                mix(4, 1)
                for d in range(2):
                    def ev_w(bv_, oc0, pk, d=d):
                        for sub in range(2):
                            oc = oc0 + sub
                            P.op("act", "activation", out=sig[:, oc, :], in_=bv_[:, sub, :], func=AF.Sigmoid, bias=pvi(0 + d, oc), scale=1.0, r=[pk, "pv"], w=["sig"])
                    lora(d, 64, AF.Tanh, ev_w, 0)
                    spill(sig, "sig", "sf" if d == 0 else "sb", ch0)
                mix(5, 0)
                for d in range(2):
                    oa = o_af if d == 0 else o_ab
                    def ev_a(bv_, oc0, pk, d=d, oa=oa):
                        for sub in range(2):
                            oc = oc0 + sub
                            P.op("act", "activation", out=oa[:, oc, :], in_=bv_[:, sub, :], func=AF.Sigmoid, bias=pvi(2 + d, oc), scale=1.0, r=[pk, "pv"], w=[("oa", d)])
                    lora(2 + d, 64, AF.Identity, ev_a, 1)
                    spill(oa, ("oa", d), "af" if d == 0 else "ab", ch0)
                P.op("dve", "tensor_tensor", tq, o_af, o_ab, op=ALU.add, r=[("oa", 0), ("oa", 1)], w=["tq"])
                for c in range(8):
                    P.op("dve", "tensor_scalar", tq[:, c, :], tq[:, c, :], pvi(6, c), tmka[:, i, c:c + 1], op0=ALU.mult, op1=ALU.add, r=["tq", "pv", "tmka"], w=["tq"])
                mix(3, 1)
                def ev_g(bv_, oc0, pk):
                    P.op("act", "copy", o_g[:, oc0:oc0 + 2, :], bv_, r=[pk], w=["o_g"])
                lora(0, 128, AF.Sigmoid, ev_g, 0, l1w=g1w, l2w=g2w)
                spill(o_g, "o_g", "g", ch0)
                mix(0, 0)
                def ev_v(bv_, oc0, pk):
                    P.op("act", "copy", o_v[:, oc0:oc0 + 2, :], bv_, r=[pk], w=["o_v"])
                proj(2, ev_v, 1)
                wload(0)
                if i == 1:
                    for q in range(2):
                        P.op("sp", "dma_start", out=vf[:, :, q * 128:(q + 1) * 128], in_=scr[(0, "v")][ch0 + q], w=["vf"], dma=True)
                    def ev_sv(bv_, oc0, pk):
                        for sub in range(2):
                            oc = oc0 + sub
                            P.op("act", "activation", out=sgv[:, oc, :], in_=bv_[:, sub, :], func=AF.Sigmoid, bias=pvi(4, oc), scale=1.0, r=[pk, "pv"], w=["o_bv"])
                    lora(1, 32, AF.Identity, ev_sv, 1, l1w=v1w, l2w=v2w)
                    P.op("dve", "tensor_tensor", vf, vf, o_v, op=ALU.subtract, r=["vf", "o_v"], w=["vf"])
                    P.op("dve", "tensor_tensor", vf, vf, sgv, op=ALU.mult, r=["vf", "o_bv"], w=["vf"])
                    P.op("dve", "tensor_tensor", o_v, o_v, vf, op=ALU.add, r=["vf", "o_v"], w=["o_v"])
                spill(o_v, "o_v", "v", ch0)
                mix(2, 1)
                def ev_r(bv_, oc0, pk):
                    P.op("act", "copy", o_r[:, oc0:oc0 + 2, :], bv_, r=[pk], w=["o_r"])
                proj(0, ev_r, 0)
                wload(1)
                spill(o_r, "o_r", "r", ch0)
                def ev_k(bv_, oc0, pk):
                    P.op("act", "copy", o_k[:, oc0:oc0 + 2, :], bv_, r=[pk], w=["o_k"])
                proj(1, ev_k, 1)
                wload(2)
                spill(o_k, "o_k", "k", ch0)
                for c in range(8):
                    P.op("dve", "tensor_scalar", o_kk[:, c, :], o_k[:, c, :], pvi(5, c), None, op0=ALU.mult, r=["o_k", "pv"], w=["o_kk"])
                    sqb = t1[c % 2]
                    P.op("act", "activation", out=sqb, in_=o_kk[:, c, :], func=AF.Square, r=["o_kk"], w=[("t1", c % 2)])
                    P.op("pe", "matmul", banks[7][:, 0:256], onesbd[:], sqb, start=True, stop=True, r=[("t1", c % 2), "onesbd"], w=[("bk", 7)])
                    P.op("act", "activation", out=tmpa, in_=banks[7][:, 0:256], func=AF.Sqrt, bias=epsk[:, 0:1], scale=1.0, r=[("bk", 7), "epsk"], w=["tmpa"])
                    P.op("dve", "reciprocal", tmpb, tmpa, r=["tmpa"], w=["tmpb"])
                    P.op("dve", "tensor_tensor", o_kk[:, c, :], o_kk[:, c, :], tmpb, op=ALU.mult, r=["o_kk", "tmpb"], w=["o_kk"])
                spill(o_kk, "o_kk", "kk", ch0)
                for c in range(8):
                    bq = t1[c % 2]
                    P.op("dve", "scalar_tensor_tensor", tmpa, o_r[:, c, :], pvi(7, c), o_k[:, c, :], op0=ALU.mult, op1=ALU.mult, r=["o_r", "o_k", "pv"], w=["tmpa"])
                    P.op("dve", "tensor_tensor", bq, tmpa, tq[:, c, :], op=ALU.mult, r=["tmpa", "tq"], w=[("t1", c % 2)])
                    P.op("pe", "matmul", banks[7][:, 256:512], onesbd[:], bq, start=True, stop=True, r=[("t1", c % 2), "onesbd"], w=[("bk7b",)])
                    P.op("dve", "tensor_tensor", o_bv[:, c, :], banks[7][:, 256:512], o_v[:, c, :], op=ALU.mult, r=[("bk7b",), "o_v"], w=["o_bv"])
                spill(o_bv, "o_bv", "bv", ch0)

            _rwstop = os.environ.get("KRW", "")
            for d in range(2):
                if _rwstop == "proj" or (_rwstop == "fwd" and d == 1):
                    break
                P.barrier()
                AR.reset()
                ld = {}
                for nm in ("r", "k", "v", "kk", "a"):
                    ld[nm] = [AR.bf16(8, 128)] * 2
                ld["s"] = [AR.f32(8, 128)] * 2
                pinc = AR.f32(8, 128, name="pinc")
                eLi_f = AR.f32(1024, name="eLi")
                emL_f = AR.f32(1024, name="emL")
                eLe_f = AR.f32(1024, name="eLe")
                eLi = eLi_f.rearrange("p (a b) -> p a b", a=8)
                emL = emL_f.rearrange("p (a b) -> p a b", a=8)
                eLe = eLe_f.rearrange("p (a b) -> p a b", a=8)
                PC = AR.f32(8, name="PC")
                bPN = AR.f32(2, 8)
                ar = AR.bf16(8, 256, name="ar")
                bt = AR.bf16(8, 128, name="bt")
                kt = AR.bf16(8, 128, name="kt")
                ka = AR.bf16(8, 128)
                tqs = AR.bf16(8, 128)
                Vtok = AR.bf16(1024, name="Vtok")
                Ktok = AR.bf16(1024, name="Ktok")
                Btok = AR.bf16(1024, name="Btok")
                msk = [AR.bf16(2, 512) for _ in range(4)]
                DDb = [[AR.bf16(2, 2, 128) for _ in range(2)] for _ in range(4)]
                Eb = [AR.bf16(2, 128) for _ in range(4)]
                Dt7 = [AR.bf16(2, 128) for _ in range(4)]
                Xb = [AR.bf16(128) for _ in range(4)]
                Ub = [AR.bf16(128) for _ in range(4)]
                pad = [AR.bf16(4, 2, 128) for _ in range(4)]
                ident2 = bass.AP(ident[:].tensor, ident[:].offset, [list(ident[:].ap[0]), [0, 2], [1, 128]])

                def blk_ap(dd, lev):
                    v_ = blkm[:, dd, lev, :]
                    return bass.AP(v_.tensor, v_.offset, [list(v_.ap[0]), [0, 2], [1, 128]])
                if d == 0:
                    yfb = [AR.bf16(1024) for _ in range(2)]
                else:
                    yfl = [AR.bf16(1024)] * 2
                    ysum = eLe_f
                    ysq = eLi_f
                    ynb = tqs.rearrange("p a b -> p (a b)")
                    yo1 = emL
                    yob = ka
                    ldg = [AR.bf16(8, 128)] * 2
                    ldbv = [AR.bf16(8, 128)] * 2
                    Wo = AR.bf16(8, 1024)
                    st16 = AR.f32(6, 16)
                    for hb in range(2):
                        P.op("pool", "dma_start", out=Wo[:, :, hb * 512:(hb + 1) * 512], in_=d_rw[i, 3, :, :, hb * 512:(hb + 1) * 512], w=["Wo"], dma=True)
                aname = "af" if d == 0 else "ab"
                sname = "sf" if d == 0 else "sb"
                order = []
                for (c0, ncx, kind, idx) in SEQS:
                    chs = list(range(c0, c0 + ncx))
                    if d == 1:
                        chs = chs[::-1]
                    for k_, ch in enumerate(chs):
                        order.append((ch, kind, idx, k_ == 0, k_ == ncx - 1))

                def loads(n_):
                    ch = order[n_][0]
                    sl = n_ % 2
                    for nm, sn in (("r", "r"), ("k", "k"), ("v", "v"), ("kk", "kk"), ("a", aname), ("s", sname)):
                        P.op("sp", "dma_start", out=ld[nm][sl], in_=scr[(i, sn)][ch], w=[("ld", nm, 0)], dma=True)

                loads(0)
                for n_, (ch, kind, idx, first, last) in enumerate(order):
                    sl = n_ % 2
                    if d == 1:
                        P.op("sp", "dma_start", out=yfl[0], in_=scr[(i, "yf")][ch], w=[("yfl", 0)], dma=True)
                        P.op("sp", "dma_start", out=ldg[0], in_=scr[(i, "g")][ch], w=[("ldg", 0)], dma=True)
                        P.op("sp", "dma_start", out=ldbv[0], in_=scr[(i, "bv")][ch], w=[("ldbv", 0)], dma=True)
                    L = {nm: ld[nm][sl] for nm in ld}
                    lk = lambda nm: ("ld", nm, 0)
                    if first:
                        if kind == "s":
                            P.op("sp", "dma_start", out=Hst[:], in_=d_state[:, i, d], w=[("Hst", c_) for c_ in range(8)], dma=True)
                        else:
                            P.op("dve", "memset", Hst[:], 0.0, w=[("Hst", c_) for c_ in range(8)])
                        P.op("act", "copy", Hb[:], Hst[:], r=[("Hst", c_) for c_ in range(8)], w=[("Hb", c_) for c_ in range(8)])
                    _ks = os.environ.get("KSCAN", "")
                    if _ks == "load":
                        continue
                    for c in range(8):
                        P.op("dve", "tensor_tensor_scan", out=pinc[:, c, :], data0=onesf[:], data1=L["s"][:, c, :], initial=0.0, op0=ALU.mult, op1=ALU.add,
                             r=[lk("s"), "onesf"], w=["pinc"])
                    pexc = L["s"]
                    P.op("dve", "tensor_tensor", pexc, pinc, L["s"], op=ALU.subtract, r=["pinc", lk("s")], w=[lk("s")])
                    P.op("act", "activation", out=PC, in_=pinc[:, :, 127], func=AF.Exp, scale=NEGC, r=["pinc"], w=["PC"])
                    if d == 0:
                        P.op("act", "activation", out=eLi, in_=pinc, func=AF.Exp, scale=NEGC, r=["pinc"], w=["eLi"])
                        P.op("act", "activation", out=emL, in_=pinc, func=AF.Exp, scale=-NEGC, r=["pinc"], w=["emL"])
                        P.op("act", "activation", out=eLe, in_=pexc, func=AF.Exp, scale=NEGC, r=[lk("s")], w=["eLe"])
                    else:
                        P.op("dve", "tensor_scalar", bPN[:, 0, :], pinc[:, :, 127], NEGC, None, op0=ALU.mult, r=["pinc"], w=["bPN"])
                        P.op("dve", "tensor_scalar", bPN[:, 1, :], pinc[:, :, 127], -NEGC, None, op0=ALU.mult, r=["pinc"], w=["bPN"])
                        for c in range(8):
                            P.op("act", "activation", out=eLi[:, c, :], in_=pexc[:, c, :], func=AF.Exp, scale=-NEGC, bias=bPN[:, 0, c:c + 1], r=[lk("s"), "bPN"], w=["eLi"])
                            P.op("act", "activation", out=emL[:, c, :], in_=pexc[:, c, :], func=AF.Exp, scale=NEGC, bias=bPN[:, 1, c:c + 1], r=[lk("s"), "bPN"], w=["emL"])
                            P.op("act", "activation", out=eLe[:, c, :], in_=pinc[:, c, :], func=AF.Exp, scale=-NEGC, bias=bPN[:, 0, c:c + 1], r=["pinc", "bPN"], w=["eLe"])
                    P.op("dve", "tensor_tensor", ar[:, :, 128:256], L["r"], eLi, op=ALU.mult, r=[lk("r"), "eLi"], w=["ar"])
                    P.op("dve", "scalar_tensor_tensor", ar[:, :, 0:128], L["kk"], -1.0, eLe, op0=ALU.mult, op1=ALU.mult, r=[lk("kk"), "eLe"], w=["ar"])
                    P.op("dve", "tensor_tensor", ka, L["kk"], L["a"], op=ALU.mult, r=[lk("kk"), lk("a")], w=["ka"])
                    P.op("dve", "tensor_tensor", bt, ka, emL, op=ALU.mult, r=["ka", "emL"], w=["bt"])
                    P.op("dve", "tensor_tensor", tqs, L["a"], bcast_free(pv[:, i, 6, :], 128), op=ALU.mult, r=[lk("a"), "pv"], w=["tqs"])
                    P.op("dve", "tensor_tensor", tqs, tqs, bcast_free(omka[:, i, :], 128), op=ALU.add, r=["tqs", "omka"], w=["tqs"])
                    P.op("dve", "tensor_tensor", ka, L["k"], tqs, op=ALU.mult, r=[lk("k"), "tqs", "bt"], w=["ka"])
                    P.op("dve", "tensor_tensor", kt, ka, emL, op=ALU.mult, r=["ka", "emL"], w=["kt"])
                    if _ks == "prep":
                        continue
                    for c in range(8):
                        P.op("pe", "transpose", bankbf[6][:, c * 128:(c + 1) * 128], L["v"][:, c, :], ident[:], r=[lk("v"), "ident"], w=[("bk", 6)])
                    P.op("act", "copy", Vtok, bankbf[6], r=[("bk", 6)], w=["Vtok"])
                    for c in range(8):
                        P.op("pe", "transpose", bankbf[7][:, c * 128:(c + 1) * 128], kt[:, c, :], ident[:], r=["kt", "ident"], w=[("bk", 7)])
                    P.op("dve", "tensor_copy", Ktok, bankbf[7], r=[("bk", 7)], w=["Ktok"])
                    for c in range(8):
                        P.op("pe", "transpose", bankbf[6][:, c * 128:(c + 1) * 128], bt[:, c, :], ident[:], r=["bt", "ident"], w=[("bk", 6)])
                    P.op("act", "copy", Btok, bankbf[6], r=[("bk", 6)], w=["Btok"])

                    if n_ + 1 < len(order):
                        loads(n_ + 1)
                    def unit(c, s):
                        A, B = banks[2 * s], banks[2 * s + 1]
                        kA, kB = ("bk", 2 * s), ("bk", 2 * s + 1)
                        kmsk, kUb, kEb, kXb, kDt7, kpad = ("msk", s), ("Ub", s), ("Eb", s), ("Xb", s), ("Dt7", s), ("pad", s)
                        pd = pad[s]
                        hmk = ["hm00", "hm10", "hm01", "hm11"]
                        for he in range(2):
                            P.op("act", "activation", out=pd[:, 0, he, :], in_=bt[:, c, :], func=AF.Identity, scale=hm[:, he:he + 1], r=["bt"] + hmk, w=[kpad])
                            P.op("act", "activation", out=pd[:, 1, he, :], in_=kt[:, c, :], func=AF.Identity, scale=hm[:, he:he + 1], r=["kt"] + hmk, w=[kpad])
                            P.op("act", "activation", out=pd[:, 2, he, :], in_=ar[:, c, 0:128], func=AF.Identity, scale=hm[:, he:he + 1], r=["ar"] + hmk, w=[kpad])
                            P.op("act", "activation", out=pd[:, 3, he, :], in_=ar[:, c, 128:256], func=AF.Identity, scale=hm[:, he:he + 1], r=["ar"] + hmk, w=[kpad])
                        yield
                        for he, bank, kb in ((0, A, kA), (1, B, kB)):
                            P.op("pe", "matmul", bank[:, 0:256], pd[:, 0, he, :], ar[:, c, :], start=True, stop=True, r=[kpad, "ar"], w=[kb])
                            P.op("pe", "matmul", bank[:, 256:512], pd[:, 1, he, :], ar[:, c, :], start=True, stop=True, r=[kpad, "ar"], w=[kb])
                        yield
                        P.op("dve", "tensor_tensor", msk[s][:, 0, :], A[:], maska[:, d, :], op=ALU.mult, r=[kA, "maska"], w=[kmsk])
                        P.op("dve", "tensor_tensor", msk[s][:, 1, :], B[:], maska[:, d, :], op=ALU.mult, r=[kB, "maska"], w=[kmsk])
                        yield
                        for he in range(2):
                            b0 = he * 64
                            P.op("pe", "matmul", A[:, 256 + he * 64:256 + (he + 1) * 64], pd[:, 2, he, :], Hb[:, c, :], start=(he == 0), stop=False,
                                 skip_group_check=True, r=[kpad, ("Hb", c), kmsk], w=[kA])
                            P.op("pe", "matmul", A[:, 256 + he * 64:256 + (he + 1) * 64], msk[s][:, he, 256:384], Vtok[:, c * 128 + b0:c * 128 + b0 + 64], start=False, stop=(he == 1),
                                 skip_group_check=True, r=[kmsk, "Vtok"], w=[kA])
                        yield
                        P.op("act", "copy", Xb[s], A[:, 256:384], r=[kA], w=[kXb])
                        yield
                        curD = [ident[:], ident[:]]
                        curT = [ident[:], ident[:]]
                        curv = bass.AP(ident[:].tensor, ident[:].offset, [list(ident[:].ap[0]), [0, 2], [0, 2], [1, 128]])
                        kcur = "ident"
                        for lev in range(7):
                            for he in range(2):
                                P.op("pe", "matmul", A[:, he * 128:(he + 1) * 128], msk[s][:, he, 0:128], curD[he], start=True, stop=True, r=[kmsk, kcur], w=[kA])
                            yield
                            P.op("dve", "tensor_tensor", Eb[s], A[:, 0:256].rearrange("p (a b) -> p a b", a=2), blk_ap(d, lev), op=ALU.mult, r=[kA, "blk"], w=[kEb])
                            yield
                            if lev < 6:
                                nxt = DDb[s][lev % 2]
                                knxt = ("DD", s, lev % 2)
                                for he in range(2):
                                    P.op("pe", "matmul", B[:, he * 256:he * 256 + 128], curT[he], Eb[s][:, he, :], start=True, stop=True, r=[kcur, kEb], w=[kB])
                                    P.op("pe", "matmul", B[:, he * 256 + 128:he * 256 + 256], Eb[s][:, he, :], curT[he], start=True, stop=True, r=[kcur, kEb], w=[kB])
                                yield
                                P.op("dve", "tensor_tensor", nxt, B[:].rearrange("p (a b c) -> p a b c", a=2, b=2), curv, op=ALU.add, r=[kB, kcur], w=[knxt])
                                curD = [nxt[:, he, 0, :] for he in range(2)]
                                curT = [nxt[:, he, 1, :] for he in range(2)]
                                curv = nxt
                                kcur = knxt
                            else:
                                for he in range(2):
                                    P.op("pe", "matmul", B[:, he * 128:(he + 1) * 128], Eb[s][:, he, :], curT[he], start=True, stop=True, r=[kcur, kEb], w=[kB])
                                yield
                                P.op("dve", "tensor_tensor", Dt7[s], B[:, 0:256].rearrange("p (a b) -> p a b", a=2), curv[:, :, 1, :], op=ALU.add, r=[kB, kcur], w=[kDt7])
                            yield
                        for he in range(2):
                            P.op("pe", "matmul", A[:, 256 + he * 64:256 + (he + 1) * 64], Dt7[s][:, he, :], Xb[s][:, he * 64:(he + 1) * 64], start=True, stop=True,
                                 r=[kDt7, kXb], w=[kA])
                        yield
                        P.op("act", "copy", Ub[s], A[:, 256:384], r=[kA], w=[kUb])
                        yield
                        for he in range(2):
                            b0 = he * 64
                            yo_ = A[:, 384 + he * 64:384 + (he + 1) * 64]
                            P.op("pe", "matmul", yo_, pd[:, 3, he, :], Hb[:, c, :], start=(he == 0), stop=False, r=[kpad, ("Hb", c), kUb], w=[kA])
                            P.op("pe", "matmul", yo_, msk[s][:, he, 128:256], Ub[s][:, b0:b0 + 64], start=False, stop=False, r=[kmsk, kUb], w=[kA])
                            P.op("pe", "matmul", yo_, msk[s][:, he, 384:512], Vtok[:, c * 128 + b0:c * 128 + b0 + 64], start=False, stop=(he == 1), r=[kmsk, "Vtok"], w=[kA])
                        for he in range(2):
                            b0 = he * 64
                            ph = A[b0:b0 + 64, 0:64]
                            P.op("pe", "matmul", ph, Btok[:, c * 128 + b0:c * 128 + b0 + 64], Ub[s][:, b0:b0 + 64], start=True, stop=False, tile_position=(0, b0),
                                 r=["Btok", kUb], w=[kA])
                            P.op("pe", "matmul", ph, Ktok[:, c * 128 + b0:c * 128 + b0 + 64], Vtok[:, c * 128 + b0:c * 128 + b0 + 64], start=False, stop=True, tile_position=(0, b0),
                                 r=["Ktok", "Vtok"], w=[kA])
                        yield
                        if d == 0:
                            P.op("dve", "tensor_copy", yfb[n_ % 2][:, c * 128:(c + 1) * 128], A[:, 384:512], r=[kA], w=[("yfb", n_ % 2)])
                        else:
                            P.op("dve", "tensor_tensor", ysum[:, c * 128:(c + 1) * 128], A[:, 384:512], yfl[0][:, c * 128:(c + 1) * 128], op=ALU.add,
                                 r=[kA, ("yfl", 0)], w=["eLe"])
                        P.op("dve", "tensor_tensor", Hst[:, c, :], A[:, 0:64], Hst[:, c, :], op=ALU.add, r=[kA, ("Hst", c)], w=[("Hst", c)])
                        P.op("dve", "tensor_scalar", Hst[:, c, :], Hst[:, c, :], PC[:, c:c + 1], None, op0=ALU.mult, r=[("Hst", c), "PC"], w=[("Hst", c)])
                        P.op("act", "copy", Hb[:, c, :], Hst[:, c, :], r=[("Hst", c)], w=[("Hb", c)])
                        yield

                    for cg in range(2):
                        gens = [unit(4 * cg + q, q) for q in range(4)]
                        alive = [True] * 4
                        while any(alive):
                            for gi, g_ in enumerate(gens):
                                if alive[gi]:
                                    try:
                                        next(g_)
                                    except StopIteration:
                                        alive[gi] = False

                    if d == 0:
                        P.op("sp", "dma_start", out=scr[(i, "yf")][ch], in_=yfb[n_ % 2], r=[("yfb", n_ % 2)], w=[("scr", "yf", ch)], dma=True)
                    else:
                        w = 0 if ch < 16 else 1
                        col0 = ch * 128
                        y3 = ysum.rearrange("p (h n) -> p h n", n=64)
                        P.op("dve", "tensor_reduce", out=st16[:, 0, :], in_=y3, axis=AX.X, op=ALU.add, r=["eLe"], w=["st16a"])
                        P.op("act", "activation", out=ysq, in_=ysum, func=AF.Square, r=["eLe"], w=["eLi"])
                        P.op("dve", "tensor_reduce", out=st16[:, 1, :], in_=ysq.rearrange("p (h n) -> p h n", n=64), axis=AX.X, op=ALU.add, r=["eLi"], w=["st16b"])
                        P.op("dve", "tensor_scalar", st16[:, 2, :], st16[:, 0, :], 1.0 / 64.0, None, op0=ALU.mult, r=["st16a"], w=["st16c"])
                        P.op("dve", "tensor_tensor", st16[:, 3, :], st16[:, 2, :], st16[:, 2, :], op=ALU.mult, r=["st16c"], w=["st16d"])
                        P.op("dve", "scalar_tensor_tensor", st16[:, 4, :], st16[:, 1, :], 1.0 / 64.0, st16[:, 3, :], op0=ALU.mult, op1=ALU.subtract,
                             r=["st16b", "st16d"], w=["st16e"])
                        P.op("act", "activation", out=st16[:, 5, :], in_=st16[:, 4, :], func=AF.Sqrt, bias=epsg[:, 0:1], scale=1.0, r=["st16e", "epsg"], w=["st16f"])
                        P.op("dve", "reciprocal", st16[:, 4, :], st16[:, 5, :], r=["st16f"], w=["st16g"])
                        P.op("dve", "tensor_tensor", y3, y3, bcast_free(st16[:, 2, :], 64), op=ALU.subtract, r=["eLe", "st16c"], w=["eLe"])
                        P.op("dve", "tensor_tensor", ynb.rearrange("p (h n) -> p h n", n=64), y3, bcast_free(st16[:, 4, :], 64), op=ALU.mult, r=["eLe", "st16g"], w=["tqs"])
                        for c in range(8):
                            P.op("pe", "transpose", bankbf[6][:, c * 128:(c + 1) * 128], ynb[:, c * 128:(c + 1) * 128], ident[:], r=["tqs", "ident"], w=[("bk", 6)])
                        for c in range(8):
                            P.op("act", "activation", out=yo1[:, c, :], in_=bankbf[6][:, c * 128:(c + 1) * 128], func=AF.Identity, scale=pvi(8, c), bias=pvi(9, c),
                                 r=[("bk", 6), "pv"], w=["emL"])
                        P.op("dve", "tensor_tensor", yo1, yo1, ldbv[sl], op=ALU.add, r=["emL", ("ldbv", 0)], w=["emL"])
                        P.op("dve", "tensor_tensor", yob, yo1, ldg[sl], op=ALU.mult, r=["emL", ("ldg", 0)], w=["ka"])
                        for oc in range(8):
                            bk = oc // 4
                            for kc in range(8):
                                P.op("pe", "matmul", banks[bk][:, (oc % 4) * 128:(oc % 4 + 1) * 128], Wo[:, kc, oc * 128:(oc + 1) * 128], yob[:, kc, :],
                                     start=(kc == 0), stop=(kc == 7), r=["Wo", "ka"], w=[("bk", bk)])
                        for oc in range(8):
                            bk = oc // 4
                            P.op("dve", "scalar_tensor_tensor", x_sb[:, oc, col0:col0 + 128], banks[bk][:, (oc % 4) * 128:(oc % 4 + 1) * 128], modT[:, l, 16 + oc, w:w + 1],
                                 x_sb[:, oc, col0:col0 + 128], op0=ALU.mult, op1=ALU.add, r=[("bk", bk), ("modT", l)] + xkeys(col0, 128), w=xkeys(col0, 128))
                    if last and kind == "p":
                        P.op("sp", "dma_start", out=d_ns[idx, i, d], in_=Hst[:], r=[("Hst", c_) for c_ in range(8)], w=[("ns", idx, i, d)], dma=True)

        for l in range(depth):
            if l % 2 == 0:
                sgu_layer(l)
            else:
                rwkv_layer(l)
            mlp_layer(l)

        P.barrier()
        AR.reset()
        ntmp = norm_tmp(512)
        yo = [AR.f32(8, 512) for _ in range(2)]
        for ti, (t0, w) in enumerate(TILES512):
            yb = yo[ti % 2]
            emit_norm(t0, 512, lambda c: fg[:, c:c + 1], None, lambda c: yb[:, c, :], ntmp, banks[0], lambda c: [("yo", ti % 2)])
            P.op("sp", "dma_start", out=d_y[:, :, t0:t0 + 512], in_=yb, r=[("yo", ti % 2)], w=[("dy", ti)], dma=True)
        if n_rw == 0:
            pass

        P.finalize()
        sems = {k: es.enter_context(nc.semaphore("s_%s_%d" % k)) for k in sorted(P.sem_keys)}
        block = es.enter_context(nc.Block())
        P.emit(sems, block)
    return nc


def _pc(a):
    a = np.asarray(a, np.float32)
    lead = a.shape[:-1]
    b = a.reshape(lead + (8, 128))
    return np.ascontiguousarray(np.moveaxis(b, -1, 0))


def _kmajor(w):
    w = np.asarray(w, np.float32)
    K, N = w.shape[-2], w.shape[-1]
    lead = w.shape[:-2]
    b = w.reshape(lead + (K // 128, 128, N))
    return np.ascontiguousarray(np.swapaxes(b, -3, -2))


_NC_CACHE = {}
_RUNNER = [None]


def kernel(x_prompt, x_sample, state_wkv, c, c_ctx, norm1_g, norm2_g, ada_w, ada_b,
           sgu_w_in, sgu_ln_g, sgu_ln_b, sgu_w_s, sgu_b_s, sgu_w_out,
           rwkv_mu, rwkv_w_r, rwkv_w_k, rwkv_w_v, rwkv_w_o, rwkv_w0, rwkv_w1, rwkv_w2,
           rwkv_a0, rwkv_a1, rwkv_a2, rwkv_v0, rwkv_v1, rwkv_v2, rwkv_g1, rwkv_g2,
           rwkv_k_k, rwkv_k_a, rwkv_r_k, rwkv_ln_g, rwkv_ln_b, mlp_w1, mlp_w2, final_g, _depth=DEPTH):
    f = lambda a: np.asarray(a, np.float32)
    depth = _depth
    shared = {}
    shared["ident"] = np.eye(128, dtype=np.float32)
    bd = np.zeros((128, 128), np.float32); bd[:64, :64] = 1; bd[64:, 64:] = 1
    shared["onesbd"] = bd
    s_i = np.arange(128)[:, None]; t_i = np.arange(128)[None, :]
    su = (s_i < t_i).astype(np.float32); iu = (s_i <= t_i).astype(np.float32)
    sl = (s_i > t_i).astype(np.float32); il = (s_i >= t_i).astype(np.float32)
    shared["maska"] = np.stack([np.concatenate([su, iu, su, iu], 1), np.concatenate([sl, il, sl, il], 1)])
    shared["maskb"] = np.stack([np.concatenate([sl, sl], 1), np.concatenate([su, su], 1)])
    bl = []
    for lev in range(7):
        bs = 2 ** lev
        bl.append(((s_i // (2 * bs) == t_i // (2 * bs)) & ((s_i // bs) % 2 == 1) & ((t_i // bs) % 2 == 0)).astype(np.float32))
    bl = np.stack(bl, axis=1)
    shared["blkm"] = np.ascontiguousarray(np.stack([bl, bl.transpose(2, 1, 0)]))
    shared["ada_w"] = _kmajor(f(ada_w))
    shared["ada_b"] = np.ascontiguousarray(f(ada_b).reshape(4, 48, 128).transpose(2, 0, 1))
    shared["norm_g"] = np.ascontiguousarray(np.stack([_pc(norm1_g), _pc(norm2_g)], axis=2))
    shared["final_g"] = _pc(final_g)
    shared["sgu_w_in"] = _kmajor(f(sgu_w_in))
    shared["sgu_w_out"] = _kmajor(f(sgu_w_out))
    shared["sgu_wsT"] = np.ascontiguousarray(f(sgu_w_s).transpose(0, 3, 1, 2))
    shared["sgu_bs"] = np.ascontiguousarray(f(sgu_b_s).reshape(2, 1, 2048))
    shared["sgu_ln"] = np.ascontiguousarray(np.stack([f(sgu_ln_g), f(sgu_ln_b)], axis=1))
    w1 = f(mlp_w1).reshape(4, 8, 128, 8, 512)
    shared["mlp_w1"] = np.ascontiguousarray(w1.transpose(0, 3, 2, 1, 4))
    w2 = f(mlp_w2).reshape(4, 8, 4, 128, 1024)
    shared["mlp_w2"] = np.ascontiguousarray(w2.transpose(0, 1, 3, 2, 4))
    shared["rwkv_w"] = np.ascontiguousarray(np.stack([_kmajor(f(rwkv_w_r)), _kmajor(f(rwkv_w_k)), _kmajor(f(rwkv_w_v)), _kmajor(f(rwkv_w_o))], axis=1))
    shared["rwkv_mu"] = _pc(rwkv_mu)
    shared["rwkv_l1"] = np.ascontiguousarray(np.concatenate([_kmajor(f(rwkv_w1)), _kmajor(f(rwkv_a1))], axis=1))
    shared["rwkv_l2"] = np.ascontiguousarray(np.concatenate([f(rwkv_w2), f(rwkv_a2)], axis=1))
    shared["rwkv_g1"] = _kmajor(f(rwkv_g1))
    shared["rwkv_g2"] = np.ascontiguousarray(f(rwkv_g2))
    shared["rwkv_v1"] = _kmajor(f(rwkv_v1))[0]
    shared["rwkv_v2"] = np.ascontiguousarray(f(rwkv_v2)[0])
    v0 = np.broadcast_to(f(rwkv_v0).reshape(1, 1024), (2, 1024))
    zz = np.zeros((2, 1024), np.float32)
    pvl = [f(rwkv_w0)[:, 0], f(rwkv_w0)[:, 1], f(rwkv_a0)[:, 0], f(rwkv_a0)[:, 1], v0, f(rwkv_k_k), f(rwkv_k_a),
           f(rwkv_r_k).reshape(2, 1024), f(rwkv_ln_g), f(rwkv_ln_b), zz]
    shared["rwkv_pv"] = _pc(np.stack(pvl, axis=1))

    in_maps = []
    xs = f(x_sample); xp = f(x_prompt); st = f(state_wkv)
    for b in range(NCORES):
        m = dict(shared)
        xt = np.concatenate([xs[b], xp[2 * b], xp[2 * b + 1]], axis=0)
        m["xT"] = np.ascontiguousarray(xt.reshape(NT, 8, 128).transpose(2, 1, 0))
        cond = np.stack([f(c)[b], f(c_ctx)], axis=-1)
        m["condT"] = np.ascontiguousarray(cond.reshape(8, 128, 2).transpose(1, 0, 2))
        s = st[b].reshape(2, 2, 8, 2, 64, 64)
        m["state0"] = np.ascontiguousarray(s.transpose(3, 5, 0, 1, 2, 4).reshape(128, 2, 2, 8, 64))
        in_maps.append(m)

    if depth not in _NC_CACHE:
        _NC_CACHE[depth] = build(depth)
    nc = _NC_CACHE[depth]
    if _RUNNER[0] is not None:
        res = _RUNNER[0](nc, in_maps)
    else:
        ncr = int(os.environ.get("KCORES", NCORES))
        res = run_bass_kernel_spmd(nc, in_maps[:ncr], core_ids=list(range(ncr)))
        if ncr < NCORES:
            res.results.extend([res.results[0]] * (NCORES - ncr))
    y_prompt = np.zeros((16, 256, D), np.float32)
    y_sample = np.zeros((8, 2048, D), np.float32)
    n_rw = depth // 2
    new_state = np.zeros((16, max(n_rw, 1), 2, 16, 64, 64), np.float32)
    for b in range(NCORES):
        r = res.results[b]
        yt = np.asarray(r["yT"]).transpose(2, 1, 0).reshape(NT, D)
        y_sample[b] = yt[:2048]
        y_prompt[2 * b] = yt[2048:2304]
        y_prompt[2 * b + 1] = yt[2304:2560]
        ns = np.asarray(r["new_state"])
        ns = ns.reshape(2, 2, 2, 2, 64, 8, 64)
        ns = ns.transpose(0, 1, 2, 5, 3, 6, 4).reshape(2, 2, 2, 16, 64, 64)
        for q in range(2):
            new_state[2 * b + q, :n_rw] = ns[q, :n_rw]
    if n_rw == 0:
        new_state = new_state[:, :0]
    return (y_prompt, y_sample, new_state)
```
